# Optimizing a Trainium2 kernel written in Bass

```python
import math
import jax
import jax.numpy as jnp
from jax import lax
import numpy as np

D_MODEL = 1024
BATCH = 16
SEQ = 2048
DEPTH = 2
DEC_BATCH = 8
DEC_SEQ = 64
PAST_LEN = 1024

CHUNK = 64
Q_BLOCK = 128
HEAD_DIM = 64
GROUP_WIDTH = 512
D_MIX = 3 * GROUP_WIDTH
RMS_EPS = 1e-6
A_HEADS = GROUP_WIDTH // HEAD_DIM
A_DECAY_RANK = 64
A_ICLR_RANK = 64
A_SHIFT_W = 3 * GROUP_WIDTH + A_DECAY_RANK + A_ICLR_RANK
GN_EPS = 64e-5
B_HEADS = GROUP_WIDTH // HEAD_DIM
C_HEADS = GROUP_WIDTH // HEAD_DIM
C_KV_HEADS = 2
C_IDX_HEADS = 4
C_IDX_DIM = 64
C_TOPK_MAX = 256
T5_BUCKETS = 32
T5_MAX_DIST = 128

IN_SPLITS = (
    ('a_r', GROUP_WIDTH), ('a_k', GROUP_WIDTH), ('a_v', GROUP_WIDTH),
    ('a_w', A_DECAY_RANK), ('a_a', A_ICLR_RANK), ('a_g', GROUP_WIDTH),
    ('b_q', GROUP_WIDTH), ('b_k', GROUP_WIDTH), ('b_v', GROUP_WIDTH),
    ('b_f', B_HEADS), ('b_g', GROUP_WIDTH),
    ('c_q', GROUP_WIDTH), ('c_k', C_KV_HEADS * HEAD_DIM), ('c_v', C_KV_HEADS * HEAD_DIM),
    ('c_qi', C_IDX_HEADS * C_IDX_DIM), ('c_ki', C_IDX_DIM), ('c_wi', C_IDX_HEADS),
    ('c_g', GROUP_WIDTH),
)
D_IN = sum(w for _, w in IN_SPLITS)
STATE_KEYS = ('a_wkv', 'a_shift', 'b_k', 'b_v', 'b_logf', 'c_k', 'c_v', 'c_kidx')

kernel_name = 'hybrid_rwkv7_fox_dsa_streaming_step'


def rms_norm(x, g):
    xf = x.astype(jnp.float32)
    y = xf * lax.rsqrt(jnp.mean(xf * xf, axis=-1, keepdims=True) + RMS_EPS)
    return (y * g.astype(jnp.float32)).astype(x.dtype)


def split_cols(z):
    out = {}
    off = 0
    for name, width in IN_SPLITS:
        out[name] = z[..., off:off + width]
        off += width
    return out


def t5_bucket(rel):
    nb = T5_BUCKETS // 2
    max_exact = nb // 2
    base = jnp.where(rel > 0, nb, 0)
    n = jnp.abs(rel)
    nf = jnp.maximum(n, 1).astype(jnp.float32)
    large = max_exact + (jnp.log(nf / max_exact) / math.log(T5_MAX_DIST / max_exact)
                         * (nb - max_exact)).astype(jnp.int32)
    large = jnp.minimum(large, nb - 1)
    return base + jnp.where(n < max_exact, n, large)


def sweep_queries(block_fn, n_q):
    if n_q <= Q_BLOCK or n_q % Q_BLOCK:
        return block_fn(0, n_q)
    starts = jnp.arange(n_q // Q_BLOCK, dtype=jnp.int32) * Q_BLOCK
    out = lax.map(lambda s: block_fn(s, Q_BLOCK), starts)
    out = jnp.moveaxis(out, 0, 1)
    return out.reshape((out.shape[0], n_q) + out.shape[3:])


def rwkv7_mixer(z, shift_prev, wkv0, mu, w0, w_b, a0, a_b, k_k, k_a, r_k, lnx_w, lnx_b):
    B, T, _ = z.shape
    G, H, N = GROUP_WIDTH, A_HEADS, HEAD_DIM
    prev = jnp.concatenate([shift_prev[:, None, :].astype(z.dtype), z[:, :-1]], axis=1)
    zm = (z + (prev - z) * mu).astype(jnp.float32)
    r, k, v, w_in, a_in = jnp.split(zm, [G, 2 * G, 3 * G, 3 * G + A_DECAY_RANK], axis=-1)
    decay = jnp.exp(-jnp.exp(-jax.nn.softplus(-(w0 + jnp.tanh(w_in) @ w_b)) - 0.5))
    a = jax.nn.sigmoid(a0 + a_in @ a_b)
    kk = (k * k_k).reshape(B, T, H, N)
    kk = kk * lax.rsqrt(jnp.sum(kk * kk, axis=-1, keepdims=True) + 1e-12)
    k = k * (1.0 + (a - 1.0) * k_a)
    heads = lambda t: t.reshape(B, T, H, N)
    r, k, v, decay, a = heads(r), heads(k), heads(v), heads(decay), heads(a)

    def step(S, inp):
        r_t, w_t, k_t, v_t, kk_t, a_t = inp
        sa = jnp.einsum('bhij,bhj->bhi', S, -kk_t)
        S = (S * w_t[:, :, None, :] + sa[..., None] * (kk_t * a_t)[:, :, None, :]
             + v_t[..., None] * k_t[:, :, None, :])
        return S, jnp.einsum('bhij,bhj->bhi', S, r_t)

    tm = lambda t: jnp.moveaxis(t, 1, 0)
    wkv, y = lax.scan(step, wkv0.astype(jnp.float32),
                      (tm(r), tm(decay), tm(k), tm(v), tm(kk), tm(a)))
    y = jnp.moveaxis(y, 0, 1)
    mean = jnp.mean(y, axis=-1, keepdims=True)
    var = jnp.mean(jnp.square(y - mean), axis=-1, keepdims=True)
    y = ((y - mean) * lax.rsqrt(var + GN_EPS)).reshape(B, T, G) * lnx_w + lnx_b
    y = y + (jnp.sum(r * k * r_k, axis=-1, keepdims=True) * v).reshape(B, T, G)
    return y.astype(z.dtype), z[:, -1], wkv


def fox_attention(q, k, v, c_q, c_k, q_pos, k_pos):
    scale = HEAD_DIM ** -0.5
    ckT = jnp.swapaxes(c_k, 1, 2)

    def block(start, size):
        qb = lax.dynamic_slice_in_dim(q, start, size, 1)
        cqb = lax.dynamic_slice_in_dim(c_q, start, size, 1)
        qpb = lax.dynamic_slice_in_dim(q_pos, start, size, 0)
        s = jnp.einsum('bqhd,bkhd->bhqk', qb, k).astype(jnp.float32) * scale
        s = s + jnp.swapaxes(cqb, 1, 2)[..., None] - ckT[:, :, None, :]
        s = jnp.where((k_pos[None, :] <= qpb[:, None])[None, None], s, -jnp.inf)
        p = jax.nn.softmax(s, axis=-1).astype(v.dtype)
        return jnp.einsum('bhqk,bkhd->bqhd', p, v)

    return sweep_queries(block, q.shape[1])


def dsa_attention(q, k, v, q_idx, k_idx, w_idx, t5_table, q_pos, k_pos):
    B = q.shape[0]
    n_keys = k.shape[1]
    topk = min(C_TOPK_MAX, n_keys // 4)
    grp = C_HEADS // C_KV_HEADS
    scale = HEAD_DIM ** -0.5
    k_chunk = k_pos // CHUNK

    def block(start, size):
        qb = lax.dynamic_slice_in_dim(q, start, size, 1)
        qib = lax.dynamic_slice_in_dim(q_idx, start, size, 1)
        wib = lax.dynamic_slice_in_dim(w_idx, start, size, 1).astype(jnp.float32) * C_IDX_HEADS ** -0.5
        qpb = lax.dynamic_slice_in_dim(q_pos, start, size, 0)
        q_chunk = qpb // CHUNK
        isc = jax.nn.relu(jnp.einsum('bqhd,bkd->bqhk', qib, k_idx).astype(jnp.float32) * C_IDX_DIM ** -0.5)
        score = jnp.einsum('bqhk,bqh->bqk', isc, wib)
        score = jnp.where((k_chunk[None, :] <= q_chunk[:, None])[None], score, -jnp.inf)
        _, idx = lax.top_k(score, topk)
        k_sel = jax.vmap(lambda kb, ib: kb[ib])(k, idx)
        v_sel = jax.vmap(lambda vb, ib: vb[ib])(v, idx)
        pos_sel = k_pos[idx]
        valid = (pos_sel // CHUNK) <= q_chunk[None, :, None]
        bias = t5_table[t5_bucket(pos_sel - qpb[None, :, None])]
        bias = bias.reshape(B, size, topk, C_KV_HEADS, grp).transpose(0, 1, 3, 4, 2)
        qg = qb.reshape(B, size, C_KV_HEADS, grp, HEAD_DIM)
        s = jnp.einsum('bqngd,bqknd->bqngk', qg, k_sel).astype(jnp.float32) * scale
        s = jnp.where(valid[:, :, None, None, :], s + bias.astype(jnp.float32), -jnp.inf)
        p = jax.nn.softmax(s, axis=-1).astype(v.dtype)
        o = jnp.einsum('bqngk,bqknd->bqngd', p, v_sel)
        return o.reshape(B, size, C_HEADS, HEAD_DIM)

    return sweep_queries(block, q.shape[1])


def mixer_layer(h, l, prm, past):
    B, T, _ = h.shape
    z = rms_norm(h, prm['norm_g'][l]) @ prm['w_in'][l]
    cols = split_cols(z)
    n_past = 0 if past is None else past['b_k'].shape[1]
    q_pos = n_past + jnp.arange(T, dtype=jnp.int32)
    k_pos = jnp.arange(n_past + T, dtype=jnp.int32)
    with_past = lambda new, name: new if past is None else jnp.concatenate([past[name].astype(new.dtype), new], axis=1)

    if past is None:
        shift_prev = jnp.zeros((B, A_SHIFT_W), z.dtype)
        wkv0 = jnp.zeros((B, A_HEADS, HEAD_DIM, HEAD_DIM), jnp.float32)
    else:
        shift_prev = past['a_shift']
        wkv0 = past['a_wkv']
    y_a, a_shift, a_wkv = rwkv7_mixer(
        z[..., :A_SHIFT_W], shift_prev, wkv0, prm['a_mu'][l], prm['a_w0'][l], prm['a_w_b'][l],
        prm['a_a0'][l], prm['a_a_b'][l], prm['a_k_k'][l], prm['a_k_a'][l], prm['a_r_k'][l],
        prm['a_lnx_w'][l], prm['a_lnx_b'][l])

    heads = lambda t, n: t.reshape(B, T, n, HEAD_DIM)
    b_k = heads(cols['b_k'], B_HEADS)
    b_v = heads(cols['b_v'], B_HEADS)
    b_logf = jax.nn.log_sigmoid((cols['b_f'] + prm['b_f_bias'][l]).astype(jnp.float32))
    c_cum = jnp.cumsum(with_past(b_logf, 'b_logf').astype(jnp.float32), axis=1)
    y_b = fox_attention(heads(cols['b_q'], B_HEADS), with_past(b_k, 'b_k'), with_past(b_v, 'b_v'),
                        c_cum[:, n_past:], c_cum, q_pos, k_pos)

    c_k = heads(cols['c_k'], C_KV_HEADS)
    c_v = heads(cols['c_v'], C_KV_HEADS)
    c_kidx = cols['c_ki']
    y_c = dsa_attention(heads(cols['c_q'], C_HEADS), with_past(c_k, 'c_k'), with_past(c_v, 'c_v'),
                        cols['c_qi'].reshape(B, T, C_IDX_HEADS, C_IDX_DIM), with_past(c_kidx, 'c_kidx'),
                        cols['c_wi'], prm['t5_table'], q_pos, k_pos)

    gated = jnp.concatenate([
        y_a * jax.nn.silu(cols['a_g']),
        y_b.reshape(B, T, GROUP_WIDTH) * jax.nn.silu(cols['b_g']),
        y_c.reshape(B, T, GROUP_WIDTH) * jax.nn.silu(cols['c_g'])], axis=-1)
    h = h + gated @ prm['w_out'][l]
    new_state = {'a_wkv': a_wkv, 'a_shift': a_shift, 'b_k': b_k, 'b_v': b_v, 'b_logf': b_logf,
                 'c_k': c_k, 'c_v': c_v, 'c_kidx': c_kidx}
    return h, new_state


def run_trunk(x, prm, caches):
    h = x
    states = []
    for l in range(DEPTH):
        past = None if caches is None else {name: caches[name][l] for name in STATE_KEYS}
        h, st = mixer_layer(h, l, prm, past)
        states.append(st)
    y = rms_norm(h, prm['final_g'])
    stacked = {name: jnp.stack([st[name] for st in states]) for name in STATE_KEYS}
    return y, stacked


def setup_inputs(seed: int = 0) -> dict:
    key = jax.random.key(seed)
    ks = jax.random.split(key, 32)
    f32 = jnp.float32
    nrm = lambda kk, shape, s: jax.random.normal(kk, shape, f32) * s
    L = DEPTH
    return {
        'x_prompt': nrm(ks[0], (BATCH, SEQ, D_MODEL), 1.0),
        'x_sample': nrm(ks[1], (DEC_BATCH, DEC_SEQ, D_MODEL), 1.0),
        'state_a_wkv': nrm(ks[2], (L, DEC_BATCH, A_HEADS, HEAD_DIM, HEAD_DIM), 0.5),
        'state_a_shift': nrm(ks[3], (L, DEC_BATCH, A_SHIFT_W), 1.0),
        'cache_b_k': nrm(ks[4], (L, DEC_BATCH, PAST_LEN, B_HEADS, HEAD_DIM), 1.0),
        'cache_b_v': nrm(ks[5], (L, DEC_BATCH, PAST_LEN, B_HEADS, HEAD_DIM), 1.0),
        'cache_b_logf': jax.nn.log_sigmoid(2.0 + nrm(ks[6], (L, DEC_BATCH, PAST_LEN, B_HEADS), 1.0)),
        'cache_c_k': nrm(ks[7], (L, DEC_BATCH, PAST_LEN, C_KV_HEADS, HEAD_DIM), 1.0),
        'cache_c_v': nrm(ks[8], (L, DEC_BATCH, PAST_LEN, C_KV_HEADS, HEAD_DIM), 1.0),
        'cache_c_kidx': nrm(ks[9], (L, DEC_BATCH, PAST_LEN, C_IDX_DIM), 1.0),
        'norm_g': 1.0 + nrm(ks[10], (L, D_MODEL), 0.02),
        'w_in': nrm(ks[11], (L, D_MODEL, D_IN), D_MODEL ** -0.5),
        'w_out': nrm(ks[12], (L, D_MIX, D_MODEL), D_MIX ** -0.5),
        'a_mu': jax.random.uniform(ks[13], (L, A_SHIFT_W), f32),
        'a_w0': jax.random.uniform(ks[14], (L, GROUP_WIDTH), f32, -6.0, -1.0),
        'a_w_b': nrm(ks[15], (L, A_DECAY_RANK, GROUP_WIDTH), 0.1),
        'a_a0': nrm(ks[16], (L, GROUP_WIDTH), 0.1),
        'a_a_b': nrm(ks[17], (L, A_ICLR_RANK, GROUP_WIDTH), 0.1),
        'a_k_k': 0.85 + nrm(ks[18], (L, GROUP_WIDTH), 0.05),
        'a_k_a': 1.0 + nrm(ks[19], (L, GROUP_WIDTH), 0.05),
        'a_r_k': nrm(ks[20], (L, A_HEADS, HEAD_DIM), 0.1),
        'a_lnx_w': 1.0 + nrm(ks[21], (L, GROUP_WIDTH), 0.02),
        'a_lnx_b': nrm(ks[22], (L, GROUP_WIDTH), 0.02),
        'b_f_bias': jax.random.uniform(ks[23], (L, B_HEADS), f32, 1.0, 4.0),
        't5_table': nrm(ks[24], (T5_BUCKETS, C_HEADS), 0.5),
        'final_g': 1.0 + nrm(ks[25], (D_MODEL,), 0.02),
    }


def reference(x_prompt, x_sample, state_a_wkv, state_a_shift, cache_b_k, cache_b_v, cache_b_logf,
              cache_c_k, cache_c_v, cache_c_kidx, norm_g, w_in, w_out, a_mu, a_w0, a_w_b, a_a0,
              a_a_b, a_k_k, a_k_a, a_r_k, a_lnx_w, a_lnx_b, b_f_bias, t5_table, final_g):
    prm = {'norm_g': norm_g, 'w_in': w_in, 'w_out': w_out, 'a_mu': a_mu, 'a_w0': a_w0,
           'a_w_b': a_w_b, 'a_a0': a_a0, 'a_a_b': a_a_b, 'a_k_k': a_k_k, 'a_k_a': a_k_a,
           'a_r_k': a_r_k, 'a_lnx_w': a_lnx_w, 'a_lnx_b': a_lnx_b, 'b_f_bias': b_f_bias,
           't5_table': t5_table, 'final_g': final_g}
    y_prompt, p = run_trunk(x_prompt, prm, None)
    caches = {'a_wkv': state_a_wkv, 'a_shift': state_a_shift, 'b_k': cache_b_k, 'b_v': cache_b_v,
              'b_logf': cache_b_logf, 'c_k': cache_c_k, 'c_v': cache_c_v, 'c_kidx': cache_c_kidx}
    y_sample, s = run_trunk(x_sample, prm, caches)
    return (y_prompt, y_sample,
            p['a_wkv'], p['a_shift'], p['b_k'], p['b_v'], p['b_logf'], p['c_k'], p['c_v'], p['c_kidx'],
            s['a_wkv'], s['a_shift'], s['b_k'], s['b_v'], s['b_logf'], s['c_k'], s['c_v'], s['c_kidx'])
```

```python
import math
from contextlib import ExitStack

import numpy as np
import ml_dtypes

import concourse.bass as bass
import concourse.mybir as mybir
from concourse.bass_utils import run_bass_kernel_spmd

F32 = mybir.dt.float32
BF16 = mybir.dt.bfloat16
ALU = mybir.AluOpType
AF = mybir.ActivationFunctionType
AX = mybir.AxisListType

D = 1024
DIN = 5836
DMIX = 1536
L = 2
NCORES = 8
RMS_EPS = 1e-6
GN_EPS = 64e-5
NEG = -1.0e30
NEGR = -3.0e38
EXPM05 = math.exp(-0.5)


class Buf:
    __slots__ = ("w", "r", "excl", "grp")

    def __init__(self, excl=False):
        self.w = {}
        self.r = {}
        self.grp = set()
        self.excl = excl


class _Cap:
    def __init__(self):
        self.call = None

    def __getattr__(self, name):
        def f(*a, **k):
            self.call = (name, a, k)
            return self
        return f


class Sched:
    def __init__(self, nc, es):
        self.nc = nc
        self.eng = {"pe": nc.tensor, "act": nc.scalar, "dve": nc.vector, "pool": nc.gpsimd, "sp": nc.sync}
        self.sem = {}
        self.cnt = {}
        for e in ("pe", "act", "dve", "pool"):
            self.sem[e] = es.enter_context(nc.semaphore("s_" + e))
            self.cnt[e] = 0
        self.waited = {e: {} for e in self.eng}
        self.dq = {}
        self.dqi = {}
        for q, n in (("sp", 12), ("act", 8), ("pool", 6)):
            self.dq[q] = []
            self.dqi[q] = 0
            for i in range(n):
                k = ("d", q, i)
                self.sem[k] = es.enter_context(nc.semaphore("d_%s%d" % (q, i)))
                self.dq[q].append([k, 0])

    def _collect(self, e, r, w, is_dma=False, part=0):
        deps = {}

        def need(k, v):
            if k == e and e == "pe" and not is_dma:
                return
            if deps.get(k, 0) < v:
                deps[k] = v

        for b in r:
            for k, v in b.w.items():
                need(k, v)
            if b.excl:
                for k, v in b.r.items():
                    if k != e:
                        need(k, v)
        for b in w:
            for k, v in b.w.items():
                if part == 2 and k in b.grp:
                    continue
                need(k, v)
            for k, v in b.r.items():
                need(k, v)
        return deps

    def _wait(self, e, deps):
        wd = self.waited[e]
        eng = self.eng[e]
        for k, v in deps.items():
            if wd.get(k, 0) < v:
                eng.wait_ge(self.sem[k], v)
                wd[k] = v

    def begin(self):
        self.rec = []

    def end(self):
        ops, self.rec = self.rec, None
        if not ops:
            return
        for i in self._schedule(ops):
            o = ops[i]
            if o[0] == "op":
                name, a, k = o[2]
                self.op(o[1], lambda eng, name=name, a=a, k=k: getattr(eng, name)(*a, **k), o[3], o[4])
            else:
                self.dma(o[1], o[2], o[3], o[4], o[5], part=o[7])

    _DEF_COST = {"pe": 0.09, "act": 0.55, "dve": 0.55, "pool": 1.2}

    @staticmethod
    def _free(ap):
        try:
            n = 1
            for d in list(ap.shape)[1:]:
                n *= int(d)
            return n
        except Exception:
            return None

    def _estimate(self, e, call):
        name, a, k = call
        try:
            if e == "pe":
                mv = k.get("rhs") if name == "matmul" else (a[2] if len(a) > 2 else k.get("identity"))
                n = self._free(mv) or 128
                return 0.05 + max(64, n) * 0.62e-3
            out = k.get("out", a[0] if a else None)
            n = self._free(out)
            if n is None:
                return self._DEF_COST[e]
            if e == "act":
                return 0.2 + n * 0.85e-3
            if e == "dve":
                return 0.12 + n * 1.05e-3
            return 0.3 + n * 2.2e-3
        except Exception:
            return self._DEF_COST[e]

    def _schedule(self, ops):
        n = len(ops)
        last_w, readers = {}, {}
        preds = [None] * n
        for i, o in enumerate(ops):
            r, w = (o[3], o[4]) if o[0] == "op" else (o[4], o[5])
            ps = set()
            for b in r:
                k = id(b)
                if k in last_w:
                    ps.add(last_w[k])
                if b.excl:
                    ps |= readers.get(k, set())
            for b in w:
                k = id(b)
                if k in last_w:
                    ps.add(last_w[k])
                ps |= readers.get(k, set())
            for b in r:
                k = id(b)
                if b.excl:
                    last_w[k] = i
                    readers[k] = set()
                else:
                    readers.setdefault(k, set()).add(i)
            for b in w:
                k = id(b)
                last_w[k] = i
                readers[k] = set()
            ps.discard(i)
            preds[i] = ps
        succs = [[] for _ in range(n)]
        indeg = [0] * n
        for i in range(n):
            indeg[i] = len(preds[i])
            for p in preds[i]:
                succs[p].append(i)
        cost = [0.0] * n
        for i, o in enumerate(ops):
            if o[0] == "op":
                cost[i] = (o[5] if o[5] is not None else self._DEF_COST[o[1]]) + 0.15
            else:
                cost[i] = o[6] if o[6] is not None else 3.0
        bl = [0.0] * n
        for i in range(n - 1, -1, -1):
            m = 0.0
            for sx in succs[i]:
                if bl[sx] > m:
                    m = bl[sx]
            bl[i] = cost[i] + m
        avail = [0.0] * n
        done = [0.0] * n
        ready = {}
        for i in range(n):
            if indeg[i] == 0:
                ready.setdefault(ops[i][1], []).append(i)
        free = {}
        order = []
        while len(order) < n:
            best = None
            for st, lst in ready.items():
                if not lst:
                    continue
                f = free.get(st, 0.0)
                ci = min(lst, key=lambda i: (max(avail[i], f), -bl[i], i))
                key = (max(avail[ci], f), -bl[ci], ci)
                if best is None or key < best[0]:
                    best = (key, st, ci)
            (start, _, _), st, ci = best
            ready[st].remove(ci)
            o = ops[ci]
            if o[0] == "op":
                c = o[5] if o[5] is not None else self._DEF_COST[st]
                free[st] = start + c
                done[ci] = start + c + 0.15
            else:
                free[st] = start + 0.06
                done[ci] = start + (o[6] if o[6] is not None else 3.0)
            order.append(ci)
            for sx in succs[ci]:
                if done[ci] > avail[sx]:
                    avail[sx] = done[ci]
                indeg[sx] -= 1
                if indeg[sx] == 0:
                    ready.setdefault(ops[sx][1], []).append(sx)
        return order

    def op(self, e, fn, r=(), w=(), c=None):
        if _DEAD["v"]:
            return
        if getattr(self, "rec", None) is not None:
            cap = _Cap()
            fn(cap)
            if c is None:
                c = self._estimate(e, cap.call)
            self.rec.append(("op", e, cap.call, tuple(r), tuple(w), c))
            return
        self._wait(e, self._collect(e, r, w))
        ins = fn(self.eng[e])
        self.cnt[e] += 1
        c = self.cnt[e]
        ins.then_inc(self.sem[e], 1)
        for b in r:
            b.r[e] = c
        for b in w:
            b.w = {e: c}
            b.grp = set()
            b.r = {}

    def dma(self, q, out, in_, r=(), w=(), c=None, part=0):
        if _DEAD["v"]:
            return
        if getattr(self, "rec", None) is not None:
            self.rec.append(("dma", q, out, in_, tuple(r), tuple(w), c, part))
            return
        deps = self._collect(q, r, w, is_dma=True, part=part)
        slot = self.dq[q][self.dqi[q] % len(self.dq[q])]
        self.dqi[q] += 1
        k = slot[0]
        if slot[1] > 0 and deps.get(k, 0) < 16 * slot[1]:
            deps[k] = 16 * slot[1]
        self._wait(q, deps)
        ins = self.eng[q].dma_start(out=out, in_=in_)
        slot[1] += 1
        v = 16 * slot[1]
        ins.then_inc(self.sem[k], 16)
        for b in r:
            b.r[k] = v
        for b in w:
            if part == 2:
                b.w[k] = v
                b.grp.add(k)
            else:
                b.w = {k: v}
                b.grp = {k} if part == 1 else set()
            b.r = {}

    def all_deps(self):
        deps = {e: c for e, c in self.cnt.items() if c > 0}
        for q in self.dq:
            for k, n in self.dq[q]:
                if n > 0:
                    deps[k] = 16 * n
        return deps

    def barrier(self):
        if _DEAD["v"]:
            return
        assert getattr(self, "rec", None) is None
        deps = self.all_deps()
        for e in ("pe", "act", "dve", "pool", "sp"):
            self._wait(e, dict(deps))


def _t5_bucket_np(rel):
    nb = 16
    me = 8
    base = np.where(rel > 0, nb, 0)
    n = np.abs(rel)
    nf = np.maximum(n, 1).astype(np.float32)
    large = me + (np.log(nf / np.float32(me)) / np.float32(math.log(128 / me)) * np.float32(nb - me)).astype(np.int32)
    large = np.minimum(large, nb - 1)
    return base + np.where(n < me, n, large)


T5B = [b for b in range(32) if b != 15]
CF_ID, CF_TRI, CF_ONE, CF_E0, CF_N = 0, 128, 256, 384, 512
CB_ID, CB_SU, CB_IU, CB_NSU, CB_NSL, CB_T5, CB_N = 0, 128, 256, 384, 512, 640, 640 + 31 * 256


def host_consts(nkmax):
    p = np.arange(128)[:, None]
    c = np.arange(128)[None, :]
    cf = np.zeros((128, CF_N + nkmax), np.float32)
    cf[:, CF_ID:CF_ID + 128] = (p == c)
    cf[:, CF_TRI:CF_TRI + 128] = (p <= c)
    cf[:, CF_ONE:CF_ONE + 128] = 1.0
    cf[:, CF_E0:CF_E0 + 128] = (p == 0) * np.ones((1, 128))
    cf[:, CF_N:] = -(2.0 ** -34) * (np.arange(nkmax)[None, :] + 1.0)
    cb = np.zeros((128, CB_N), np.float32)
    cb[:, CB_ID:CB_ID + 128] = (p == c)
    cb[:, CB_SU:CB_SU + 128] = (p < c)
    cb[:, CB_IU:CB_IU + 128] = (p <= c)
    cb[:, CB_NSU:CB_NSU + 128] = -1.0 * (p < c)
    cb[:, CB_NSL:CB_NSL + 128] = -1.0 * (p > c)
    for off, base in ((0, 0), (1, -128)):
        bk = _t5_bucket_np((p - c + base).astype(np.int32))
        for bi, b in enumerate(T5B):
            cb[:, CB_T5 + bi * 256 + off * 128: CB_T5 + bi * 256 + off * 128 + 128] = (bk == b)
    return cf, cb.astype(ml_dtypes.bfloat16)


class _Stop(Exception):
    pass


import os as _os
_STOP = int(_os.environ.get("K_STOP", "999"))
_SUB = int(_os.environ.get("K_SUB", "0"))
_USE_SCHED = _os.environ.get("K_SCHED", "1") == "1"
_ACT_TOPK = _os.environ.get("K_ACT_TOPK", "1") == "1"


_DEAD = {"v": False}


def sub(n):
    if n == _SUB:
        _DEAD["v"] = True


def build(NP, T, NS, TS, PAST):
    nc = bass.Bass("TRN2", target_bir_lowering=False)
    NKMAX = max(T, PAST + 128)

    def din(name, shape, dt=F32):
        return nc.dram_tensor(name, list(shape), dt, kind="ExternalInput").ap()

    def dout(name, shape, dt=F32):
        return nc.dram_tensor(name, list(shape), dt, kind="ExternalOutput").ap()

    I = {}
    I["x_p"] = din("x_p", [NP, T, D])
    I["x_s"] = din("x_s", [NS, TS, D])
    I["st_wkv"] = din("st_wkv", [L, NS, 8, 64, 64])
    I["st_shift"] = din("st_shift", [L, NS, 1664])
    I["cb_k"] = din("cb_k", [L, NS, PAST, 512])
    I["cb_v"] = din("cb_v", [L, NS, PAST, 512])
    I["cb_logf"] = din("cb_logf", [L, NS, PAST, 8])
    I["cc_k"] = din("cc_k", [L, NS, PAST, 128])
    I["cc_v"] = din("cc_v", [L, NS, PAST, 128])
    I["cc_kidx"] = din("cc_kidx", [L, NS, PAST, 64])
    I["norm_g"] = din("norm_g", [L, D])
    I["w_in"] = din("w_in", [L, D, DIN])
    I["w_out"] = din("w_out", [L, DMIX, D])
    I["a_mu"] = din("a_mu", [L, 1664])
    for nm in ("a_w0", "a_a0", "a_k_k", "a_k_a", "a_r_k", "a_lnx_w", "a_lnx_b"):
        I[nm] = din(nm, [L, 512])
    I["a_w_b"] = din("a_w_b", [L, 64, 512])
    I["a_a_b"] = din("a_a_b", [L, 64, 512])
    I["b_f_bias"] = din("b_f_bias", [L, 8])
    I["t5_table"] = din("t5_table", [256])
    I["final_g"] = din("final_g", [D])
    I["cf"] = din("cf", [128, CF_N + NKMAX])
    I["cb"] = din("cb", [128, CB_N], BF16)

    O = {}
    for g, nb, tt in (("p", NP, T), ("s", NS, TS)):
        O["y_" + g] = dout("y_" + g, [nb, tt, D])
        O["wkv_" + g] = dout("wkv_" + g, [L, nb, 8, 64, 64])
        O["shift_" + g] = dout("shift_" + g, [L, nb, 1664])
        O["bk_" + g] = dout("bk_" + g, [L, nb, tt, 512])
        O["bv_" + g] = dout("bv_" + g, [L, nb, tt, 512])
        O["blogf_" + g] = dout("blogf_" + g, [L, nb, tt, 8])
        O["ck_" + g] = dout("ck_" + g, [L, nb, tt, 128])
        O["cv_" + g] = dout("cv_" + g, [L, nb, tt, 128])
        O["cki_" + g] = dout("cki_" + g, [L, nb, tt, 64])

    seqs = [("p", b, T, 0) for b in range(NP)] + [("s", b, TS, PAST) for b in range(NS)]
    hbuf = {}
    utd = {}
    for (g, b, tq, past) in seqs:
        hbuf[(g, b)] = nc.dram_tensor("hb_%s%d" % (g, b), [tq, D], F32, kind="Internal").ap()
        utd[(g, b)] = nc.dram_tensor("ut_%s%d" % (g, b), [128, 8 * tq], BF16, kind="Internal").ap()

    es = ExitStack()
    with es:
        S = Sched(nc, es)
        uid = {"n": 0}

        def uname(name):
            uid["n"] += 1
            return "%s_u%d" % (name, uid["n"])

        st = lambda name, shape, dt=F32: es.enter_context(nc.sbuf_tensor(uname(name), list(shape), dt))

        cf = st("cf", [128, CF_N + NKMAX])
        cb = st("cb", [128, CB_N - 31 * 256], BF16)
        ebt = st("ebt", [128, 8, 2, 128], BF16)
        B_cf, B_cb, B_ebt = Buf(), Buf(), Buf()
        psum = es.enter_context(nc.psum_tensor("psum", [128, 8, 512], F32))
        psum_bf = psum[:].bitcast(BF16)
        PB = [Buf(excl=True) for _ in range(8)]
        S.dma("sp", cf[:], I["cf"][:, :], w=[B_cf])
        S.dma("sp", cb[:], I["cb"][:, 0:CB_T5], w=[B_cb])
        idf = cf[:, CF_ID:CF_ID + 128]
        tri = cf[:, CF_TRI:CF_TRI + 128]
        onesf = cf[:, CF_ONE:CF_ONE + 128]
        e0row = cf[:, CF_E0:CF_E0 + 128]
        negeps = cf[:, CF_N:]
        idb = cb[:, CB_ID:CB_ID + 128]
        m_su = cb[:, CB_SU:CB_SU + 128]
        m_iu = cb[:, CB_IU:CB_IU + 128]
        m_nsu = cb[:, CB_NSU:CB_NSU + 128]
        m_nsl = cb[:, CB_NSL:CB_NSL + 128]

        def bc_h(ap2, nh, kp, P):
            return ap2[0:kp, 0:P].unsqueeze(1).broadcast_to([kp, nh, P])

        with nc.sbuf_tensor("t5m_sb", [128, 31 * 256], BF16) as t5m, \
                nc.sbuf_tensor("t5tb_sb", [128, 32, 8], F32) as t5tb, \
                nc.sbuf_tensor("t5d_sb", [128, 32, 8], F32) as t5d, \
                nc.sbuf_tensor("t5s_sb", [128, 8, 256], F32) as t5s:
            B_m, B_tb, B_d = Buf(), Buf(), Buf()
            B_s = [Buf() for _ in range(8)]
            S.dma("sp", t5m[:], I["cb"][:, CB_T5:CB_N], w=[B_m])
            S.dma("sp", t5tb[:].rearrange("p a b -> p (a b)"), I["t5_table"].partition_broadcast(128), w=[B_tb])
            S.op("dve", lambda e: e.tensor_tensor(t5d[:], t5tb[:], t5tb[:, 15:16, :].broadcast_to([128, 32, 8]), ALU.subtract),
                 r=[B_tb], w=[B_d])
            for hq in range(8):
                en = "dve"
                S.op(en, lambda e, hq=hq: e.memset(t5s[:, hq, :], 0.0), w=[B_s[hq]])
                for bi, b in enumerate(T5B):
                    S.op(en, lambda e, hq=hq, bi=bi, b=b: e.scalar_tensor_tensor(
                        out=t5s[:, hq, :], in0=t5m[:, bi * 256:(bi + 1) * 256], scalar=t5d[:, b, hq:hq + 1],
                        in1=t5s[:, hq, :], op0=ALU.mult, op1=ALU.add), r=[B_m, B_d, B_s[hq]], w=[B_s[hq]])
                S.op("act", lambda e, hq=hq: e.mul(ebt[:, hq, :, :], t5s[:, hq, :].rearrange("p (o t) -> p o t", o=2), 8.0),
                     r=[B_s[hq]], w=[B_ebt])
            S.barrier()

        rot = {"i": 0}

        def ps1(banks):
            b = banks[rot["i"] % len(banks)]
            rot["i"] += 1
            return b

        def proj(uT, W, c0, n, P, bank, B_u, B_W):
            for k in range(8):
                S.op("pe", lambda e, k=k: e.matmul(psum[0:P, bank, 0:n], lhsT=uT[:, k, 0:P], rhs=W[:, k, c0:c0 + n],
                                                   start=(k == 0), stop=(k == 7)), r=[B_u, B_W], w=[PB[bank]], c=0.05 + n * 0.6e-3)

        def transposes(src, nblk, blkw, P, bank, B_src, dst, B_dst, evac="act", src_blocks=None):
            for k in range(nblk):
                in_ap = src_blocks[k] if src_blocks is not None else src[0:P, k * blkw:(k + 1) * blkw]
                S.op("pe", lambda e, k=k, in_ap=in_ap: e.transpose(psum_bf[0:blkw, bank, k * P:(k + 1) * P], in_ap, idb[0:P, 0:P]),
                     r=[B_src, B_cb], w=[PB[bank]])
            src_ps = psum_bf[0:blkw, bank, 0:nblk * P].rearrange("p (k t) -> p k t", k=nblk)
            if evac == "act":
                S.op("act", lambda e: e.activation(out=dst, in_=src_ps, func=AF.Copy), r=[PB[bank]], w=[B_dst])
            else:
                S.op("dve", lambda e: e.tensor_copy(dst, src_ps), r=[PB[bank]], w=[B_dst])

        def load_w(W, l, c0, n, B_W, dst0=0):
            src = I["w_in"][l, :, c0:c0 + n].rearrange("(k p) n -> p k n", p=128)
            for k0 in range(0, 8, 2):
                S.dma("pool", W[:, k0:k0 + 2, dst0:dst0 + n], src[:, k0:k0 + 2, :], w=[B_W], part=(1 if k0 == 0 else 2))

        def load_wo(Wo, l, r0, B_Wo):
            src = I["w_out"][l, r0:r0 + 512, :].rearrange("(k p) n -> p k n", p=128)
            S.dma("pool", Wo[:, :, :], src, w=[B_Wo])

        def out_proj_rmw(gated, P, Wo, B_g, B_Wo, hsrc, hdst, B_hs, B_hd, ht, B_ht, gT, B_gT, banks, final=None):
            tb = banks[0]
            transposes(gated, 4, 128, P, tb, B_g, gT[:, :, 0:P], B_gT, evac="act")
            S.dma("sp", ht[0:P, :], hsrc, r=[B_hs], w=[B_ht])
            for half in range(2):
                bank = banks[1 + half]
                for k in range(4):
                    S.op("pe", lambda e, k=k, half=half, bank=bank: e.matmul(
                        psum[0:P, bank, :], lhsT=gT[:, k, 0:P], rhs=Wo[:, k, half * 512:(half + 1) * 512],
                        start=(k == 0), stop=(k == 3)), r=[B_gT, B_Wo], w=[PB[bank]])
                S.op("dve", lambda e, half=half, bank=bank: e.tensor_tensor(
                    ht[0:P, half * 512:(half + 1) * 512], ht[0:P, half * 512:(half + 1) * 512], psum[0:P, bank, :], ALU.add),
                    r=[PB[bank], B_ht], w=[B_ht])
            if final is None:
                S.dma("sp", hdst, ht[0:P, :], r=[B_ht], w=[B_hd])
            else:
                fin_g, B_fg, ydst, B_y, junk, B_junk, stat, B_stat = final
                rms_scale(ht, P, junk, B_junk, stat, B_stat, B_ht)
                S.op("dve", lambda e: e.scalar_tensor_tensor(out=ht[0:P, :], in0=ht[0:P, :], scalar=stat[0:P, 2:3], in1=fin_g[0:P, :],
                                                             op0=ALU.mult, op1=ALU.mult), r=[B_ht, B_stat, B_fg], w=[B_ht])
                S.dma("sp", ydst, ht[0:P, :], r=[B_ht], w=[B_y])

        def rms_scale(ht, P, junk, B_junk, stat, B_stat, B_ht):
            S.op("act", lambda e: e.activation(out=junk[0:P, :], in_=ht[0:P, :], func=AF.Square, accum_out=stat[0:P, 0:1]),
                 r=[B_ht], w=[B_junk, B_stat])
            S.op("act", lambda e: e.activation(out=stat[0:P, 1:2], in_=stat[0:P, 0:1], func=AF.Ln, scale=1.0 / D, bias=RMS_EPS),
                 r=[B_stat], w=[B_stat])
            S.op("act", lambda e: e.activation(out=stat[0:P, 2:3], in_=stat[0:P, 1:2], func=AF.Exp, scale=-0.5),
                 r=[B_stat], w=[B_stat])

        def affine_tanh(dst, src_ap, B_src, B_dst, mul, add, scale=0.5):
            S.op("act", lambda e: e.activation(out=dst, in_=src_ap, func=AF.Tanh, scale=scale), r=[B_src], w=[B_dst])
            S.op("dve", lambda e: e.tensor_scalar(dst, dst, mul, add, op0=ALU.mult, op1=ALU.add), r=[B_dst], w=[B_dst])

        def silu_gate(gate, P, bank, tmp, B_tmp, tmp2, B_tmp2, B_gate):
            affine_tanh(tmp[0:P, 0:512], psum[0:P, bank, :], PB[bank], B_tmp, 0.5, 0.5)
            S.op("dve", lambda e: e.tensor_tensor(gate[0:P, :], tmp[0:P, 0:512], psum[0:P, bank, :], ALU.mult),
                 r=[B_tmp, PB[bank]], w=[B_gate])

        def pipeline(front, back, ntiles):
            for _ in front(0):
                pass
            for i in range(ntiles):
                gens = [back(i)]
                if i + 1 < ntiles:
                    gens.append(front(i + 1))
                while gens:
                    for gq in list(gens):
                        try:
                            next(gq)
                        except StopIteration:
                            gens.remove(gq)

        stage = {"n": 0}

        def checkpoint():
            stage["n"] += 1
            if stage["n"] >= _STOP:
                _DEAD["v"] = True

        try:
            for (g, b, Tq, past) in (seqs if _STOP > 0 else []):
                P = min(128, Tq)
                NT = Tq // P
                NPT = past // 128
                NKT = NPT + NT
                NK = NKT * 128
                TOPK = min(256, (past + Tq) // 4)
                xin = (I["x_p"] if g == "p" else I["x_s"])[b]
                yout = O["y_" + g][b]
                hb = hbuf[(g, b)]
                ut = utd[(g, b)]
                B_h = [Buf() for _ in range(NT)]
                B_ut = [Buf() for _ in range(NT)]
                B_y = Buf()
                kp_of = lambda j: 128 if j < NPT else P

                for l in range(L):
                    last = (l == L - 1)
                    S.barrier()
                    with ExitStack() as pa:
                        if _USE_SCHED:
                            S.begin()
                        sa = lambda name, shape, dt=F32: pa.enter_context(nc.sbuf_tensor(uname(name), list(shape), dt))
                        Wa = sa("Wa", [128, 8, 2176], BF16)
                        Wo = sa("Woa", [128, 4, 1024], BF16)
                        PRM = sa("PRM", [128, 5248])
                        WAB = sa("WAB", [128, 512], BF16)
                        gb = sa("gb", [128, D])
                        Sst = sa("Sst", [128, 4, 64])
                        Sbf = sa("Sbf", [128, 2, 4, 64], BF16)
                        zz = [sa("z0", [128, 1664]), sa("z1", [128, 1664])]
                        prev = sa("prev", [128, 1664])
                        ht = [sa("ht0", [128, D]), sa("ht1", [128, D])]
                        junk = sa("junk", [128, D], BF16)
                        stat = sa("stat", [128, 4])
                        ubf = sa("ubf", [128, D], BF16)
                        uT = [sa("uT0", [128, 8, 128], BF16), sa("uT1", [128, 8, 128], BF16)]
                        gate2 = [sa("gateA0", [128, 512], BF16), sa("gateA1", [128, 512], BF16)]
                        fb = sa("fbk", [128, 512])
                        bon2 = [sa("bon0", [128, 512]), sa("bon1", [128, 512])]
                        hX = {q: sa("hX%d" % q, [128, 512], BF16) for q in (1, 2, 4)}
                        tX = {q: sa("tX%d" % q, [128, 4, 128], BF16) for q in (0, 3)}
                        QX = [sa("QX0", [128, 8, 128], BF16), sa("QX1", [128, 8, 128], BF16)]
                        LX = [sa("LX%d" % q, [128, 8, 128], BF16) for q in range(3)]
                        f_ = [sa("f%d" % i, [128, 512]) for i in range(8)]
                        h_ = [sa("h%d" % i, [128, 512], BF16) for i in range(7)]
                        s8 = [sa("s8_%d" % i, [128, 8]) for i in range(6)]
                        tT = [sa("tT%d" % i, [128, 4, 128], BF16) for i in range(4)]
                        tw = sa("tw", [128, 128], BF16)
                        twT = sa("twT", [128, 128], BF16)
                        QQ = [[sa("Q%d%d" % (i, j), [128, 8, 128], BF16) for j in range(2)] for i in range(2)]
                        LL = [sa("LL%d" % i, [128, 8, 128], BF16) for i in range(3)]
                        XX = [sa("X%d" % i, [128, 8, 64], BF16) for i in range(2)]
                        gC2 = [sa("gC0", [128, 4]), sa("gC1", [128, 4])]
                        gT = sa("gTa", [128, 4, 128], BF16)
                        wkvst = sa("wkvst", [64, 8, 64])
                        wkvo = sa("wkvo", [64, 4, 128])
                        B_Wa, B_Wo, B_PRM, B_WAB, B_gb, B_S, B_Sb = Buf(), Buf(), Buf(), Buf(), Buf(), Buf(), Buf()
                        B_z = [Buf(), Buf()]
                        B_prev, B_junk, B_stat, B_ubf = Buf(), Buf(), Buf(), Buf()
                        B_gate2, B_bon2, B_fb = [Buf(), Buf()], [Buf(), Buf()], Buf()
                        B_hX = {q: Buf() for q in (1, 2, 4)}
                        B_tX = {q: Buf() for q in (0, 3)}
                        B_QX = [Buf(), Buf()]
                        B_LX = [Buf(), Buf(), Buf()]
                        B_ht = [Buf(), Buf()]
                        B_uT = [Buf(), Buf()]
                        B_f = [Buf() for _ in range(8)]
                        B_hh = [Buf() for _ in range(7)]
                        B_s8 = [Buf() for _ in range(6)]
                        B_tT = [Buf() for _ in range(4)]
                        B_tw, B_twT, B_gT, B_wst, B_wo2 = Buf(), Buf(), Buf(), Buf(), Buf()
                        B_gC2 = [Buf(), Buf()]
                        B_Q = [[Buf(), Buf()], [Buf(), Buf()]]
                        B_LL = [Buf(), Buf(), Buf()]
                        B_X = [Buf(), Buf()]

                        load_w(Wa, l, 0, 2176, B_Wa)
                        load_wo(Wo, l, 0, B_Wo)
                        S.dma("sp", gb[:], I["norm_g"][l, :].partition_broadcast(128), w=[B_gb])
                        S.dma("sp", PRM[:, 0:1664], I["a_mu"][l, :].partition_broadcast(128), w=[B_PRM], part=1)
                        for i, nm in enumerate(("a_w0", "a_a0", "a_k_k", "a_k_a", "a_r_k", "a_lnx_w", "a_lnx_b")):
                            S.dma("sp", PRM[:, 1664 + 512 * i:1664 + 512 * (i + 1)], I[nm][l, :].partition_broadcast(128), w=[B_PRM], part=2)
                        S.dma("pool", WAB[0:64, :], I["a_w_b"][l], w=[B_WAB], part=1)
                        S.dma("pool", WAB[64:128, :], I["a_a_b"][l], w=[B_WAB], part=2)
                        mu = PRM[:, 0:1664]
                        pw0, pa0, pkk, pka, prk, plw, plb = [PRM[:, 1664 + 512 * i:1664 + 512 * (i + 1)] for i in range(7)]
                        if past > 0:
                            S.dma("sp", wkvst[:], I["st_wkv"][l, b].rearrange("h i j -> i h j"), w=[B_wst])
                            for hp in range(4):
                                S.op("pe", lambda e, hp=hp: e.transpose(psum[:, 0, hp * 64:(hp + 1) * 64],
                                                                        wkvst[:, 2 * hp:2 * hp + 2, :].rearrange("i e j -> i (e j)"),
                                                                        idf[0:64, 0:64]), r=[B_wst, B_cf], w=[PB[0]])
                            S.op("dve", lambda e: e.tensor_copy(Sst[:].rearrange("p a i -> p (a i)"), psum[:, 0, 0:256]), r=[PB[0]], w=[B_S])
                            S.dma("sp", zz[1][127:128, :], I["st_shift"][l, b:b + 1, :], w=[B_z[1]])
                        else:
                            S.op("dve", lambda e: e.memset(Sst[:], 0.0), w=[B_S])
                            S.op("dve", lambda e: e.memset(zz[1][:], 0.0), w=[B_z[1]])
                        S.op("pool", lambda e: e.memset(Sbf[:].rearrange("p a b c -> p (a b c)"), 0.0), w=[B_Sb])
                        for eo in range(2):
                            S.op("dve", lambda e, eo=eo: e.tensor_copy(Sbf[eo * 64:eo * 64 + 64, eo, :, :], Sst[eo * 64:eo * 64 + 64, :, :]), r=[B_S], w=[B_Sb])

                        h_base, B_hh_base, tT_base, B_tT_base = h_, B_hh, tT, B_tT
                        QQ_base, B_Q_base, LL_base, B_LL_base = QQ, B_Q, LL, B_LL
                        for i in range(NT):
                            t0 = i * P
                            z = zz[i % 2]
                            zp = zz[(i + 1) % 2]
                            B_zc, B_zp = B_z[i % 2], B_z[(i + 1) % 2]
                            hti, B_hti = ht[i % 2], B_ht[i % 2]
                            uTi, B_uTi = uT[i % 2], B_uT[i % 2]
                            par = i % 2
                            gate, B_gate = gate2[par], B_gate2[par]
                            gC, B_gC = gC2[par], B_gC2[par]
                            bon, B_bon = bon2[par], B_bon2[par]
                            h_, B_hh, tT, B_tT = list(h_base), list(B_hh_base), list(tT_base), list(B_tT_base)
                            QQ, B_Q, LL, B_LL = [list(x) for x in QQ_base], [list(x) for x in B_Q_base], list(LL_base), list(B_LL_base)
                            if par == 1:
                                for q in (1, 2, 4):
                                    h_[q], B_hh[q] = hX[q], B_hX[q]
                                for q in (0, 3):
                                    tT[q], B_tT[q] = tX[q], B_tX[q]
                                QQ[0], B_Q[0] = QX, B_QX
                                LL, B_LL = LX, B_LX
                            sub(1)
                            def load_h(ii):
                                tt0 = ii * P
                                hsrc = xin[tt0:tt0 + P, :] if l == 0 else hb[tt0:tt0 + P, :]
                                S.dma("act", ht[ii % 2][0:P, :], hsrc, r=([] if l == 0 else [B_h[ii]]), w=[B_ht[ii % 2]])

                            if i == 0:
                                load_h(0)
                            if i + 1 < NT:
                                load_h(i + 1)
                            rms_scale(hti, P, junk, B_junk, stat, B_stat, B_hti)
                            S.op("dve", lambda e: e.scalar_tensor_tensor(out=ubf[0:P, :], in0=hti[0:P, :], scalar=stat[0:P, 2:3], in1=gb[0:P, :],
                                                                         op0=ALU.mult, op1=ALU.mult), r=[B_hti, B_stat, B_gb], w=[B_ubf])
                            transposes(ubf, 8, 128, P, 0, B_ubf, uTi[:, :, 0:P], B_uTi, evac="act")
                            S.dma("sp", ut[:, 8 * t0:8 * t0 + 8 * P].rearrange("p (k t) -> p k t", k=8), uTi[:, :, 0:P], r=[B_uTi], w=[B_ut[i]])
                            sub(2)
                            for cblk in range(3):
                                bank = 1 + cblk % 2
                                proj(uTi, Wa, cblk * 512, 512, P, bank, B_uTi, B_Wa)
                                S.op("act", lambda e, cblk=cblk, bank=bank: e.activation(out=z[0:P, cblk * 512:(cblk + 1) * 512],
                                                                                         in_=psum[0:P, bank, :], func=AF.Copy),
                                     r=[PB[bank]], w=[B_zc])
                            proj(uTi, Wa, 1536, 128, P, 2, B_uTi, B_Wa)
                            S.op("act", lambda e: e.activation(out=z[0:P, 1536:1664], in_=psum[0:P, 2, 0:128], func=AF.Copy), r=[PB[2]], w=[B_zc])
                            proj(uTi, Wa, 1664, 512, P, 1, B_uTi, B_Wa)
                            silu_gate(gate, P, 1, f_[0], B_f[0], f_[1], B_f[1], B_gate)
                            if i == NT - 1:
                                S.dma("sp", O["shift_" + g][l, b:b + 1, :], z[P - 1:P, :], r=[B_zc], w=[Buf()])
                            sub(3)
                            S.dma("act", prev[0:1, :], zp[127:128, :] if (i == 0 or P == 128) else zp[P - 1:P, :], r=[B_zp], w=[B_prev], part=1)
                            p0 = 0
                            while p0 < P - 1:
                                n_ = P - 1 - p0
                                n_ = (n_ // 16) * 16 if n_ >= 16 else n_
                                n_ = min(n_, 64)
                                S.dma("act", prev[1 + p0:1 + p0 + n_, :], z[p0:p0 + n_, :], r=[B_zc], w=[B_prev], part=2)
                                p0 += n_
                            S.op("dve", lambda e: e.tensor_tensor(prev[0:P, :], prev[0:P, :], z[0:P, :], ALU.subtract), r=[B_prev, B_zc], w=[B_prev])
                            S.op("dve", lambda e: e.tensor_tensor(prev[0:P, :], prev[0:P, :], mu[0:P, :], ALU.mult), r=[B_prev, B_PRM], w=[B_prev])
                            S.op("dve", lambda e: e.tensor_tensor(prev[0:P, :], prev[0:P, :], z[0:P, :], ALU.add), r=[B_prev, B_zc], w=[B_prev])
                            zr, zk, zv = prev[0:P, 0:512], prev[0:P, 512:1024], prev[0:P, 1024:1536]
                            sub(4)
                            S.op("act", lambda e: e.activation(out=tw[0:P, 0:64], in_=prev[0:P, 1536:1600], func=AF.Tanh), r=[B_prev], w=[B_tw])
                            S.op("dve", lambda e: e.tensor_copy(tw[0:P, 64:128], prev[0:P, 1600:1664]), r=[B_prev], w=[B_tw])
                            S.op("pe", lambda e: e.transpose(psum_bf[:, 0, 0:P], tw[0:P, :], idb[0:P, 0:P]), r=[B_tw, B_cb], w=[PB[0]])
                            S.op("act", lambda e: e.activation(out=twT[:, 0:P], in_=psum_bf[:, 0, 0:P], func=AF.Copy), r=[PB[0]], w=[B_twT])
                            S.op("pe", lambda e: e.matmul(psum[0:P, 1, :], lhsT=twT[0:64, 0:P], rhs=WAB[0:64, :], start=True, stop=True),
                                 r=[B_twT, B_WAB], w=[PB[1]])
                            S.op("pe", lambda e: e.matmul(psum[0:P, 2, :], lhsT=twT[64:128, 0:P], rhs=WAB[64:128, :], start=True, stop=True),
                                 r=[B_twT, B_WAB], w=[PB[2]])
                            S.op("dve", lambda e: e.tensor_tensor(f_[2][0:P, :], psum[0:P, 1, :], pw0[0:P, :], ALU.add), r=[PB[1], B_PRM], w=[B_f[2]])
                            affine_tanh(f_[2][0:P, :], f_[2][0:P, :], B_f[2], B_f[2], -0.5 * EXPM05, -0.5 * EXPM05)
                            S.op("dve", lambda e: e.tensor_tensor(f_[3][0:P, :], psum[0:P, 2, :], pa0[0:P, :], ALU.add), r=[PB[2], B_PRM], w=[B_f[3]])
                            affine_tanh(f_[3][0:P, :], f_[3][0:P, :], B_f[3], B_f[3], 0.5, 0.5)
                            logw, aa = f_[2], f_[3]
                            sub(5)
                            S.op("pool", lambda e: e.tensor_tensor(f_[4][0:P, :], zk, pkk[0:P, :], ALU.mult), r=[B_prev, B_PRM], w=[B_f[4]])
                            S.op("act", lambda e: e.activation(out=f_[0][0:P, :], in_=f_[4][0:P, :], func=AF.Square), r=[B_f[4]], w=[B_f[0]])
                            S.op("dve", lambda e: e.tensor_reduce(s8[0][0:P, :], f_[0][0:P, :].rearrange("p (h d) -> p h d", h=8), axis=AX.X, op=ALU.add),
                                 r=[B_f[0]], w=[B_s8[0]])
                            S.op("act", lambda e: e.activation(out=s8[1][0:P, :], in_=s8[0][0:P, :], func=AF.Ln, bias=1e-12), r=[B_s8[0]], w=[B_s8[1]])
                            S.op("act", lambda e: e.activation(out=s8[1][0:P, :], in_=s8[1][0:P, :], func=AF.Exp, scale=-0.5), r=[B_s8[1]], w=[B_s8[1]])
                            S.op("dve", lambda e: e.tensor_tensor(f_[4][0:P, :].rearrange("p (h d) -> p h d", h=8),
                                                                  f_[4][0:P, :].rearrange("p (h d) -> p h d", h=8),
                                                                  s8[1][0:P, :].unsqueeze(2).broadcast_to([P, 8, 64]), ALU.mult),
                                 r=[B_f[4], B_s8[1]], w=[B_f[4]])
                            kkn = f_[4]
                            S.op("dve", lambda e: e.scalar_tensor_tensor(out=f_[5][0:P, :], in0=aa[0:P, :], scalar=-1.0, in1=pka[0:P, :],
                                                                         op0=ALU.add, op1=ALU.mult), r=[B_f[3], B_PRM], w=[B_f[5]])
                            S.op("dve", lambda e: e.scalar_tensor_tensor(out=f_[5][0:P, :], in0=f_[5][0:P, :], scalar=1.0, in1=zk,
                                                                         op0=ALU.add, op1=ALU.mult), r=[B_f[5], B_prev], w=[B_f[5]])
                            k2 = f_[5]
                            S.op("pool", lambda e: e.tensor_tensor(f_[6][0:P, :], zr, k2[0:P, :], ALU.mult), r=[B_prev, B_f[5]], w=[B_f[6]])
                            S.op("pool", lambda e: e.tensor_tensor(f_[6][0:P, :], f_[6][0:P, :], prk[0:P, :], ALU.mult), r=[B_f[6], B_PRM], w=[B_f[6]])
                            S.op("dve", lambda e: e.tensor_reduce(s8[2][0:P, :], f_[6][0:P, :].rearrange("p (h d) -> p h d", h=8), axis=AX.X, op=ALU.add),
                                 r=[B_f[6]], w=[B_s8[2]])
                            S.op("dve", lambda e: e.tensor_tensor(bon[0:P, :].rearrange("p (h d) -> p h d", h=8),
                                                                  prev[0:P, 1024:1536].rearrange("p (h d) -> p h d", h=8),
                                                                  s8[2][0:P, :].unsqueeze(2).broadcast_to([P, 8, 64]), ALU.mult),
                                 r=[B_prev, B_s8[2]], w=[B_bon])
                            sub(6)
                            S.op("pe", lambda e: e.matmul(psum[0:P, 3, :], lhsT=tri[0:P, 0:P], rhs=logw[0:P, :], start=True, stop=True),
                                 r=[B_cf, B_f[2]], w=[PB[3]])
                            for hp in range(4):
                                S.op("pe", lambda e, hp=hp: e.matmul(psum[:, 0, hp:hp + 1], lhsT=logw[0:P, hp * 128:(hp + 1) * 128], rhs=onesf[0:P, 0:1],
                                                                     start=True, stop=True), r=[B_cf, B_f[2]], w=[PB[0]])
                            S.op("act", lambda e: e.activation(out=gC[:, :], in_=psum[:, 0, 0:4], func=AF.Exp), r=[PB[0]], w=[B_gC])
                            S.op("act", lambda e: e.activation(out=f_[6][0:P, :], in_=psum[0:P, 3, :], func=AF.Exp), r=[PB[3]], w=[B_f[6]])
                            S.op("act", lambda e: e.activation(out=f_[7][0:P, :], in_=psum[0:P, 3, :], func=AF.Exp, scale=-1.0), r=[PB[3]], w=[B_f[7]])
                            S.op("dve", lambda e: e.tensor_tensor(f_[0][0:P, :], psum[0:P, 3, :], logw[0:P, :], ALU.subtract), r=[PB[3], B_f[2]], w=[B_f[0]])
                            S.op("act", lambda e: e.activation(out=f_[0][0:P, :], in_=f_[0][0:P, :], func=AF.Exp), r=[B_f[0]], w=[B_f[0]])
                            sub(7)
                            S.op("dve", lambda e: e.tensor_tensor(h_[0][0:P, :], zr, f_[6][0:P, :], ALU.mult), r=[B_prev, B_f[6]], w=[B_hh[0]])
                            S.op("dve", lambda e: e.tensor_tensor(h_[1][0:P, :], k2[0:P, :], f_[7][0:P, :], ALU.mult), r=[B_f[5], B_f[7]], w=[B_hh[1]])
                            S.op("pool", lambda e: e.tensor_tensor(f_[1][0:P, :], kkn[0:P, :], aa[0:P, :], ALU.mult), r=[B_f[4], B_f[3]], w=[B_f[1]])
                            S.op("dve", lambda e: e.tensor_tensor(h_[2][0:P, :], f_[1][0:P, :], f_[7][0:P, :], ALU.mult), r=[B_f[1], B_f[7]], w=[B_hh[2]])
                            S.op("dve", lambda e: e.tensor_tensor(h_[3][0:P, :], kkn[0:P, :], f_[0][0:P, :], ALU.mult), r=[B_f[4], B_f[0]], w=[B_hh[3]])
                            S.op("pool", lambda e: e.tensor_copy(h_[4][0:P, :], zv), r=[B_prev], w=[B_hh[4]])
                            vbf = h_[4]
                            for q in range(4):
                                transposes(h_[q], 4, 128, P, 1 + q % 2, B_hh[q], tT[q][:, :, 0:P], B_tT[q], evac=("act" if q % 2 == 0 else "dve"))
                            rT, kT, bT, aT = tT

                            sub(8)
                            def pair_prod(lt, B_l, rt, B_r, dst, B_dst, mask, banks2):
                                for h in range(8):
                                    hp, po = h // 2, (h % 2) * 64
                                    bank = banks2 + h % 2
                                    S.op("pe", lambda e, hp=hp, po=po, bank=bank, h=h: e.matmul(
                                        psum[0:P, bank, hp * P:(hp + 1) * P], lhsT=lt[po:po + 64, hp, 0:P], rhs=rt[po:po + 64, hp, 0:P],
                                        start=True, stop=True), r=[B_l, B_r], w=[PB[bank]])
                                for par in range(2):
                                    bank = banks2 + par
                                    S.op("dve", lambda e, bank=bank, par=par: e.tensor_tensor(
                                        dst[0:P, par:8:2, 0:P], psum[0:P, bank, 0:4 * P].rearrange("p (h t) -> p h t", h=4),
                                        bc_h(mask, 4, P, P), ALU.mult), r=[PB[bank], B_cb], w=[B_dst])

                            pair_prod(bT, B_tT[2], aT, B_tT[3], QQ[0][1], B_Q[0][1], m_nsu, 2)
                            pair_prod(aT, B_tT[3], bT, B_tT[2], QQ[0][0], B_Q[0][0], m_nsl, 0)
                            pair_prod(kT, B_tT[1], aT, B_tT[3], LL[0], B_LL[0], m_su, 2)
                            pair_prod(bT, B_tT[2], rT, B_tT[0], LL[1], B_LL[1], m_iu, 0)
                            pair_prod(kT, B_tT[1], rT, B_tT[0], LL[2], B_LL[2], m_iu, 2)
                            sub(9)
                            for h in range(8):
                                hp, po = h // 2, (h % 2) * 64
                                S.op("pe", lambda e, h=h, hp=hp, po=po: e.matmul(psum[0:P, 6, h * 64:(h + 1) * 64], lhsT=aT[:, hp, 0:P],
                                                                                 rhs=Sbf[:, h % 2, hp, :], start=True, stop=False),
                                     r=[B_tT[3], B_Sb], w=[PB[6]])
                                S.op("pe", lambda e, h=h: e.matmul(psum[0:P, 6, h * 64:(h + 1) * 64], lhsT=LL[0][0:P, h, 0:P],
                                                                   rhs=vbf[0:P, h * 64:(h + 1) * 64], start=False, stop=True),
                                     r=[B_LL[0], B_hh[4]], w=[PB[6]])
                            S.op("act", lambda e: e.activation(out=XX[0][0:P, :, :].rearrange("p h d -> p (h d)"), in_=psum[0:P, 6, :], func=AF.Copy),
                                 r=[PB[6]], w=[B_X[0]])
                            sub(10)
                            nlev = int(round(math.log2(P)))
                            cur = 0
                            for lev in range(nlev):
                                Qc, QcT = QQ[cur][0], QQ[cur][1]
                                Xc, Xn = XX[lev % 2], XX[(lev + 1) % 2]
                                xb = 6 + (lev + 1) % 2
                                for h in range(8):
                                    S.op("pe", lambda e, h=h, xb=xb, QcT=QcT, Xc=Xc: e.matmul(psum[0:P, xb, h * 64:(h + 1) * 64], lhsT=QcT[0:P, h, 0:P],
                                                                                              rhs=Xc[0:P, h, :], start=True, stop=True),
                                         r=[B_Q[cur][1], B_X[lev % 2]], w=[PB[xb]])
                                last_lev = (lev == nlev - 1)
                                if last_lev:
                                    S.op("dve", lambda e, xb=xb, Xn=Xn, Xc=Xc: e.scalar_tensor_tensor(
                                        out=Xn[0:P, :, :].rearrange("p h d -> p (h d)"), in0=psum[0:P, xb, :], scalar=-1.0,
                                        in1=Xc[0:P, :, :].rearrange("p h d -> p (h d)"), op0=ALU.mult, op1=ALU.subtract),
                                        r=[PB[xb], B_X[lev % 2]], w=[B_X[(lev + 1) % 2]])
                                else:
                                    S.op("dve", lambda e, xb=xb, Xn=Xn, Xc=Xc: e.tensor_tensor(
                                        Xn[0:P, :, :].rearrange("p h d -> p (h d)"), psum[0:P, xb, :],
                                        Xc[0:P, :, :].rearrange("p h d -> p (h d)"), ALU.add),
                                        r=[PB[xb], B_X[lev % 2]], w=[B_X[(lev + 1) % 2]])
                                if not last_lev:
                                    nxt = 1 - cur
                                    Qn, QnT = QQ[nxt][0], QQ[nxt][1]
                                    for h in range(8):
                                        S.op("pe", lambda e, h=h: e.matmul(psum[0:P, 4 + h // 4, (h % 4) * P:(h % 4 + 1) * P], lhsT=Qc[0:P, h, 0:P],
                                                                           rhs=QcT[0:P, h, 0:P], start=True, stop=True),
                                             r=[B_Q[cur][0], B_Q[cur][1]], w=[PB[4 + h // 4]])
                                    for half in range(2):
                                        S.op("dve", lambda e, half=half: e.tensor_copy(QnT[0:P, half * 4:half * 4 + 4, 0:P],
                                                                                       psum[0:P, 4 + half, 0:4 * P].rearrange("p (h t) -> p h t", h=4)),
                                             r=[PB[4 + half]], w=[B_Q[nxt][1]])
                                    for h in range(8):
                                        S.op("pe", lambda e, h=h: e.matmul(psum[0:P, 4 + h // 4, (h % 4) * P:(h % 4 + 1) * P], lhsT=QcT[0:P, h, 0:P],
                                                                           rhs=Qc[0:P, h, 0:P], start=True, stop=True),
                                             r=[B_Q[cur][0], B_Q[cur][1]], w=[PB[4 + h // 4]])
                                    for half in range(2):
                                        S.op("act", lambda e, half=half: e.activation(out=Qn[0:P, half * 4:half * 4 + 4, 0:P],
                                                                                      in_=psum[0:P, 4 + half, 0:4 * P].rearrange("p (h t) -> p h t", h=4),
                                                                                      func=AF.Copy), r=[PB[4 + half]], w=[B_Q[nxt][0]])
                                    cur = nxt
                            NU, B_NU = XX[nlev % 2], B_X[nlev % 2]
                            sub(11)
                            for h in range(8):
                                hp, po = h // 2, (h % 2) * 64
                                S.op("pe", lambda e, h=h, hp=hp, po=po: e.matmul(psum[0:P, 4, h * 64:(h + 1) * 64], lhsT=rT[:, hp, 0:P],
                                                                                 rhs=Sbf[:, h % 2, hp, :], start=True, stop=False),
                                     r=[B_tT[0], B_Sb], w=[PB[4]])
                                S.op("pe", lambda e, h=h: e.matmul(psum[0:P, 4, h * 64:(h + 1) * 64], lhsT=LL[1][0:P, h, 0:P], rhs=NU[0:P, h, :],
                                                                   start=False, stop=False), r=[B_LL[1], B_NU], w=[PB[4]])
                                S.op("pe", lambda e, h=h: e.matmul(psum[0:P, 4, h * 64:(h + 1) * 64], lhsT=LL[2][0:P, h, 0:P],
                                                                   rhs=vbf[0:P, h * 64:(h + 1) * 64], start=False, stop=True),
                                     r=[B_LL[2], B_hh[4]], w=[PB[4]])
                            sub(12)
                            for hp in range(4):
                                for eo in range(2):
                                    h = 2 * hp + eo
                                    osl = psum[:, 5, (hp * 2 + eo) * 64:(hp * 2 + eo + 1) * 64]
                                    S.op("pe", lambda e, hp=hp, h=h, osl=osl: e.matmul(osl, lhsT=h_[1][0:P, hp * 128:(hp + 1) * 128],
                                                                                       rhs=vbf[0:P, h * 64:(h + 1) * 64], start=True, stop=False),
                                         r=[B_hh[1], B_hh[4]], w=[PB[5]])
                                    S.op("pe", lambda e, hp=hp, h=h, osl=osl: e.matmul(osl, lhsT=h_[2][0:P, hp * 128:(hp + 1) * 128],
                                                                                       rhs=NU[0:P, h, :], start=False, stop=True),
                                         r=[B_hh[2], B_NU], w=[PB[5]])
                            for eo in range(2):
                                pr = slice(eo * 64, eo * 64 + 64)
                                S.op("dve", lambda e, pr=pr, eo=eo: e.tensor_tensor(
                                    Sst[pr, :, :], Sst[pr, :, :], psum[pr, 5, :].rearrange("p (a e i) -> p a e i", a=4, e=2)[:, :, eo, :], ALU.add),
                                    r=[PB[5], B_S], w=[B_S])
                            S.op("dve", lambda e: e.tensor_tensor(Sst[:], Sst[:], gC[:, :].unsqueeze(2).broadcast_to([128, 4, 64]), ALU.mult),
                                 r=[B_S, B_gC], w=[B_S])
                            for eo in range(2):
                                S.op("act", lambda e, eo=eo: e.activation(out=Sbf[eo * 64:eo * 64 + 64, eo, :, :], in_=Sst[eo * 64:eo * 64 + 64, :, :],
                                                                          func=AF.Copy), r=[B_S], w=[B_Sb])
                            sub(13)
                            Y3 = psum[0:P, 4, :].rearrange("p (h d) -> p h d", h=8)
                            S.op("dve", lambda e: e.tensor_reduce(s8[3][0:P, :], Y3, axis=AX.X, op=ALU.add), r=[PB[4]], w=[B_s8[3]])
                            S.op("act", lambda e: e.activation(out=fb[0:P, :], in_=psum[0:P, 4, :], func=AF.Square), r=[PB[4]], w=[B_fb])
                            S.op("dve", lambda e: e.tensor_reduce(s8[4][0:P, :], fb[0:P, :].rearrange("p (h d) -> p h d", h=8), axis=AX.X, op=ALU.add),
                                 r=[B_fb], w=[B_s8[4]])
                            S.op("dve", lambda e: e.tensor_scalar_mul(s8[3][0:P, :], s8[3][0:P, :], 1.0 / 64), r=[B_s8[3]], w=[B_s8[3]])
                            S.op("dve", lambda e: e.tensor_tensor(s8[5][0:P, :], s8[3][0:P, :], s8[3][0:P, :], ALU.mult), r=[B_s8[3]], w=[B_s8[5]])
                            S.op("dve", lambda e: e.scalar_tensor_tensor(out=s8[4][0:P, :], in0=s8[4][0:P, :], scalar=1.0 / 64, in1=s8[5][0:P, :],
                                                                         op0=ALU.mult, op1=ALU.subtract), r=[B_s8[4], B_s8[5]], w=[B_s8[4]])
                            S.op("act", lambda e: e.activation(out=s8[4][0:P, :], in_=s8[4][0:P, :], func=AF.Ln, bias=GN_EPS), r=[B_s8[4]], w=[B_s8[4]])
                            S.op("act", lambda e: e.activation(out=s8[4][0:P, :], in_=s8[4][0:P, :], func=AF.Exp, scale=-0.5), r=[B_s8[4]], w=[B_s8[4]])
                            f3v = lambda t: t[0:P, :].rearrange("p (h d) -> p h d", h=8)
                            b8 = lambda t: t[0:P, :].unsqueeze(2).broadcast_to([P, 8, 64])
                            S.op("dve", lambda e: e.tensor_tensor(f3v(fb), Y3, b8(s8[3]), ALU.subtract), r=[PB[4], B_s8[3]], w=[B_fb])
                            S.op("dve", lambda e: e.tensor_tensor(f3v(fb), f3v(fb), b8(s8[4]), ALU.mult), r=[B_fb, B_s8[4]], w=[B_fb])
                            S.op("pool", lambda e: e.tensor_tensor(fb[0:P, :], fb[0:P, :], plw[0:P, :], ALU.mult), r=[B_fb, B_PRM], w=[B_fb])
                            S.op("pool", lambda e: e.tensor_tensor(fb[0:P, :], fb[0:P, :], plb[0:P, :], ALU.add), r=[B_fb, B_PRM], w=[B_fb])
                            S.op("pool", lambda e: e.tensor_tensor(fb[0:P, :], fb[0:P, :], bon[0:P, :], ALU.add), r=[B_fb, B_bon], w=[B_fb])
                            S.op("dve", lambda e: e.tensor_tensor(h_[5][0:P, :], fb[0:P, :], gate[0:P, :], ALU.mult), r=[B_fb, B_gate], w=[B_hh[5]])
                            sub(14)
                            transposes(h_[5], 4, 128, P, 6, B_hh[5], gT[:, :, 0:P], B_gT, evac="act")
                            for half in range(2):
                                bank = 7 - half
                                for k in range(4):
                                    S.op("pe", lambda e, k=k, half=half, bank=bank: e.matmul(
                                        psum[0:P, bank, :], lhsT=gT[:, k, 0:P], rhs=Wo[:, k, half * 512:(half + 1) * 512],
                                        start=(k == 0), stop=(k == 3)), r=[B_gT, B_Wo], w=[PB[bank]])
                                S.op("dve", lambda e, half=half, bank=bank: e.tensor_tensor(
                                    hti[0:P, half * 512:(half + 1) * 512], hti[0:P, half * 512:(half + 1) * 512], psum[0:P, bank, :], ALU.add),
                                    r=[PB[bank], B_hti], w=[B_hti])
                            S.dma("sp", hb[t0:t0 + P, :], hti[0:P, :], r=[B_hti], w=[B_h[i]])
                        sub(15)
                        for hp in range(4):
                            S.op("pe", lambda e, hp=hp: e.transpose(psum[0:64, 0, hp * 128:(hp + 1) * 128], Sst[:, hp, :], idf[:, :]),
                                 r=[B_S, B_cf], w=[PB[0]])
                        S.op("dve", lambda e: e.tensor_copy(wkvo[:].rearrange("p a b -> p (a b)"), psum[0:64, 0, :]), r=[PB[0]], w=[B_wo2])
                        S.dma("sp", O["wkv_" + g][l, b].rearrange("h i j -> i h j"), wkvo[:].rearrange("p a (e j) -> p (a e) j", e=2),
                              r=[B_wo2], w=[Buf()])
                        if _USE_SCHED:
                            S.end()
                    S.barrier()
                    checkpoint()

                    with ExitStack() as pbc:
                        Wc = pbc.enter_context(nc.sbuf_tensor(uname("Wc"), [128, 8, 1604], BF16))
                        Woc = pbc.enter_context(nc.sbuf_tensor(uname("Woc"), [128, 4, 1024], BF16))
                        B_Wc, B_Woc = Buf(), Buf()
                        with ExitStack() as pb_:
                            if _USE_SCHED:
                                S.begin()
                            sb = lambda name, shape, dt=F32: pb_.enter_context(nc.sbuf_tensor(uname(name), list(shape), dt))
                            Wb = sb("Wb", [128, 8, 2056], BF16)
                            Wo = sb("Wob", [128, 4, 1024], BF16)
                            KT = sb("KT", [128, 4, NK], BF16)
                            V = sb("Vb", [128, NKT, 8, 65], BF16)
                            C = sb("Cc", [128, NKT, 8])
                            nbias2 = [sb("nbias0", [128, NKT, 8]), sb("nbias1", [128, NKT, 8])]
                            tot = sb("tot", [128, 8])
                            fbias = sb("fbias", [128, 8])
                            LOGF = sb("LOGF", [128, NT, 8])
                            uT = [sb("uTb0", [128, 8, 128], BF16), sb("uTb1", [128, 8, 128], BF16)]
                            kf = [sb("kf0", [128, 512]), sb("kf1", [128, 512])]
                            vf = [sb("vf0", [128, 512]), sb("vf1", [128, 512])]
                            kbf = sb("kbf", [128, 512], BF16)
                            qbf = sb("qbf", [128, 512], BF16)
                            qT2 = [sb("qT0", [128, 4, 128], BF16), sb("qT1", [128, 4, 128], BF16)]
                            gate2 = [sb("gateb0", [128, 512], BF16), sb("gateb1", [128, 512], BF16)]
                            t1 = sb("t1b", [128, 512])
                            t2 = sb("t2b", [128, 512])
                            lf = sb("lf", [128, 8])
                            lf2 = sb("lf2", [128, 8])
                            cref = sb("cref", [128, 8])
                            pT = [sb("pT0", [128, 8, 128], BF16), sb("pT1", [128, 8, 128], BF16)]
                            rec = sb("rec", [128, 8])
                            yb = sb("yb", [128, 512])
                            gated = sb("gatedb", [128, 512], BF16)
                            gT = sb("gTb", [128, 4, 128], BF16)
                            ht = [sb("htb0", [128, D]), sb("htb1", [128, D])]
                            B_W, B_Wo, B_tot, B_fb, B_LOGF = [Buf() for _ in range(5)]
                            B_KTj = [Buf() for _ in range(NKT)]
                            B_Vj = [Buf() for _ in range(NKT)]
                            B_Cj = [Buf() for _ in range(NKT)]
                            B_nb2, B_qT2, B_gate2 = [Buf(), Buf()], [Buf(), Buf()], [Buf(), Buf()]
                            B_uT = [Buf(), Buf()]
                            B_kf = [Buf(), Buf()]
                            B_vf = [Buf(), Buf()]
                            B_kbf, B_qbf, B_t1, B_t2, B_lf, B_lf2, B_cref, B_rec, B_yb, B_gated, B_gT = [Buf() for _ in range(11)]
                            B_pT = [Buf(), Buf()]
                            B_ht = [Buf(), Buf()]
                            load_w(Wb, l, 2176, 2056, B_W)
                            load_wo(Wo, l, 512, B_Wo)
                            load_w(Wc, l, 4232 + 512, 1604 - 512, B_Wc, dst0=512)
                            for gg in range(4):
                                for n_ in range(2):
                                    cs = 4232 + n_ * 256 + gg * 64
                                    S.dma("pool", Wc[:, :, gg * 128 + n_ * 64:gg * 128 + n_ * 64 + 64],
                                          I["w_in"][l, :, cs:cs + 64].rearrange("(k p) n -> p k n", p=128), w=[B_Wc], part=2)
                            load_wo(Woc, l, 1024, B_Woc)
                            S.dma("sp", fbias[:], I["b_f_bias"][l, :].partition_broadcast(128), w=[B_fb])
                            S.op("pool", lambda e: e.memset(V[:].rearrange("p a h d -> p (a h d)"), 1.0), w=B_Vj)
                            S.op("dve", lambda e: e.memset(tot[:], 0.0), w=[B_tot])
                            S.op("pool", lambda e: e.memset(C[:].rearrange("p a h -> p (a h)"), 0.0), w=B_Cj)

                            def add_keys(j, kp, k_ap, B_k, v_ap, B_v, lf_ap, B_lf):
                                transposes(None, 4, 128, kp, 0, B_k, KT[:, :, j * 128:j * 128 + kp], B_KTj[j], evac="dve",
                                           src_blocks=[k_ap[0:kp, hp * 128:(hp + 1) * 128] for hp in range(4)])
                                S.op("act", lambda e: e.activation(out=V[0:kp, j, :, 0:64], in_=v_ap.rearrange("p (h d) -> p h d", h=8), func=AF.Copy),
                                     r=[B_v], w=[B_Vj[j]])
                                S.op("pe", lambda e: e.matmul(psum[0:kp, 1, 0:8], lhsT=tri[0:kp, 0:kp], rhs=lf_ap, start=True, stop=True),
                                     r=[B_cf, B_lf], w=[PB[1]])
                                S.op("pe", lambda e: e.matmul(psum[:, 1, 8:16], lhsT=e0row[0:kp, :], rhs=lf_ap, start=True, stop=True),
                                     r=[B_cf, B_lf], w=[PB[1]])
                                S.op("pe", lambda e: e.matmul(psum[:, 1, 16:24], lhsT=onesf[0:kp, :], rhs=lf_ap, start=True, stop=True),
                                     r=[B_cf, B_lf], w=[PB[1]])
                                S.op("dve", lambda e: e.tensor_tensor(C[0:kp, j, :], psum[0:kp, 1, 0:8], tot[0:kp, :], ALU.add), r=[PB[1], B_tot], w=[B_Cj[j]])
                                S.op("dve", lambda e: e.tensor_tensor(cref[:, :], psum[:, 1, 8:16], tot[:, :], ALU.add), r=[PB[1], B_tot], w=[B_cref])
                                S.op("dve", lambda e: e.tensor_tensor(tot[:], tot[:], psum[:, 1, 16:24], ALU.add), r=[PB[1], B_tot], w=[B_tot])

                            for j in range(NPT):
                                kfj, vfj = kf[j % 2], vf[j % 2]
                                S.dma("sp", kfj[:], I["cb_k"][l, b, j * 128:(j + 1) * 128, :], w=[B_kf[j % 2]])
                                S.dma("sp", vfj[:], I["cb_v"][l, b, j * 128:(j + 1) * 128, :], w=[B_vf[j % 2]])
                                S.dma("sp", lf[:], I["cb_logf"][l, b, j * 128:(j + 1) * 128, :], w=[B_lf])
                                S.op("pool", lambda e, kfj=kfj: e.tensor_copy(kbf[:], kfj[:]), r=[B_kf[j % 2]], w=[B_kbf])
                                add_keys(j, 128, kbf, B_kbf, vfj[:, :], B_vf[j % 2], lf[:, :], B_lf)

                            def tile_ctx_b(i):
                                par = i % 2
                                return (i * P, NPT + i, uT[par], B_uT[par], kf[par], vf[par], ht[par], B_ht[par],
                                        qT2[par], B_qT2[par], gate2[par], B_gate2[par], nbias2[par], B_nb2[par])

                            def front_b(i):
                                t0, j_new, uTi, B_uTi, kfi, vfi, hti, B_hti, qT, B_qT, gate, B_gate, nbias, B_nb = tile_ctx_b(i)
                                S.dma("act", uTi[:, :, 0:P], ut[:, 8 * t0:8 * t0 + 8 * P].rearrange("p (k t) -> p k t", k=8), r=[B_ut[i]], w=[B_uTi])
                                S.dma("act", hti[0:P, :], hb[t0:t0 + P, :], r=[B_h[i]], w=[B_hti])
                                proj(uTi, Wb, 0, 512, P, 1, B_uTi, B_W)
                                S.op("act", lambda e: e.activation(out=qbf[0:P, :], in_=psum[0:P, 1, :], func=AF.Copy), r=[PB[1]], w=[B_qbf])
                                transposes(qbf, 4, 128, P, 0, B_qbf, qT[:, :, 0:P], B_qT, evac="dve")
                                yield
                                proj(uTi, Wb, 512, 512, P, 1, B_uTi, B_W)
                                S.op("act", lambda e: e.activation(out=kfi[0:P, :], in_=psum[0:P, 1, :], func=AF.Copy), r=[PB[1]], w=[B_kf[i % 2]])
                                S.op("dve", lambda e: e.tensor_copy(kbf[0:P, :], psum[0:P, 1, :]), r=[PB[1]], w=[B_kbf])
                                S.dma("sp", O["bk_" + g][l, b, t0:t0 + P, :], kfi[0:P, :], r=[B_kf[i % 2]], w=[Buf()])
                                yield
                                proj(uTi, Wb, 1024, 512, P, 1, B_uTi, B_W)
                                S.op("act", lambda e: e.activation(out=vfi[0:P, :], in_=psum[0:P, 1, :], func=AF.Copy), r=[PB[1]], w=[B_vf[i % 2]])
                                S.dma("sp", O["bv_" + g][l, b, t0:t0 + P, :], vfi[0:P, :], r=[B_vf[i % 2]], w=[Buf()])
                                yield
                                proj(uTi, Wb, 1536, 8, P, 1, B_uTi, B_W)
                                S.op("dve", lambda e: e.tensor_tensor(lf2[0:P, :], psum[0:P, 1, 0:8], fbias[0:P, :], ALU.add), r=[PB[1], B_fb], w=[B_lf2])
                                S.op("act", lambda e: e.activation(out=lf2[0:P, :], in_=lf2[0:P, :], func=AF.Exp, scale=-1.0), r=[B_lf2], w=[B_lf2])
                                S.op("act", lambda e: e.activation(out=lf2[0:P, :], in_=lf2[0:P, :], func=AF.Ln, bias=1.0), r=[B_lf2], w=[B_lf2])
                                S.op("dve", lambda e: e.tensor_scalar_mul(LOGF[0:P, i, :], lf2[0:P, :], -1.0), r=[B_lf2], w=[B_LOGF])
                                yield
                                proj(uTi, Wb, 1544, 512, P, 1, B_uTi, B_W)
                                silu_gate(gate, P, 1, t1, B_t1, t2, B_t2, B_gate)
                                yield
                                add_keys(j_new, P, kbf, B_kbf, vfi[0:P, :], B_vf[i % 2], LOGF[0:P, i, :], B_LOGF)
                                nk = j_new + 1
                                S.op("dve", lambda e: e.tensor_tensor(nbias[:, 0:nk, :], cref[:, :].unsqueeze(1).broadcast_to([128, nk, 8]),
                                                                      C[:, 0:nk, :], ALU.subtract), r=[B_cref] + B_Cj[0:nk], w=[B_nb])
                                yield

                            def back_b(i):
                                t0, j_new, uTi, B_uTi, kfi, vfi, hti, B_hti, qT, B_qT, gate, B_gate, nbias, B_nb = tile_ctx_b(i)
                                nk = j_new + 1
                                def fox_qk(j):
                                    kp = kp_of(j)
                                    sbank = 2 + 2 * (j % 2)
                                    for h in range(8):
                                        hp, po = h // 2, (h % 2) * 64
                                        bank = sbank + h % 2
                                        S.op("pe", lambda e, h=h, hp=hp, po=po, bank=bank, j=j, kp=kp: e.matmul(
                                            psum[0:kp, bank, hp * P:(hp + 1) * P], lhsT=KT[po:po + 64, hp, j * 128:j * 128 + kp],
                                            rhs=qT[po:po + 64, hp, 0:P], start=True, stop=True), r=[B_KTj[j], B_qT], w=[PB[bank]])

                                def fox_exp(j):
                                    kp = kp_of(j)
                                    sbank = 2 + 2 * (j % 2)
                                    pTj, B_pTj = pT[j % 2], B_pT[j % 2]
                                    for h in range(8):
                                        bank = sbank + h % 2
                                        S.op("act", lambda e, h=h, bank=bank, j=j, kp=kp, pTj=pTj: e.activation(
                                            out=pTj[0:kp, h, 0:P], in_=psum[0:kp, bank, (h // 2) * P:(h // 2 + 1) * P], func=AF.Exp,
                                            scale=0.125, bias=nbias[0:kp, j, h:h + 1]), r=[PB[bank], B_nb], w=[B_pTj])
                                    if j == nk - 1:
                                        S.op("dve", lambda e, kp=kp, pTj=pTj: e.tensor_tensor(pTj[0:kp, :, 0:P], pTj[0:kp, :, 0:P], bc_h(m_iu, 8, kp, P), ALU.mult),
                                             r=[B_pTj, B_cb], w=[B_pTj])

                                def fox_pv(j):
                                    kp = kp_of(j)
                                    pTj, B_pTj = pT[j % 2], B_pT[j % 2]
                                    for h in range(8):
                                        ab = 6 + h // 4
                                        S.op("pe", lambda e, h=h, ab=ab, j=j, kp=kp, pTj=pTj: e.matmul(
                                            psum[0:P, ab, (h % 4) * 65:(h % 4 + 1) * 65], lhsT=pTj[0:kp, h, 0:P], rhs=V[0:kp, j, h, :],
                                            start=(j == 0 and h % 4 == 0), stop=(j == nk - 1), skip_group_check=True), r=[B_pTj, B_Vj[j]], w=[PB[ab]])

                                fox_qk(0)
                                for j in range(nk):
                                    if j + 1 < nk:
                                        fox_qk(j + 1)
                                    fox_exp(j)
                                    fox_pv(j)
                                    yield
                                for half in range(2):
                                    ab = 6 + half
                                    acc3 = psum[0:P, ab, 0:260].rearrange("p (h d) -> p h d", h=4)
                                    S.op("dve", lambda e, half=half, acc3=acc3: e.reciprocal(rec[0:P, half * 4:half * 4 + 4].unsqueeze(2), acc3[:, :, 64:65]),
                                         r=[PB[ab]], w=[B_rec])
                                    S.op("dve", lambda e, half=half, acc3=acc3: e.tensor_tensor(
                                        yb[0:P, half * 256:(half + 1) * 256].rearrange("p (h d) -> p h d", h=4), acc3[:, :, 0:64],
                                        rec[0:P, half * 4:half * 4 + 4].unsqueeze(2).broadcast_to([P, 4, 64]), ALU.mult), r=[PB[ab], B_rec], w=[B_yb])
                                S.op("pool", lambda e: e.tensor_tensor(gated[0:P, :], yb[0:P, :], gate[0:P, :], ALU.mult), r=[B_yb, B_gate], w=[B_gated])
                                transposes(gated, 4, 128, P, 2, B_gated, gT[:, :, 0:P], B_gT, evac="act")
                                yield
                                for half in range(2):
                                    bank = 3 + half
                                    for k in range(4):
                                        S.op("pe", lambda e, k=k, half=half, bank=bank: e.matmul(
                                            psum[0:P, bank, :], lhsT=gT[:, k, 0:P], rhs=Wo[:, k, half * 512:(half + 1) * 512],
                                            start=(k == 0), stop=(k == 3)), r=[B_gT, B_Wo], w=[PB[bank]])
                                    S.op("dve", lambda e, half=half, bank=bank: e.tensor_tensor(
                                        hti[0:P, half * 512:(half + 1) * 512], hti[0:P, half * 512:(half + 1) * 512], psum[0:P, bank, :], ALU.add),
                                        r=[PB[bank], B_hti], w=[B_hti])
                                S.dma("sp", hb[t0:t0 + P, :], hti[0:P, :], r=[B_hti], w=[B_h[i]])
                                yield

                            pipeline(front_b, back_b, NT)
                            S.dma("sp", O["blogf_" + g][l, b].rearrange("(t p) h -> p t h", p=P), LOGF[0:P, :, :], r=[B_LOGF], w=[Buf()])
                            if _USE_SCHED:
                                S.end()
                        S.barrier()
                        checkpoint()

                        with ExitStack() as pc_:
                            if _USE_SCHED:
                                S.begin()
                            sc = lambda name, shape, dt=F32: pc_.enter_context(nc.sbuf_tensor(uname(name), list(shape), dt))
                            Wo = Woc
                            KcT = sc("KcT", [128, NK], BF16)
                            Vc = sc("Vc", [128, NKT, 2, 65], BF16)
                            KiT = sc("KiT", [64, NK], BF16)
                            uT = [sc("uTc0", [128, 8, 128], BF16), sc("uTc1", [128, 8, 128], BF16)]
                            kvf = [sc("kvf0", [128, 256]), sc("kvf1", [128, 256])]
                            kif = [sc("kif0", [128, 64]), sc("kif1", [128, 64])]
                            kcb = sc("kcb", [128, 128], BF16)
                            kib = sc("kib", [128, 64], BF16)
                            qbf = sc("qbfc", [128, 512], BF16)
                            qib = sc("qib", [128, 256], BF16)
                            qT2 = [sc("qTc0", [128, 512], BF16), sc("qTc1", [128, 512], BF16)]
                            qiT = sc("qiT", [64, 4, 128], BF16)
                            wi = sc("wi", [128, 4])
                            gate2 = [sc("gatec0", [128, 512], BF16), sc("gatec1", [128, 512], BF16)]
                            t1 = sc("t1c", [128, 512])
                            t2 = sc("t2c", [128, 512])
                            SC2 = [sc("SC0", [128, NK]), sc("SC1", [128, NK])]
                            WK2 = [sc("WK0", [128, NK]), sc("WK1", [128, NK])]
                            RL = [sc("RL0", [128, 512]), sc("RL1", [128, 512])]
                            m82 = [sc("m8_0", [128, 8]), sc("m8_1", [128, 8])]
                            bs2 = [sc("bs_0", [128, 4]), sc("bs_1", [128, 4])]
                            B_bs2 = [Buf(), Buf()]
                            msk2 = [sc("msk0", [128, NK], BF16), sc("msk1", [128, NK], BF16)]
                            MT2 = [sc("MT0", [128, NKT, 128], BF16), sc("MT1", [128, NKT, 128], BF16)]
                            pT = [sc("pTc0", [128, 8, 128], BF16), sc("pTc1", [128, 8, 128], BF16)]
                            rec = sc("recc", [128, 8])
                            yb = sc("ybc", [128, 512])
                            gated = sc("gatedc", [128, 512], BF16)
                            gT = sc("gTc", [128, 4, 128], BF16)
                            ht = [sc("htc0", [128, D]), sc("htc1", [128, D])]
                            fin_g = sc("fin_g", [128, D])
                            junk = sc("junkc", [128, D], BF16)
                            stat = sc("statc", [128, 4])
                            B_W, B_Wo = B_Wc, B_Woc
                            B_KcTj = [Buf() for _ in range(NKT)]
                            B_Vcj = [Buf() for _ in range(NKT)]
                            B_KiTj = [Buf() for _ in range(NKT)]
                            B_qT2, B_gate2, B_MT2 = [Buf(), Buf()], [Buf(), Buf()], [Buf(), Buf()]
                            B_uT = [Buf(), Buf()]
                            B_kvf = [Buf(), Buf()]
                            B_kif = [Buf(), Buf()]
                            B_kcb, B_kib, B_qbf, B_qib, B_qiT, B_wi, B_t1, B_t2 = [Buf() for _ in range(8)]
                            B_rec, B_yb, B_gated, B_gT, B_fg, B_junk, B_stat = [Buf() for _ in range(7)]
                            B_SC2, B_WK2, B_m82, B_msk2 = [Buf(), Buf()], [Buf(), Buf()], [Buf(), Buf()], [Buf(), Buf()]
                            B_RL = [Buf(), Buf()]
                            B_pT = [Buf(), Buf()]
                            B_ht = [Buf(), Buf()]
                            if last:
                                S.dma("sp", fin_g[:], I["final_g"].partition_broadcast(128), w=[B_fg])
                            S.op("pool", lambda e: e.memset(Vc[:].rearrange("p a h d -> p (a h d)"), 1.0), w=B_Vcj)

                            def add_keys_c(j, kp, k_ap, B_k, v_ap, B_v, ki_ap, B_ki):
                                S.op("pe", lambda e: e.transpose(psum_bf[:, 0, 0:kp], k_ap, idb[0:kp, 0:kp]), r=[B_k, B_cb], w=[PB[0]])
                                S.op("pe", lambda e: e.transpose(psum_bf[0:64, 0, 128:128 + kp], ki_ap, idb[0:kp, 0:kp]), r=[B_ki, B_cb], w=[PB[0]])
                                S.op("dve", lambda e: e.tensor_copy(KcT[:, j * 128:j * 128 + kp], psum_bf[:, 0, 0:kp]), r=[PB[0]], w=[B_KcTj[j]])
                                S.op("dve", lambda e: e.tensor_copy(KiT[:, j * 128:j * 128 + kp], psum_bf[0:64, 0, 128:128 + kp]), r=[PB[0]], w=[B_KiTj[j]])
                                S.op("act", lambda e: e.activation(out=Vc[0:kp, j, :, 0:64], in_=v_ap.rearrange("p (h d) -> p h d", h=2), func=AF.Copy),
                                     r=[B_v], w=[B_Vcj[j]])

                            for j in range(NPT):
                                kvj, kij = kvf[j % 2], kif[j % 2]
                                S.dma("sp", kvj[:, 0:128], I["cc_k"][l, b, j * 128:(j + 1) * 128, :], w=[B_kvf[j % 2]], part=1)
                                S.dma("sp", kvj[:, 128:256], I["cc_v"][l, b, j * 128:(j + 1) * 128, :], w=[B_kvf[j % 2]], part=2)
                                S.dma("sp", kij[:], I["cc_kidx"][l, b, j * 128:(j + 1) * 128, :], w=[B_kif[j % 2]])
                                S.op("pool", lambda e, kvj=kvj: e.tensor_copy(kcb[:], kvj[:, 0:128]), r=[B_kvf[j % 2]], w=[B_kcb])
                                S.op("pool", lambda e, kij=kij: e.tensor_copy(kib[:], kij[:]), r=[B_kif[j % 2]], w=[B_kib])
                                add_keys_c(j, 128, kcb[:, :], B_kcb, kvj[:, 128:256], B_kvf[j % 2], kib[:, :], B_kib)

                            def tile_ctx_c(i):
                                par = i % 2
                                return (i * P, NPT + i, uT[par], B_uT[par], kvf[par], kif[par], ht[par], B_ht[par],
                                        qT2[par], B_qT2[par], gate2[par], B_gate2[par], MT2[par], B_MT2[par])

                            def front_c(i):
                                t0, j_new, uTi, B_uTi, kvi, kii, hti, B_hti, qT, B_qT, gate, B_gate, MT, B_MT = tile_ctx_c(i)
                                nk = j_new + 1
                                nvis = j_new * 128 + P
                                SC, WK, m8, msk = SC2[i % 2], WK2[i % 2], m82[i % 2], msk2[i % 2]
                                B_SC, B_WK, B_m8, B_msk = B_SC2[i % 2], B_WK2[i % 2], B_m82[i % 2], B_msk2[i % 2]
                                S.dma("act", uTi[:, :, 0:P], ut[:, 8 * t0:8 * t0 + 8 * P].rearrange("p (k t) -> p k t", k=8), r=[B_ut[i]], w=[B_uTi])
                                S.dma("act", hti[0:P, :], hb[t0:t0 + P, :], r=[B_h[i]], w=[B_hti])
                                proj(uTi, Wc, 0, 512, P, 1, B_uTi, B_W)
                                S.op("act", lambda e: e.activation(out=qbf[0:P, :], in_=psum[0:P, 1, :], func=AF.Copy), r=[PB[1]], w=[B_qbf])
                                transposes(qbf, 4, 128, P, 0, B_qbf, qT[:, 0:4 * P].rearrange("p (g t) -> p g t", g=4), B_qT, evac="dve")
                                yield
                                proj(uTi, Wc, 512, 512, P, 1, B_uTi, B_W)
                                S.op("act", lambda e: e.activation(out=kvi[0:P, :], in_=psum[0:P, 1, 0:256], func=AF.Copy), r=[PB[1]], w=[B_kvf[i % 2]])
                                S.op("dve", lambda e: e.tensor_copy(kcb[0:P, :], psum[0:P, 1, 0:128]), r=[PB[1]], w=[B_kcb])
                                S.op("dve", lambda e: e.tensor_copy(qib[0:P, :], psum[0:P, 1, 256:512]), r=[PB[1]], w=[B_qib])
                                S.dma("sp", O["ck_" + g][l, b, t0:t0 + P, :], kvi[0:P, 0:128], r=[B_kvf[i % 2]], w=[Buf()])
                                S.dma("sp", O["cv_" + g][l, b, t0:t0 + P, :], kvi[0:P, 128:256], r=[B_kvf[i % 2]], w=[Buf()])
                                yield
                                proj(uTi, Wc, 1024, 68, P, 1, B_uTi, B_W)
                                S.op("act", lambda e: e.activation(out=kii[0:P, :], in_=psum[0:P, 1, 0:64], func=AF.Copy), r=[PB[1]], w=[B_kif[i % 2]])
                                S.op("dve", lambda e: e.tensor_copy(kib[0:P, :], psum[0:P, 1, 0:64]), r=[PB[1]], w=[B_kib])
                                S.op("dve", lambda e: e.tensor_scalar_mul(wi[0:P, :], psum[0:P, 1, 64:68], 1.0 / 16), r=[PB[1]], w=[B_wi])
                                S.dma("sp", O["cki_" + g][l, b, t0:t0 + P, :], kii[0:P, :], r=[B_kif[i % 2]], w=[Buf()])
                                yield
                                proj(uTi, Wc, 1092, 512, P, 1, B_uTi, B_W)
                                silu_gate(gate, P, 1, t1, B_t1, t2, B_t2, B_gate)
                                add_keys_c(j_new, P, kcb[0:P, :], B_kcb, kvi[0:P, 128:256], B_kvf[i % 2], kib[0:P, :], B_kib)
                                yield
                                transposes(None, 4, 64, P, 0, B_qib, qiT[:, :, 0:P], B_qiT, evac="dve",
                                           src_blocks=[qib[0:P, ih * 64:(ih + 1) * 64] for ih in range(4)])
                                nch = (nvis + 511) // 512
                                for cch in range(nch):
                                    c0 = cch * 512
                                    cn = min(512, nvis - c0)
                                    for ih in range(4):
                                        bank = (cch * 4 + ih) % 2
                                        RLb, B_RLb = RL[(cch * 4 + ih) % 2], B_RL[(cch * 4 + ih) % 2]
                                        S.op("pe", lambda e, ih=ih, bank=bank, c0=c0, cn=cn: e.matmul(psum[0:P, bank, 0:cn], lhsT=qiT[:, ih, 0:P],
                                                                                                      rhs=KiT[:, c0:c0 + cn], start=True, stop=True),
                                             r=[B_qiT] + B_KiTj[c0 // 128:(c0 + cn + 127) // 128], w=[PB[bank]])
                                        S.op("act", lambda e, bank=bank, cn=cn, RLb=RLb: e.activation(out=RLb[0:P, 0:cn], in_=psum[0:P, bank, 0:cn], func=AF.Relu),
                                             r=[PB[bank]], w=[B_RLb])
                                        S.op("dve", lambda e, ih=ih, c0=c0, cn=cn, RLb=RLb: e.scalar_tensor_tensor(
                                            out=SC[0:P, c0:c0 + cn], in0=RLb[0:P, 0:cn], scalar=wi[0:P, ih:ih + 1],
                                            in1=(negeps[0:P, c0:c0 + cn] if ih == 0 else SC[0:P, c0:c0 + cn]), op0=ALU.mult, op1=ALU.add),
                                            r=[B_RLb, B_wi, B_cf] + ([B_SC] if ih else []), w=[B_SC])
                                    yield
                                if P == 128:
                                    S.op("pool", lambda e: e.memset(SC[0:64, nvis - 64:nvis], NEG), w=[B_SC])
                                nvalid_max = nvis
                                if nvalid_max > TOPK and _ACT_TOPK and P == 128 and i % 2 == 1:
                                    bsv, B_bsv = bs2[i % 2], B_bs2[i % 2]
                                    S.op("dve", lambda e: e.memset(bsv[0:P, :], 0.0), w=[B_bsv])
                                    delta = 8.0
                                    for t_ in range(41):
                                        S.op("act", lambda e: e.activation(out=msk[0:P, 0:nvis], in_=SC[0:P, 0:nvis], func=AF.Sign,
                                                                           bias=bsv[0:P, 0:1], accum_out=bsv[0:P, 1:2]),
                                             r=[B_SC, B_bsv], w=[B_bsv, B_msk], c=0.25 + nvis * 0.85e-3)
                                        S.op("act", lambda e: e.activation(out=bsv[0:P, 2:3], in_=bsv[0:P, 1:2], func=AF.Sign,
                                                                           bias=-(2.0 * TOPK - nvis - 0.5)), r=[B_bsv], w=[B_bsv], c=0.25)
                                        S.op("act", lambda e, delta=delta: e.activation(out=bsv[0:P, 0:1], in_=bsv[0:P, 2:3], func=AF.Identity,
                                                                                        scale=-delta, bias=bsv[0:P, 0:1]), r=[B_bsv], w=[B_bsv], c=0.25)
                                        delta *= 0.5
                                        yield
                                    S.op("dve", lambda e, delta=delta: e.tensor_scalar(m8[0:P, 7:8], bsv[0:P, 0:1], -1.0, -2.0 * delta, op0=ALU.mult, op1=ALU.add),
                                         r=[B_bsv], w=[B_m8])
                                    S.op("dve", lambda e: e.tensor_scalar(msk[0:P, 0:nvis], SC[0:P, 0:nvis], m8[0:P, 7:8], None, op0=ALU.is_ge),
                                         r=[B_SC, B_m8], w=[B_msk])
                                elif nvalid_max > TOPK:
                                    S.op("pool", lambda e: e.tensor_copy(WK[0:P, 0:nvis], SC[0:P, 0:nvis]), r=[B_SC], w=[B_WK], c=0.3 + nvis * 2.0e-3)
                                    for r_ in range(TOPK // 8):
                                        S.op("dve", lambda e: e.max(out=m8[0:P, :], in_=WK[0:P, 0:nvis]), r=[B_WK], w=[B_m8], c=0.15 + nvis * 1.05e-3)
                                        if r_ < TOPK // 8 - 1:
                                            S.op("dve", lambda e: e.match_replace(out=WK[0:P, 0:nvis], in_to_replace=m8[0:P, :], in_values=WK[0:P, 0:nvis],
                                                                                  imm_value=NEGR), r=[B_m8, B_WK], w=[B_WK], c=0.2 + nvis * 1.05e-3)
                                        yield
                                    S.op("dve", lambda e: e.tensor_scalar(msk[0:P, 0:nvis], SC[0:P, 0:nvis], m8[0:P, 7:8], None, op0=ALU.is_ge),
                                         r=[B_SC, B_m8], w=[B_msk])
                                else:
                                    S.op("dve", lambda e: e.tensor_scalar(msk[0:P, 0:nvis], SC[0:P, 0:nvis], -1.0e29, None, op0=ALU.is_ge),
                                         r=[B_SC], w=[B_msk])
                                yield

                            def back_c(i):
                                t0, j_new, uTi, B_uTi, kvi, kii, hti, B_hti, qT, B_qT, gate, B_gate, MT, B_MT = tile_ctx_c(i)
                                nk = j_new + 1
                                msk, B_msk = msk2[i % 2], B_msk2[i % 2]
                                for j0 in range(0, nk, 8):
                                    jn = min(8, nk - j0)
                                    for jj in range(jn):
                                        j = j0 + jj
                                        kp = kp_of(j)
                                        S.op("pe", lambda e, j=j, jj=jj, kp=kp: e.transpose(psum_bf[0:kp, 2, jj * P:(jj + 1) * P],
                                                                                            msk[0:P, j * 128:j * 128 + kp], idb[0:P, 0:P]),
                                             r=[B_msk, B_cb], w=[PB[2]])
                                    S.op("act", lambda e, j0=j0, jn=jn: e.activation(out=MT[:, j0:j0 + jn, 0:P],
                                                                                     in_=psum_bf[:, 2, 0:jn * P].rearrange("p (j t) -> p j t", j=jn),
                                                                                     func=AF.Identity, scale=240000.0, bias=-240000.0),
                                         r=[PB[2]], w=[B_MT])
                                def dsa_qk(j):
                                    kp = kp_of(j)
                                    off = nk - 1 - j
                                    for n_ in range(2):
                                        bank = 2 + 2 * (j % 2) + n_
                                        S.op("pe", lambda e, n_=n_, bank=bank, j=j, kp=kp: e.matmul(
                                            psum[0:kp, bank, 0:4 * P], lhsT=KcT[n_ * 64:(n_ + 1) * 64, j * 128:j * 128 + kp],
                                            rhs=qT[n_ * 64:(n_ + 1) * 64, 0:4 * P], start=True, stop=False), r=[B_KcTj[j], B_qT], w=[PB[bank]])
                                        S.op("pe", lambda e, bank=bank, j=j, kp=kp: e.matmul(
                                            psum[0:kp, bank, 0:4 * P], lhsT=idb[0:kp, 0:kp],
                                            rhs=MT[0:kp, j, 0:P].unsqueeze(1).broadcast_to([kp, 4, P]), start=False, stop=(off > 1)),
                                            r=[B_MT, B_cb], w=[PB[bank]])
                                        if off <= 1:
                                            S.op("pe", lambda e, n_=n_, bank=bank, off=off, kp=kp: e.matmul(
                                                psum[0:kp, bank, 0:4 * P], lhsT=idb[0:kp, 0:kp],
                                                rhs=ebt[0:kp, n_ * 4:n_ * 4 + 4, off, 0:P], start=False, stop=True),
                                                r=[B_ebt, B_cb], w=[PB[bank]])

                                def dsa_exp(j):
                                    kp = kp_of(j)
                                    pTj, B_pTj = pT[j % 2], B_pT[j % 2]
                                    for n_ in range(2):
                                        bank = 2 + 2 * (j % 2) + n_
                                        S.op("act", lambda e, n_=n_, bank=bank, kp=kp, pTj=pTj: e.activation(
                                            out=pTj[0:kp, n_ * 4:n_ * 4 + 4, 0:P], in_=psum[0:kp, bank, 0:4 * P].rearrange("p (g t) -> p g t", g=4),
                                            func=AF.Exp, scale=0.125), r=[PB[bank]], w=[B_pTj])

                                def dsa_pv(j):
                                    kp = kp_of(j)
                                    pTj, B_pTj = pT[j % 2], B_pT[j % 2]
                                    for h in range(8):
                                        ab = 6 + h // 4
                                        S.op("pe", lambda e, h=h, ab=ab, j=j, kp=kp, pTj=pTj: e.matmul(
                                            psum[0:P, ab, (h % 4) * 65:(h % 4 + 1) * 65], lhsT=pTj[0:kp, h, 0:P], rhs=Vc[0:kp, j, h // 4, :],
                                            start=(j == 0 and h % 4 == 0), stop=(j == nk - 1), skip_group_check=True), r=[B_pTj, B_Vcj[j]], w=[PB[ab]])

                                dsa_qk(0)
                                for j in range(nk):
                                    if j + 1 < nk:
                                        dsa_qk(j + 1)
                                    dsa_exp(j)
                                    dsa_pv(j)
                                    yield
                                for half in range(2):
                                    ab = 6 + half
                                    acc3 = psum[0:P, ab, 0:260].rearrange("p (h d) -> p h d", h=4)
                                    S.op("dve", lambda e, half=half, acc3=acc3: e.reciprocal(rec[0:P, half * 4:half * 4 + 4].unsqueeze(2), acc3[:, :, 64:65]),
                                         r=[PB[ab]], w=[B_rec])
                                    S.op("dve", lambda e, half=half, acc3=acc3: e.tensor_tensor(
                                        yb[0:P, half * 256:(half + 1) * 256].rearrange("p (h d) -> p h d", h=4), acc3[:, :, 0:64],
                                        rec[0:P, half * 4:half * 4 + 4].unsqueeze(2).broadcast_to([P, 4, 64]), ALU.mult), r=[PB[ab], B_rec], w=[B_yb])
                                S.op("pool", lambda e: e.tensor_tensor(gated[0:P, :], yb[0:P, :], gate[0:P, :], ALU.mult), r=[B_yb, B_gate], w=[B_gated])
                                transposes(gated, 4, 128, P, 2, B_gated, gT[:, :, 0:P], B_gT, evac="act")
                                yield
                                for half in range(2):
                                    bank = 3 + half
                                    for k in range(4):
                                        S.op("pe", lambda e, k=k, half=half, bank=bank: e.matmul(
                                            psum[0:P, bank, :], lhsT=gT[:, k, 0:P], rhs=Wo[:, k, half * 512:(half + 1) * 512],
                                            start=(k == 0), stop=(k == 3)), r=[B_gT, B_Wo], w=[PB[bank]])
                                    S.op("dve", lambda e, half=half, bank=bank: e.tensor_tensor(
                                        hti[0:P, half * 512:(half + 1) * 512], hti[0:P, half * 512:(half + 1) * 512], psum[0:P, bank, :], ALU.add),
                                        r=[PB[bank], B_hti], w=[B_hti])
                                if not last:
                                    S.dma("sp", hb[t0:t0 + P, :], hti[0:P, :], r=[B_hti], w=[B_h[i]])
                                else:
                                    rms_scale(hti, P, junk, B_junk, stat, B_stat, B_hti)
                                    S.op("dve", lambda e: e.scalar_tensor_tensor(out=hti[0:P, :], in0=hti[0:P, :], scalar=stat[0:P, 2:3], in1=fin_g[0:P, :],
                                                                                 op0=ALU.mult, op1=ALU.mult), r=[B_hti, B_stat, B_fg], w=[B_hti])
                                    S.dma("sp", yout[t0:t0 + P, :], hti[0:P, :], r=[B_hti], w=[B_y])
                                yield

                            pipeline(front_c, back_c, NT)
                            if _USE_SCHED:
                                S.end()
                        S.barrier()
                        checkpoint()
        except _Stop:
            pass

        _DEAD["v"] = False
        S.barrier()
    return nc


_CACHE = {}


def _run(inputs, NP, T, NS, TS, PAST, ncores):
    key = (NP, T, NS, TS, PAST)
    if key not in _CACHE:
        _CACHE[key] = build(NP, T, NS, TS, PAST)
    nc = _CACHE[key]
    cfc, cbc = host_consts(max(T, PAST + 128))
    f = lambda a: np.ascontiguousarray(np.asarray(a, dtype=np.float32))
    in_maps = []
    for c in range(ncores):
        m = {}
        m["x_p"] = f(inputs["x_prompt"][c * NP:(c + 1) * NP])
        m["x_s"] = f(inputs["x_sample"][c * NS:(c + 1) * NS])
        m["st_wkv"] = f(inputs["state_a_wkv"][:, c * NS:(c + 1) * NS])
        m["st_shift"] = f(inputs["state_a_shift"][:, c * NS:(c + 1) * NS])
        m["cb_k"] = f(np.asarray(inputs["cache_b_k"])[:, c * NS:(c + 1) * NS].reshape(L, NS, PAST, 512))
        m["cb_v"] = f(np.asarray(inputs["cache_b_v"])[:, c * NS:(c + 1) * NS].reshape(L, NS, PAST, 512))
        m["cb_logf"] = f(inputs["cache_b_logf"][:, c * NS:(c + 1) * NS])
        m["cc_k"] = f(np.asarray(inputs["cache_c_k"])[:, c * NS:(c + 1) * NS].reshape(L, NS, PAST, 128))
        m["cc_v"] = f(np.asarray(inputs["cache_c_v"])[:, c * NS:(c + 1) * NS].reshape(L, NS, PAST, 128))
        m["cc_kidx"] = f(inputs["cache_c_kidx"][:, c * NS:(c + 1) * NS])
        for nm in ("norm_g", "w_in", "w_out", "a_mu", "a_w0", "a_a0", "a_k_k", "a_k_a", "a_lnx_w", "a_lnx_b", "a_w_b", "a_a_b",
                   "b_f_bias", "final_g"):
            m[nm] = f(inputs[nm])
        m["a_r_k"] = f(np.asarray(inputs["a_r_k"]).reshape(L, 512))
        m["t5_table"] = f(np.asarray(inputs["t5_table"]).reshape(256))
        m["cf"] = cfc
        m["cb"] = cbc
        in_maps.append(m)
    res = run_bass_kernel_spmd(nc, in_maps, core_ids=list(range(ncores)))
    R = res.results
    cat = lambda name, ax: np.concatenate([np.asarray(R[c][name]) for c in range(ncores)], axis=ax)
    outs = []
    for g, nb, tt in (("p", NP, T), ("s", NS, TS)):
        B = nb * ncores
        outs.append([
            cat("y_" + g, 0),
            cat("wkv_" + g, 1),
            cat("shift_" + g, 1),
            cat("bk_" + g, 1).reshape(L, B, tt, 8, 64),
            cat("bv_" + g, 1).reshape(L, B, tt, 8, 64),
            cat("blogf_" + g, 1),
            cat("ck_" + g, 1).reshape(L, B, tt, 2, 64),
            cat("cv_" + g, 1).reshape(L, B, tt, 2, 64),
            cat("cki_" + g, 1),
        ])
    p, s = outs
    return (p[0], s[0], p[1], p[2], p[3], p[4], p[5], p[6], p[7], p[8], s[1], s[2], s[3], s[4], s[5], s[6], s[7], s[8])


def kernel(**inputs):
    out = _run(inputs, 2, 2048, 1, 64, 1024, NCORES)
    return tuple(np.ascontiguousarray(o, dtype=np.float32) for o in out)
```

```python
import math
from contextlib import ExitStack

import numpy as np
import ml_dtypes

import concourse.bass as bass
import concourse.mybir as mybir
from concourse.bass_utils import run_bass_kernel_spmd

F32 = mybir.dt.float32
BF16 = mybir.dt.bfloat16
ALU = mybir.AluOpType
AF = mybir.ActivationFunctionType
AX = mybir.AxisListType

D = 1024
DIN = 5836
DMIX = 1536
L = 2
NCORES = 8
RMS_EPS = 1e-6
GN_EPS = 64e-5
NEG = -1.0e30
NEGR = -3.0e38
EXPM05 = math.exp(-0.5)


class Buf:
    __slots__ = ("w", "r", "excl", "grp")

    def __init__(self, excl=False):
        self.w = {}
        self.r = {}
        self.grp = set()
        self.excl = excl


class _Cap:
    def __init__(self):
        self.call = None

    def __getattr__(self, name):
        def f(*a, **k):
            self.call = (name, a, k)
            return self
        return f


class Sched:
    def __init__(self, nc, es):
        self.nc = nc
        self.eng = {"pe": nc.tensor, "act": nc.scalar, "dve": nc.vector, "pool": nc.gpsimd, "sp": nc.sync}
        self.sem = {}
        self.cnt = {}
        for e in ("pe", "act", "dve", "pool"):
            self.sem[e] = es.enter_context(nc.semaphore("s_" + e))
            self.cnt[e] = 0
        self.waited = {e: {} for e in self.eng}
        self.dq = {}
        self.dqi = {}
        for q, n in (("sp", 12), ("act", 8), ("pool", 6)):
            self.dq[q] = []
            self.dqi[q] = 0
            for i in range(n):
                k = ("d", q, i)
                self.sem[k] = es.enter_context(nc.semaphore("d_%s%d" % (q, i)))
                self.dq[q].append([k, 0])

    def _collect(self, e, r, w, is_dma=False, part=0):
        deps = {}

        def need(k, v):
            if k == e and e == "pe" and not is_dma:
                return
            if deps.get(k, 0) < v:
                deps[k] = v

        for b in r:
            for k, v in b.w.items():
                need(k, v)
            if b.excl:
                for k, v in b.r.items():
                    if k != e:
                        need(k, v)
        for b in w:
            for k, v in b.w.items():
                if part == 2 and k in b.grp:
                    continue
                need(k, v)
            for k, v in b.r.items():
                need(k, v)
        return deps

    def _wait(self, e, deps):
        wd = self.waited[e]
        eng = self.eng[e]
        for k, v in deps.items():
            if wd.get(k, 0) < v:
                eng.wait_ge(self.sem[k], v)
                wd[k] = v

    def begin(self):
        self.rec = []

    def end(self):
        ops, self.rec = self.rec, None
        if not ops:
            return
        for i in self._schedule(ops):
            o = ops[i]
            if o[0] == "op":
                name, a, k = o[2]
                self.op(o[1], lambda eng, name=name, a=a, k=k: getattr(eng, name)(*a, **k), o[3], o[4])
            else:
                self.dma(o[1], o[2], o[3], o[4], o[5], part=o[7])

    _DEF_COST = {"pe": 0.09, "act": 0.55, "dve": 0.55, "pool": 1.2}

    @staticmethod
    def _free(ap):
        try:
            n = 1
            for d in list(ap.shape)[1:]:
                n *= int(d)
            return n
        except Exception:
            return None

    def _estimate(self, e, call):
        name, a, k = call
        try:
            if e == "pe":
                mv = k.get("rhs") if name == "matmul" else (a[2] if len(a) > 2 else k.get("identity"))
                n = self._free(mv) or 128
                return 0.05 + max(64, n) * 0.62e-3
            out = k.get("out", a[0] if a else None)
            n = self._free(out)
            if n is None:
                return self._DEF_COST[e]
            if e == "act":
                return 0.2 + n * 0.85e-3
            if e == "dve":
                return 0.12 + n * 1.05e-3
            return 0.3 + n * 2.2e-3
        except Exception:
            return self._DEF_COST[e]

    def _schedule(self, ops):
        n = len(ops)
        last_w, readers = {}, {}
        preds = [None] * n
        for i, o in enumerate(ops):
            r, w = (o[3], o[4]) if o[0] == "op" else (o[4], o[5])
            ps = set()
            for b in r:
                k = id(b)
                if k in last_w:
                    ps.add(last_w[k])
                if b.excl:
                    ps |= readers.get(k, set())
            for b in w:
                k = id(b)
                if k in last_w:
                    ps.add(last_w[k])
                ps |= readers.get(k, set())
            for b in r:
                k = id(b)
                if b.excl:
                    last_w[k] = i
                    readers[k] = set()
                else:
                    readers.setdefault(k, set()).add(i)
            for b in w:
                k = id(b)
                last_w[k] = i
                readers[k] = set()
            ps.discard(i)
            preds[i] = ps
        succs = [[] for _ in range(n)]
        indeg = [0] * n
        for i in range(n):
            indeg[i] = len(preds[i])
            for p in preds[i]:
                succs[p].append(i)
        cost = [0.0] * n
        for i, o in enumerate(ops):
            if o[0] == "op":
                cost[i] = (o[5] if o[5] is not None else self._DEF_COST[o[1]]) + 0.15
            else:
                cost[i] = o[6] if o[6] is not None else 3.0
        bl = [0.0] * n
        for i in range(n - 1, -1, -1):
            m = 0.0
            for sx in succs[i]:
                if bl[sx] > m:
                    m = bl[sx]
            bl[i] = cost[i] + m
        avail = [0.0] * n
        done = [0.0] * n
        ready = {}
        for i in range(n):
            if indeg[i] == 0:
                ready.setdefault(ops[i][1], []).append(i)
        free = {}
        order = []
        while len(order) < n:
            best = None
            for st, lst in ready.items():
                if not lst:
                    continue
                f = free.get(st, 0.0)
                ci = min(lst, key=lambda i: (int(max(avail[i], f) * _BUCK), -bl[i], i))
                key = (int(max(avail[ci], f) * _BUCK), -bl[ci], ci)
                if best is None or key < best[0]:
                    best = (key, st, ci)
            st, ci = best[1], best[2]
            start = max(avail[ci], free.get(st, 0.0))
            ready[st].remove(ci)
            o = ops[ci]
            if o[0] == "op":
                c = o[5] if o[5] is not None else self._DEF_COST[st]
                free[st] = start + c
                done[ci] = start + c + 0.15
            else:
                free[st] = start + 0.06
                done[ci] = start + (o[6] if o[6] is not None else 3.0)
            order.append(ci)
            for sx in succs[ci]:
                if done[ci] > avail[sx]:
                    avail[sx] = done[ci]
                indeg[sx] -= 1
                if indeg[sx] == 0:
                    ready.setdefault(ops[sx][1], []).append(sx)
        return order

    def op(self, e, fn, r=(), w=(), c=None):
        if _DEAD["v"]:
            return
        if getattr(self, "rec", None) is not None:
            cap = _Cap()
            fn(cap)
            if c is None:
                c = self._estimate(e, cap.call)
            self.rec.append(("op", e, cap.call, tuple(r), tuple(w), c))
            return
        self._wait(e, self._collect(e, r, w))
        ins = fn(self.eng[e])
        self.cnt[e] += 1
        c = self.cnt[e]
        ins.then_inc(self.sem[e], 1)
        for b in r:
            b.r[e] = c
        for b in w:
            b.w = {e: c}
            b.grp = set()
            b.r = {}

    def dma(self, q, out, in_, r=(), w=(), c=None, part=0):
        if _DEAD["v"]:
            return
        if getattr(self, "rec", None) is not None:
            self.rec.append(("dma", q, out, in_, tuple(r), tuple(w), c, part))
            return
        deps = self._collect(q, r, w, is_dma=True, part=part)
        slot = self.dq[q][self.dqi[q] % len(self.dq[q])]
        self.dqi[q] += 1
        k = slot[0]
        if slot[1] > 0 and deps.get(k, 0) < 16 * slot[1]:
            deps[k] = 16 * slot[1]
        self._wait(q, deps)
        ins = self.eng[q].dma_start(out=out, in_=in_)
        slot[1] += 1
        v = 16 * slot[1]
        ins.then_inc(self.sem[k], 16)
        for b in r:
            b.r[k] = v
        for b in w:
            if part == 2:
                b.w[k] = v
                b.grp.add(k)
            else:
                b.w = {k: v}
                b.grp = {k} if part == 1 else set()
            b.r = {}

    def all_deps(self):
        deps = {e: c for e, c in self.cnt.items() if c > 0}
        for q in self.dq:
            for k, n in self.dq[q]:
                if n > 0:
                    deps[k] = 16 * n
        return deps

    def barrier(self):
        if _DEAD["v"]:
            return
        assert getattr(self, "rec", None) is None
        deps = self.all_deps()
        for e in ("pe", "act", "dve", "pool", "sp"):
            self._wait(e, dict(deps))


def _t5_bucket_np(rel):
    nb = 16
    me = 8
    base = np.where(rel > 0, nb, 0)
    n = np.abs(rel)
    nf = np.maximum(n, 1).astype(np.float32)
    large = me + (np.log(nf / np.float32(me)) / np.float32(math.log(128 / me)) * np.float32(nb - me)).astype(np.int32)
    large = np.minimum(large, nb - 1)
    return base + np.where(n < me, n, large)


T5B = [b for b in range(32) if b != 15]
CF_ID, CF_TRI, CF_ONE, CF_E0, CF_N = 0, 128, 256, 384, 512
CB_ID, CB_SU, CB_IU, CB_NSU, CB_NSL, CB_T5, CB_N = 0, 128, 256, 384, 512, 640, 640 + 31 * 256


def host_consts(nkmax):
    p = np.arange(128)[:, None]
    c = np.arange(128)[None, :]
    cf = np.zeros((128, CF_N + nkmax), np.float32)
    cf[:, CF_ID:CF_ID + 128] = (p == c)
    cf[:, CF_TRI:CF_TRI + 128] = (p <= c)
    cf[:, CF_ONE:CF_ONE + 128] = 1.0
    cf[:, CF_E0:CF_E0 + 128] = (p == 0) * np.ones((1, 128))
    cf[:, CF_N:] = -(2.0 ** -34) * (np.arange(nkmax)[None, :] + 1.0)
    cb = np.zeros((128, CB_N), np.float32)
    cb[:, CB_ID:CB_ID + 128] = (p == c)
    cb[:, CB_SU:CB_SU + 128] = (p < c)
    cb[:, CB_IU:CB_IU + 128] = (p <= c)
    cb[:, CB_NSU:CB_NSU + 128] = -1.0 * (p < c)
    cb[:, CB_NSL:CB_NSL + 128] = -1.0 * (p > c)
    for off, base in ((0, 0), (1, -128)):
        bk = _t5_bucket_np((p - c + base).astype(np.int32))
        for bi, b in enumerate(T5B):
            cb[:, CB_T5 + bi * 256 + off * 128: CB_T5 + bi * 256 + off * 128 + 128] = (bk == b)
    return cf, cb.astype(ml_dtypes.bfloat16)


class _Stop(Exception):
    pass


import os as _os
_STOP = int(_os.environ.get("K_STOP", "999"))
_SUB = int(_os.environ.get("K_SUB", "0"))
_USE_SCHED = _os.environ.get("K_SCHED", "1") == "1"
_ACT_TOPK = _os.environ.get("K_ACT_TOPK", "1") == "1"
_BUCK = float(_os.environ.get("K_BUCK", "10.0"))
_ACT_TILES = set(int(x) for x in _os.environ.get("K_ACT_TILES", "2,4,6,8,10,12,14").split(",") if x)


_DEAD = {"v": False}


def sub(n):
    if n == _SUB:
        _DEAD["v"] = True


def build(NP, T, NS, TS, PAST):
    nc = bass.Bass("TRN2", target_bir_lowering=False)
    NKMAX = max(T, PAST + 128)

    def din(name, shape, dt=F32):
        return nc.dram_tensor(name, list(shape), dt, kind="ExternalInput").ap()

    def dout(name, shape, dt=F32):
        return nc.dram_tensor(name, list(shape), dt, kind="ExternalOutput").ap()

    I = {}
    I["x_p"] = din("x_p", [NP, T, D])
    I["x_s"] = din("x_s", [NS, TS, D])
    I["st_wkv"] = din("st_wkv", [L, NS, 8, 64, 64])
    I["st_shift"] = din("st_shift", [L, NS, 1664])
    I["cb_k"] = din("cb_k", [L, NS, PAST, 512])
    I["cb_v"] = din("cb_v", [L, NS, PAST, 512])
    I["cb_logf"] = din("cb_logf", [L, NS, PAST, 8])
    I["cc_k"] = din("cc_k", [L, NS, PAST, 128])
    I["cc_v"] = din("cc_v", [L, NS, PAST, 128])
    I["cc_kidx"] = din("cc_kidx", [L, NS, PAST, 64])
    I["norm_g"] = din("norm_g", [L, D])
    I["w_in"] = din("w_in", [L, D, DIN])
    I["w_out"] = din("w_out", [L, DMIX, D])
    I["a_mu"] = din("a_mu", [L, 1664])
    for nm in ("a_w0", "a_a0", "a_k_k", "a_k_a", "a_r_k", "a_lnx_w", "a_lnx_b"):
        I[nm] = din(nm, [L, 512])
    I["a_w_b"] = din("a_w_b", [L, 64, 512])
    I["a_a_b"] = din("a_a_b", [L, 64, 512])
    I["b_f_bias"] = din("b_f_bias", [L, 8])
    I["t5_table"] = din("t5_table", [256])
    I["final_g"] = din("final_g", [D])
    I["cf"] = din("cf", [128, CF_N + NKMAX])
    I["cb"] = din("cb", [128, CB_N], BF16)

    O = {}
    for g, nb, tt in (("p", NP, T), ("s", NS, TS)):
        O["y_" + g] = dout("y_" + g, [nb, tt, D])
        O["wkv_" + g] = dout("wkv_" + g, [L, nb, 8, 64, 64])
        O["shift_" + g] = dout("shift_" + g, [L, nb, 1664])
        O["bk_" + g] = dout("bk_" + g, [L, nb, tt, 512])
        O["bv_" + g] = dout("bv_" + g, [L, nb, tt, 512])
        O["blogf_" + g] = dout("blogf_" + g, [L, nb, tt, 8])
        O["ck_" + g] = dout("ck_" + g, [L, nb, tt, 128])
        O["cv_" + g] = dout("cv_" + g, [L, nb, tt, 128])
        O["cki_" + g] = dout("cki_" + g, [L, nb, tt, 64])

    seqs = [("p", b, T, 0) for b in range(NP)] + [("s", b, TS, PAST) for b in range(NS)]
    hbuf = {}
    utd = {}
    for (g, b, tq, past) in seqs:
        hbuf[(g, b)] = nc.dram_tensor("hb_%s%d" % (g, b), [tq, D], F32, kind="Internal").ap()
        utd[(g, b)] = nc.dram_tensor("ut_%s%d" % (g, b), [128, 8 * tq], BF16, kind="Internal").ap()

    es = ExitStack()
    with es:
        S = Sched(nc, es)
        uid = {"n": 0}

        def uname(name):
            uid["n"] += 1
            return "%s_u%d" % (name, uid["n"])

        st = lambda name, shape, dt=F32: es.enter_context(nc.sbuf_tensor(uname(name), list(shape), dt))

        cf = st("cf", [128, CF_N + NKMAX])
        cb = st("cb", [128, CB_N - 31 * 256], BF16)
        ebt = st("ebt", [128, 8, 2, 128], BF16)
        B_cf, B_cb, B_ebt = Buf(), Buf(), Buf()
        psum = es.enter_context(nc.psum_tensor("psum", [128, 8, 512], F32))
        psum_bf = psum[:].bitcast(BF16)
        PB = [Buf(excl=True) for _ in range(8)]
        S.dma("sp", cf[:], I["cf"][:, :], w=[B_cf])
        S.dma("sp", cb[:], I["cb"][:, 0:CB_T5], w=[B_cb])
        idf = cf[:, CF_ID:CF_ID + 128]
        tri = cf[:, CF_TRI:CF_TRI + 128]
        onesf = cf[:, CF_ONE:CF_ONE + 128]
        e0row = cf[:, CF_E0:CF_E0 + 128]
        negeps = cf[:, CF_N:]
        idb = cb[:, CB_ID:CB_ID + 128]
        m_su = cb[:, CB_SU:CB_SU + 128]
        m_iu = cb[:, CB_IU:CB_IU + 128]
        m_nsu = cb[:, CB_NSU:CB_NSU + 128]
        m_nsl = cb[:, CB_NSL:CB_NSL + 128]

        def bc_h(ap2, nh, kp, P):
            return ap2[0:kp, 0:P].unsqueeze(1).broadcast_to([kp, nh, P])

        with nc.sbuf_tensor("t5m_sb", [128, 31 * 256], BF16) as t5m, \
                nc.sbuf_tensor("t5tb_sb", [128, 32, 8], F32) as t5tb, \
                nc.sbuf_tensor("t5d_sb", [128, 32, 8], F32) as t5d, \
                nc.sbuf_tensor("t5s_sb", [128, 8, 256], F32) as t5s:
            B_m, B_tb, B_d = Buf(), Buf(), Buf()
            B_s = [Buf() for _ in range(8)]
            S.dma("sp", t5m[:], I["cb"][:, CB_T5:CB_N], w=[B_m])
            S.dma("sp", t5tb[:].rearrange("p a b -> p (a b)"), I["t5_table"].partition_broadcast(128), w=[B_tb])
            S.op("dve", lambda e: e.tensor_tensor(t5d[:], t5tb[:], t5tb[:, 15:16, :].broadcast_to([128, 32, 8]), ALU.subtract),
                 r=[B_tb], w=[B_d])
            for hq in range(8):
                en = "dve"
                S.op(en, lambda e, hq=hq: e.memset(t5s[:, hq, :], 0.0), w=[B_s[hq]])
                for bi, b in enumerate(T5B):
                    S.op(en, lambda e, hq=hq, bi=bi, b=b: e.scalar_tensor_tensor(
                        out=t5s[:, hq, :], in0=t5m[:, bi * 256:(bi + 1) * 256], scalar=t5d[:, b, hq:hq + 1],
                        in1=t5s[:, hq, :], op0=ALU.mult, op1=ALU.add), r=[B_m, B_d, B_s[hq]], w=[B_s[hq]])
                S.op("act", lambda e, hq=hq: e.mul(ebt[:, hq, :, :], t5s[:, hq, :].rearrange("p (o t) -> p o t", o=2), 8.0),
                     r=[B_s[hq]], w=[B_ebt])
            S.barrier()

        rot = {"i": 0}

        def ps1(banks):
            b = banks[rot["i"] % len(banks)]
            rot["i"] += 1
            return b

        def proj(uT, W, c0, n, P, bank, B_u, B_W):
            for k in range(8):
                S.op("pe", lambda e, k=k: e.matmul(psum[0:P, bank, 0:n], lhsT=uT[:, k, 0:P], rhs=W[:, k, c0:c0 + n],
                                                   start=(k == 0), stop=(k == 7)), r=[B_u, B_W], w=[PB[bank]], c=0.05 + n * 0.6e-3)

        def transposes(src, nblk, blkw, P, bank, B_src, dst, B_dst, evac="act", src_blocks=None):
            for k in range(nblk):
                in_ap = src_blocks[k] if src_blocks is not None else src[0:P, k * blkw:(k + 1) * blkw]
                S.op("pe", lambda e, k=k, in_ap=in_ap: e.transpose(psum_bf[0:blkw, bank, k * P:(k + 1) * P], in_ap, idb[0:P, 0:P]),
                     r=[B_src, B_cb], w=[PB[bank]])
            src_ps = psum_bf[0:blkw, bank, 0:nblk * P].rearrange("p (k t) -> p k t", k=nblk)
            if evac == "act":
                S.op("act", lambda e: e.activation(out=dst, in_=src_ps, func=AF.Copy), r=[PB[bank]], w=[B_dst])
            else:
                S.op("dve", lambda e: e.tensor_copy(dst, src_ps), r=[PB[bank]], w=[B_dst])

        def load_w(W, l, c0, n, B_W, dst0=0):
            src = I["w_in"][l, :, c0:c0 + n].rearrange("(k p) n -> p k n", p=128)
            for k0 in range(0, 8, 2):
                S.dma("pool", W[:, k0:k0 + 2, dst0:dst0 + n], src[:, k0:k0 + 2, :], w=[B_W], part=(1 if k0 == 0 else 2))

        def load_wo(Wo, l, r0, B_Wo):
            src = I["w_out"][l, r0:r0 + 512, :].rearrange("(k p) n -> p k n", p=128)
            S.dma("pool", Wo[:, :, :], src, w=[B_Wo])

        def out_proj_rmw(gated, P, Wo, B_g, B_Wo, hsrc, hdst, B_hs, B_hd, ht, B_ht, gT, B_gT, banks, final=None):
            tb = banks[0]
            transposes(gated, 4, 128, P, tb, B_g, gT[:, :, 0:P], B_gT, evac="act")
            S.dma("sp", ht[0:P, :], hsrc, r=[B_hs], w=[B_ht])
            for half in range(2):
                bank = banks[1 + half]
                for k in range(4):
                    S.op("pe", lambda e, k=k, half=half, bank=bank: e.matmul(
                        psum[0:P, bank, :], lhsT=gT[:, k, 0:P], rhs=Wo[:, k, half * 512:(half + 1) * 512],
                        start=(k == 0), stop=(k == 3)), r=[B_gT, B_Wo], w=[PB[bank]])
                S.op("dve", lambda e, half=half, bank=bank: e.tensor_tensor(
                    ht[0:P, half * 512:(half + 1) * 512], ht[0:P, half * 512:(half + 1) * 512], psum[0:P, bank, :], ALU.add),
                    r=[PB[bank], B_ht], w=[B_ht])
            if final is None:
                S.dma("sp", hdst, ht[0:P, :], r=[B_ht], w=[B_hd])
            else:
                fin_g, B_fg, ydst, B_y, junk, B_junk, stat, B_stat = final
                rms_scale(ht, P, junk, B_junk, stat, B_stat, B_ht)
                S.op("dve", lambda e: e.scalar_tensor_tensor(out=ht[0:P, :], in0=ht[0:P, :], scalar=stat[0:P, 2:3], in1=fin_g[0:P, :],
                                                             op0=ALU.mult, op1=ALU.mult), r=[B_ht, B_stat, B_fg], w=[B_ht])
                S.dma("sp", ydst, ht[0:P, :], r=[B_ht], w=[B_y])

        def rms_scale(ht, P, junk, B_junk, stat, B_stat, B_ht):
            S.op("act", lambda e: e.activation(out=junk[0:P, :], in_=ht[0:P, :], func=AF.Square, accum_out=stat[0:P, 0:1]),
                 r=[B_ht], w=[B_junk, B_stat])
            S.op("act", lambda e: e.activation(out=stat[0:P, 1:2], in_=stat[0:P, 0:1], func=AF.Ln, scale=1.0 / D, bias=RMS_EPS),
                 r=[B_stat], w=[B_stat])
            S.op("act", lambda e: e.activation(out=stat[0:P, 2:3], in_=stat[0:P, 1:2], func=AF.Exp, scale=-0.5),
                 r=[B_stat], w=[B_stat])

        def affine_tanh(dst, src_ap, B_src, B_dst, mul, add, scale=0.5):
            S.op("act", lambda e: e.activation(out=dst, in_=src_ap, func=AF.Tanh, scale=scale), r=[B_src], w=[B_dst])
            S.op("dve", lambda e: e.tensor_scalar(dst, dst, mul, add, op0=ALU.mult, op1=ALU.add), r=[B_dst], w=[B_dst])

        def silu_gate(gate, P, bank, tmp, B_tmp, tmp2, B_tmp2, B_gate):
            affine_tanh(tmp[0:P, 0:512], psum[0:P, bank, :], PB[bank], B_tmp, 0.5, 0.5)
            S.op("dve", lambda e: e.tensor_tensor(gate[0:P, :], tmp[0:P, 0:512], psum[0:P, bank, :], ALU.mult),
                 r=[B_tmp, PB[bank]], w=[B_gate])

        def pipeline(front, back, ntiles):
            for _ in front(0):
                pass
            for i in range(ntiles):
                gens = [back(i)]
                if i + 1 < ntiles:
                    gens.append(front(i + 1))
                while gens:
                    for gq in list(gens):
                        try:
                            next(gq)
                        except StopIteration:
                            gens.remove(gq)

        stage = {"n": 0}

        def checkpoint():
            stage["n"] += 1
            if stage["n"] >= _STOP:
                _DEAD["v"] = True

        try:
            for (g, b, Tq, past) in (seqs if _STOP > 0 else []):
                P = min(128, Tq)
                NT = Tq // P
                NPT = past // 128
                NKT = NPT + NT
                NK = NKT * 128
                TOPK = min(256, (past + Tq) // 4)
                xin = (I["x_p"] if g == "p" else I["x_s"])[b]
                yout = O["y_" + g][b]
                hb = hbuf[(g, b)]
                ut = utd[(g, b)]
                B_h = [Buf() for _ in range(NT)]
                B_ut = [Buf() for _ in range(NT)]
                B_y = Buf()
                kp_of = lambda j: 128 if j < NPT else P

                for l in range(L):
                    last = (l == L - 1)
                    S.barrier()
                    with ExitStack() as pa:
                        if _USE_SCHED:
                            S.begin()
                        sa = lambda name, shape, dt=F32: pa.enter_context(nc.sbuf_tensor(uname(name), list(shape), dt))
                        Wa = sa("Wa", [128, 8, 2176], BF16)
                        Wo = sa("Woa", [128, 4, 1024], BF16)
                        PRM = sa("PRM", [128, 5248])
                        WAB = sa("WAB", [128, 512], BF16)
                        gb = sa("gb", [128, D])
                        Sst = sa("Sst", [128, 4, 64])
                        Sbf = sa("Sbf", [128, 2, 4, 64], BF16)
                        zz = [sa("z0", [128, 1664]), sa("z1", [128, 1664])]
                        prev = sa("prev", [128, 1664])
                        ht = [sa("ht0", [128, D]), sa("ht1", [128, D])]
                        junk = sa("junk", [128, D], BF16)
                        stat = sa("stat", [128, 4])
                        ubf = sa("ubf", [128, D], BF16)
                        uT = [sa("uT0", [128, 8, 128], BF16), sa("uT1", [128, 8, 128], BF16)]
                        gate2 = [sa("gateA0", [128, 512], BF16), sa("gateA1", [128, 512], BF16)]
                        fb = sa("fbk", [128, 512])
                        bon2 = [sa("bon0", [128, 512]), sa("bon1", [128, 512])]
                        hX = {q: sa("hX%d" % q, [128, 512], BF16) for q in (1, 2, 4)}
                        tX = {q: sa("tX%d" % q, [128, 4, 128], BF16) for q in (0, 3)}
                        QX = [sa("QX0", [128, 8, 128], BF16), sa("QX1", [128, 8, 128], BF16)]
                        LX = [sa("LX%d" % q, [128, 8, 128], BF16) for q in range(3)]
                        f_ = [sa("f%d" % i, [128, 512]) for i in range(8)]
                        h_ = [sa("h%d" % i, [128, 512], BF16) for i in range(7)]
                        s8 = [sa("s8_%d" % i, [128, 8]) for i in range(6)]
                        tT = [sa("tT%d" % i, [128, 4, 128], BF16) for i in range(4)]
                        tw = sa("tw", [128, 128], BF16)
                        twT = sa("twT", [128, 128], BF16)
                        QQ = [[sa("Q%d%d" % (i, j), [128, 8, 128], BF16) for j in range(2)] for i in range(2)]
                        LL = [sa("LL%d" % i, [128, 8, 128], BF16) for i in range(3)]
                        XX = [sa("X%d" % i, [128, 8, 64], BF16) for i in range(2)]
                        gC = sa("gC", [128, 4])
                        gT = sa("gTa", [128, 4, 128], BF16)
                        wkvst = sa("wkvst", [64, 8, 64])
                        wkvo = sa("wkvo", [64, 4, 128])
                        B_Wa, B_Wo, B_PRM, B_WAB, B_gb, B_S, B_Sb = Buf(), Buf(), Buf(), Buf(), Buf(), Buf(), Buf()
                        B_z = [Buf(), Buf()]
                        B_prev, B_junk, B_stat, B_ubf = Buf(), Buf(), Buf(), Buf()
                        B_gate2, B_bon2, B_fb = [Buf(), Buf()], [Buf(), Buf()], Buf()
                        B_hX = {q: Buf() for q in (1, 2, 4)}
                        B_tX = {q: Buf() for q in (0, 3)}
                        B_QX = [Buf(), Buf()]
                        B_LX = [Buf(), Buf(), Buf()]
                        B_ht = [Buf(), Buf()]
                        B_uT = [Buf(), Buf()]
                        B_f = [Buf() for _ in range(8)]
                        B_hh = [Buf() for _ in range(7)]
                        B_s8 = [Buf() for _ in range(6)]
                        B_tT = [Buf() for _ in range(4)]
                        B_tw, B_twT, B_gC, B_gT, B_wst, B_wo2 = Buf(), Buf(), Buf(), Buf(), Buf(), Buf()
                        B_Q = [[Buf(), Buf()], [Buf(), Buf()]]
                        B_LL = [Buf(), Buf(), Buf()]
                        B_X = [Buf(), Buf()]

                        load_w(Wa, l, 0, 2176, B_Wa)
                        load_wo(Wo, l, 0, B_Wo)
                        S.dma("sp", gb[:], I["norm_g"][l, :].partition_broadcast(128), w=[B_gb])
                        S.dma("sp", PRM[:, 0:1664], I["a_mu"][l, :].partition_broadcast(128), w=[B_PRM], part=1)
                        for i, nm in enumerate(("a_w0", "a_a0", "a_k_k", "a_k_a", "a_r_k", "a_lnx_w", "a_lnx_b")):
                            S.dma("sp", PRM[:, 1664 + 512 * i:1664 + 512 * (i + 1)], I[nm][l, :].partition_broadcast(128), w=[B_PRM], part=2)
                        S.dma("pool", WAB[0:64, :], I["a_w_b"][l], w=[B_WAB], part=1)
                        S.dma("pool", WAB[64:128, :], I["a_a_b"][l], w=[B_WAB], part=2)
                        mu = PRM[:, 0:1664]
                        pw0, pa0, pkk, pka, prk, plw, plb = [PRM[:, 1664 + 512 * i:1664 + 512 * (i + 1)] for i in range(7)]
                        if past > 0:
                            S.dma("sp", wkvst[:], I["st_wkv"][l, b].rearrange("h i j -> i h j"), w=[B_wst])
                            for hp in range(4):
                                S.op("pe", lambda e, hp=hp: e.transpose(psum[:, 0, hp * 64:(hp + 1) * 64],
                                                                        wkvst[:, 2 * hp:2 * hp + 2, :].rearrange("i e j -> i (e j)"),
                                                                        idf[0:64, 0:64]), r=[B_wst, B_cf], w=[PB[0]])
                            S.op("dve", lambda e: e.tensor_copy(Sst[:].rearrange("p a i -> p (a i)"), psum[:, 0, 0:256]), r=[PB[0]], w=[B_S])
                            S.dma("sp", zz[1][127:128, :], I["st_shift"][l, b:b + 1, :], w=[B_z[1]])
                        else:
                            S.op("dve", lambda e: e.memset(Sst[:], 0.0), w=[B_S])
                            S.op("dve", lambda e: e.memset(zz[1][:], 0.0), w=[B_z[1]])
                        S.op("pool", lambda e: e.memset(Sbf[:].rearrange("p a b c -> p (a b c)"), 0.0), w=[B_Sb])
                        for eo in range(2):
                            S.op("dve", lambda e, eo=eo: e.tensor_copy(Sbf[eo * 64:eo * 64 + 64, eo, :, :], Sst[eo * 64:eo * 64 + 64, :, :]), r=[B_S], w=[B_Sb])

                        h_base, B_hh_base, tT_base, B_tT_base = h_, B_hh, tT, B_tT
                        QQ_base, B_Q_base, LL_base, B_LL_base = QQ, B_Q, LL, B_LL
                        for i in range(NT):
                            t0 = i * P
                            z = zz[i % 2]
                            zp = zz[(i + 1) % 2]
                            B_zc, B_zp = B_z[i % 2], B_z[(i + 1) % 2]
                            hti, B_hti = ht[i % 2], B_ht[i % 2]
                            uTi, B_uTi = uT[i % 2], B_uT[i % 2]
                            par = i % 2
                            gate, B_gate = gate2[par], B_gate2[par]
                            bon, B_bon = bon2[par], B_bon2[par]
                            h_, B_hh, tT, B_tT = list(h_base), list(B_hh_base), list(tT_base), list(B_tT_base)
                            QQ, B_Q, LL, B_LL = [list(x) for x in QQ_base], [list(x) for x in B_Q_base], list(LL_base), list(B_LL_base)
                            if par == 1:
                                for q in (1, 2, 4):
                                    h_[q], B_hh[q] = hX[q], B_hX[q]
                                for q in (0, 3):
                                    tT[q], B_tT[q] = tX[q], B_tX[q]
                                QQ[0], B_Q[0] = QX, B_QX
                                LL, B_LL = LX, B_LX
                            sub(1)
                            def load_h(ii):
                                tt0 = ii * P
                                hsrc = xin[tt0:tt0 + P, :] if l == 0 else hb[tt0:tt0 + P, :]
                                S.dma("act", ht[ii % 2][0:P, :], hsrc, r=([] if l == 0 else [B_h[ii]]), w=[B_ht[ii % 2]])

                            if i == 0:
                                load_h(0)
                            if i + 1 < NT:
                                load_h(i + 1)
                            rms_scale(hti, P, junk, B_junk, stat, B_stat, B_hti)
                            S.op("dve", lambda e: e.scalar_tensor_tensor(out=ubf[0:P, :], in0=hti[0:P, :], scalar=stat[0:P, 2:3], in1=gb[0:P, :],
                                                                         op0=ALU.mult, op1=ALU.mult), r=[B_hti, B_stat, B_gb], w=[B_ubf])
                            transposes(ubf, 8, 128, P, 0, B_ubf, uTi[:, :, 0:P], B_uTi, evac="act")
                            S.dma("sp", ut[:, 8 * t0:8 * t0 + 8 * P].rearrange("p (k t) -> p k t", k=8), uTi[:, :, 0:P], r=[B_uTi], w=[B_ut[i]])
                            sub(2)
                            for cblk in range(3):
                                bank = 1 + cblk % 2
                                proj(uTi, Wa, cblk * 512, 512, P, bank, B_uTi, B_Wa)
                                S.op("act", lambda e, cblk=cblk, bank=bank: e.activation(out=z[0:P, cblk * 512:(cblk + 1) * 512],
                                                                                         in_=psum[0:P, bank, :], func=AF.Copy),
                                     r=[PB[bank]], w=[B_zc])
                            proj(uTi, Wa, 1536, 128, P, 2, B_uTi, B_Wa)
                            S.op("act", lambda e: e.activation(out=z[0:P, 1536:1664], in_=psum[0:P, 2, 0:128], func=AF.Copy), r=[PB[2]], w=[B_zc])
                            proj(uTi, Wa, 1664, 512, P, 1, B_uTi, B_Wa)
                            silu_gate(gate, P, 1, f_[0], B_f[0], f_[1], B_f[1], B_gate)
                            if i == NT - 1:
                                S.dma("sp", O["shift_" + g][l, b:b + 1, :], z[P - 1:P, :], r=[B_zc], w=[Buf()])
                            sub(3)
                            S.dma("act", prev[0:1, :], zp[127:128, :] if (i == 0 or P == 128) else zp[P - 1:P, :], r=[B_zp], w=[B_prev], part=1)
                            p0 = 0
                            while p0 < P - 1:
                                n_ = P - 1 - p0
                                n_ = (n_ // 16) * 16 if n_ >= 16 else n_
                                n_ = min(n_, 64)
                                S.dma("act", prev[1 + p0:1 + p0 + n_, :], z[p0:p0 + n_, :], r=[B_zc], w=[B_prev], part=2)
                                p0 += n_
                            S.op("dve", lambda e: e.tensor_tensor(prev[0:P, :], prev[0:P, :], z[0:P, :], ALU.subtract), r=[B_prev, B_zc], w=[B_prev])
                            S.op("dve", lambda e: e.tensor_tensor(prev[0:P, :], prev[0:P, :], mu[0:P, :], ALU.mult), r=[B_prev, B_PRM], w=[B_prev])
                            S.op("dve", lambda e: e.tensor_tensor(prev[0:P, :], prev[0:P, :], z[0:P, :], ALU.add), r=[B_prev, B_zc], w=[B_prev])
                            zr, zk, zv = prev[0:P, 0:512], prev[0:P, 512:1024], prev[0:P, 1024:1536]
                            sub(4)
                            S.op("act", lambda e: e.activation(out=tw[0:P, 0:64], in_=prev[0:P, 1536:1600], func=AF.Tanh), r=[B_prev], w=[B_tw])
                            S.op("dve", lambda e: e.tensor_copy(tw[0:P, 64:128], prev[0:P, 1600:1664]), r=[B_prev], w=[B_tw])
                            S.op("pe", lambda e: e.transpose(psum_bf[:, 0, 0:P], tw[0:P, :], idb[0:P, 0:P]), r=[B_tw, B_cb], w=[PB[0]])
                            S.op("act", lambda e: e.activation(out=twT[:, 0:P], in_=psum_bf[:, 0, 0:P], func=AF.Copy), r=[PB[0]], w=[B_twT])
                            S.op("pe", lambda e: e.matmul(psum[0:P, 1, :], lhsT=twT[0:64, 0:P], rhs=WAB[0:64, :], start=True, stop=True),
                                 r=[B_twT, B_WAB], w=[PB[1]])
                            S.op("pe", lambda e: e.matmul(psum[0:P, 2, :], lhsT=twT[64:128, 0:P], rhs=WAB[64:128, :], start=True, stop=True),
                                 r=[B_twT, B_WAB], w=[PB[2]])
                            S.op("dve", lambda e: e.tensor_tensor(f_[2][0:P, :], psum[0:P, 1, :], pw0[0:P, :], ALU.add), r=[PB[1], B_PRM], w=[B_f[2]])
                            affine_tanh(f_[2][0:P, :], f_[2][0:P, :], B_f[2], B_f[2], -0.5 * EXPM05, -0.5 * EXPM05)
                            S.op("dve", lambda e: e.tensor_tensor(f_[3][0:P, :], psum[0:P, 2, :], pa0[0:P, :], ALU.add), r=[PB[2], B_PRM], w=[B_f[3]])
                            affine_tanh(f_[3][0:P, :], f_[3][0:P, :], B_f[3], B_f[3], 0.5, 0.5)
                            logw, aa = f_[2], f_[3]
                            sub(5)
                            S.op("pool", lambda e: e.tensor_tensor(f_[4][0:P, :], zk, pkk[0:P, :], ALU.mult), r=[B_prev, B_PRM], w=[B_f[4]])
                            S.op("act", lambda e: e.activation(out=f_[0][0:P, :], in_=f_[4][0:P, :], func=AF.Square), r=[B_f[4]], w=[B_f[0]])
                            S.op("dve", lambda e: e.tensor_reduce(s8[0][0:P, :], f_[0][0:P, :].rearrange("p (h d) -> p h d", h=8), axis=AX.X, op=ALU.add),
                                 r=[B_f[0]], w=[B_s8[0]])
                            S.op("act", lambda e: e.activation(out=s8[1][0:P, :], in_=s8[0][0:P, :], func=AF.Ln, bias=1e-12), r=[B_s8[0]], w=[B_s8[1]])
                            S.op("act", lambda e: e.activation(out=s8[1][0:P, :], in_=s8[1][0:P, :], func=AF.Exp, scale=-0.5), r=[B_s8[1]], w=[B_s8[1]])
                            S.op("dve", lambda e: e.tensor_tensor(f_[4][0:P, :].rearrange("p (h d) -> p h d", h=8),
                                                                  f_[4][0:P, :].rearrange("p (h d) -> p h d", h=8),
                                                                  s8[1][0:P, :].unsqueeze(2).broadcast_to([P, 8, 64]), ALU.mult),
                                 r=[B_f[4], B_s8[1]], w=[B_f[4]])
                            kkn = f_[4]
                            S.op("dve", lambda e: e.scalar_tensor_tensor(out=f_[5][0:P, :], in0=aa[0:P, :], scalar=-1.0, in1=pka[0:P, :],
                                                                         op0=ALU.add, op1=ALU.mult), r=[B_f[3], B_PRM], w=[B_f[5]])
                            S.op("dve", lambda e: e.scalar_tensor_tensor(out=f_[5][0:P, :], in0=f_[5][0:P, :], scalar=1.0, in1=zk,
                                                                         op0=ALU.add, op1=ALU.mult), r=[B_f[5], B_prev], w=[B_f[5]])
                            k2 = f_[5]
                            S.op("pool", lambda e: e.tensor_tensor(f_[6][0:P, :], zr, k2[0:P, :], ALU.mult), r=[B_prev, B_f[5]], w=[B_f[6]])
                            S.op("pool", lambda e: e.tensor_tensor(f_[6][0:P, :], f_[6][0:P, :], prk[0:P, :], ALU.mult), r=[B_f[6], B_PRM], w=[B_f[6]])
                            S.op("dve", lambda e: e.tensor_reduce(s8[2][0:P, :], f_[6][0:P, :].rearrange("p (h d) -> p h d", h=8), axis=AX.X, op=ALU.add),
                                 r=[B_f[6]], w=[B_s8[2]])
                            S.op("dve", lambda e: e.tensor_tensor(bon[0:P, :].rearrange("p (h d) -> p h d", h=8),
                                                                  prev[0:P, 1024:1536].rearrange("p (h d) -> p h d", h=8),
                                                                  s8[2][0:P, :].unsqueeze(2).broadcast_to([P, 8, 64]), ALU.mult),
                                 r=[B_prev, B_s8[2]], w=[B_bon])
                            sub(6)
                            S.op("pe", lambda e: e.matmul(psum[0:P, 3, :], lhsT=tri[0:P, 0:P], rhs=logw[0:P, :], start=True, stop=True),
                                 r=[B_cf, B_f[2]], w=[PB[3]])
                            for hp in range(4):
                                S.op("pe", lambda e, hp=hp: e.matmul(psum[:, 0, hp:hp + 1], lhsT=logw[0:P, hp * 128:(hp + 1) * 128], rhs=onesf[0:P, 0:1],
                                                                     start=True, stop=True), r=[B_cf, B_f[2]], w=[PB[0]])
                            S.op("act", lambda e: e.activation(out=gC[:, :], in_=psum[:, 0, 0:4], func=AF.Exp), r=[PB[0]], w=[B_gC])
                            S.op("act", lambda e: e.activation(out=f_[6][0:P, :], in_=psum[0:P, 3, :], func=AF.Exp), r=[PB[3]], w=[B_f[6]])
                            S.op("act", lambda e: e.activation(out=f_[7][0:P, :], in_=psum[0:P, 3, :], func=AF.Exp, scale=-1.0), r=[PB[3]], w=[B_f[7]])
                            S.op("dve", lambda e: e.tensor_tensor(f_[0][0:P, :], psum[0:P, 3, :], logw[0:P, :], ALU.subtract), r=[PB[3], B_f[2]], w=[B_f[0]])
                            S.op("act", lambda e: e.activation(out=f_[0][0:P, :], in_=f_[0][0:P, :], func=AF.Exp), r=[B_f[0]], w=[B_f[0]])
                            sub(7)
                            S.op("dve", lambda e: e.tensor_tensor(h_[0][0:P, :], zr, f_[6][0:P, :], ALU.mult), r=[B_prev, B_f[6]], w=[B_hh[0]])
                            S.op("dve", lambda e: e.tensor_tensor(h_[1][0:P, :], k2[0:P, :], f_[7][0:P, :], ALU.mult), r=[B_f[5], B_f[7]], w=[B_hh[1]])
                            S.op("pool", lambda e: e.tensor_tensor(f_[1][0:P, :], kkn[0:P, :], aa[0:P, :], ALU.mult), r=[B_f[4], B_f[3]], w=[B_f[1]])
                            S.op("dve", lambda e: e.tensor_tensor(h_[2][0:P, :], f_[1][0:P, :], f_[7][0:P, :], ALU.mult), r=[B_f[1], B_f[7]], w=[B_hh[2]])
                            S.op("dve", lambda e: e.tensor_tensor(h_[3][0:P, :], kkn[0:P, :], f_[0][0:P, :], ALU.mult), r=[B_f[4], B_f[0]], w=[B_hh[3]])
                            S.op("pool", lambda e: e.tensor_copy(h_[4][0:P, :], zv), r=[B_prev], w=[B_hh[4]])
                            vbf = h_[4]
                            for q in range(4):
                                transposes(h_[q], 4, 128, P, 1 + q % 2, B_hh[q], tT[q][:, :, 0:P], B_tT[q], evac=("act" if q % 2 == 0 else "dve"))
                            rT, kT, bT, aT = tT

                            sub(8)
                            def pair_prod(lt, B_l, rt, B_r, dst, B_dst, mask, banks2):
                                for h in range(8):
                                    hp, po = h // 2, (h % 2) * 64
                                    bank = banks2 + h % 2
                                    S.op("pe", lambda e, hp=hp, po=po, bank=bank, h=h: e.matmul(
                                        psum[0:P, bank, hp * P:(hp + 1) * P], lhsT=lt[po:po + 64, hp, 0:P], rhs=rt[po:po + 64, hp, 0:P],
                                        start=True, stop=True), r=[B_l, B_r], w=[PB[bank]])
                                for par in range(2):
                                    bank = banks2 + par
                                    S.op("dve", lambda e, bank=bank, par=par: e.tensor_tensor(
                                        dst[0:P, par:8:2, 0:P], psum[0:P, bank, 0:4 * P].rearrange("p (h t) -> p h t", h=4),
                                        bc_h(mask, 4, P, P), ALU.mult), r=[PB[bank], B_cb], w=[B_dst])

                            pair_prod(bT, B_tT[2], aT, B_tT[3], QQ[0][1], B_Q[0][1], m_nsu, 2)
                            pair_prod(aT, B_tT[3], bT, B_tT[2], QQ[0][0], B_Q[0][0], m_nsl, 0)
                            pair_prod(kT, B_tT[1], aT, B_tT[3], LL[0], B_LL[0], m_su, 2)
                            pair_prod(bT, B_tT[2], rT, B_tT[0], LL[1], B_LL[1], m_iu, 0)
                            pair_prod(kT, B_tT[1], rT, B_tT[0], LL[2], B_LL[2], m_iu, 2)
                            sub(9)
                            for h in range(8):
                                hp, po = h // 2, (h % 2) * 64
                                S.op("pe", lambda e, h=h, hp=hp, po=po: e.matmul(psum[0:P, 6, h * 64:(h + 1) * 64], lhsT=aT[:, hp, 0:P],
                                                                                 rhs=Sbf[:, h % 2, hp, :], start=True, stop=False),
                                     r=[B_tT[3], B_Sb], w=[PB[6]])
                                S.op("pe", lambda e, h=h: e.matmul(psum[0:P, 6, h * 64:(h + 1) * 64], lhsT=LL[0][0:P, h, 0:P],
                                                                   rhs=vbf[0:P, h * 64:(h + 1) * 64], start=False, stop=True),
                                     r=[B_LL[0], B_hh[4]], w=[PB[6]])
                            S.op("act", lambda e: e.activation(out=XX[0][0:P, :, :].rearrange("p h d -> p (h d)"), in_=psum[0:P, 6, :], func=AF.Copy),
                                 r=[PB[6]], w=[B_X[0]])
                            sub(10)
                            nlev = int(round(math.log2(P)))
                            cur = 0
                            for lev in range(nlev):
                                Qc, QcT = QQ[cur][0], QQ[cur][1]
                                Xc, Xn = XX[lev % 2], XX[(lev + 1) % 2]
                                xb = 6 + (lev + 1) % 2
                                for h in range(8):
                                    S.op("pe", lambda e, h=h, xb=xb, QcT=QcT, Xc=Xc: e.matmul(psum[0:P, xb, h * 64:(h + 1) * 64], lhsT=QcT[0:P, h, 0:P],
                                                                                              rhs=Xc[0:P, h, :], start=True, stop=False),
                                         r=[B_Q[cur][1], B_X[lev % 2]], w=[PB[xb]])
                                    S.op("pe", lambda e, h=h, xb=xb, Xc=Xc: e.matmul(psum[0:P, xb, h * 64:(h + 1) * 64], lhsT=idb[0:P, 0:P],
                                                                                     rhs=Xc[0:P, h, :], start=False, stop=True),
                                         r=[B_cb, B_X[lev % 2]], w=[PB[xb]])
                                last_lev = (lev == nlev - 1)
                                S.op("act", lambda e, xb=xb, Xn=Xn, last_lev=last_lev: e.mul(
                                    Xn[0:P, :, :].rearrange("p h d -> p (h d)"), psum[0:P, xb, :], (-1.0 if last_lev else 1.0)),
                                    r=[PB[xb]], w=[B_X[(lev + 1) % 2]])
                                if not last_lev:
                                    nxt = 1 - cur
                                    Qn, QnT = QQ[nxt][0], QQ[nxt][1]
                                    for h in range(8):
                                        S.op("pe", lambda e, h=h: e.matmul(psum[0:P, 4 + h // 4, (h % 4) * P:(h % 4 + 1) * P], lhsT=Qc[0:P, h, 0:P],
                                                                           rhs=QcT[0:P, h, 0:P], start=True, stop=True),
                                             r=[B_Q[cur][0], B_Q[cur][1]], w=[PB[4 + h // 4]])
                                    for half in range(2):
                                        S.op("dve", lambda e, half=half: e.tensor_copy(QnT[0:P, half * 4:half * 4 + 4, 0:P],
                                                                                       psum[0:P, 4 + half, 0:4 * P].rearrange("p (h t) -> p h t", h=4)),
                                             r=[PB[4 + half]], w=[B_Q[nxt][1]])
                                    for h in range(8):
                                        S.op("pe", lambda e, h=h: e.matmul(psum[0:P, 4 + h // 4, (h % 4) * P:(h % 4 + 1) * P], lhsT=QcT[0:P, h, 0:P],
                                                                           rhs=Qc[0:P, h, 0:P], start=True, stop=True),
                                             r=[B_Q[cur][0], B_Q[cur][1]], w=[PB[4 + h // 4]])
                                    for half in range(2):
                                        S.op("act", lambda e, half=half: e.activation(out=Qn[0:P, half * 4:half * 4 + 4, 0:P],
                                                                                      in_=psum[0:P, 4 + half, 0:4 * P].rearrange("p (h t) -> p h t", h=4),
                                                                                      func=AF.Copy), r=[PB[4 + half]], w=[B_Q[nxt][0]])
                                    cur = nxt
                            NU, B_NU = XX[nlev % 2], B_X[nlev % 2]
                            sub(11)
                            for h in range(8):
                                hp, po = h // 2, (h % 2) * 64
                                S.op("pe", lambda e, h=h, hp=hp, po=po: e.matmul(psum[0:P, 4, h * 64:(h + 1) * 64], lhsT=rT[:, hp, 0:P],
                                                                                 rhs=Sbf[:, h % 2, hp, :], start=True, stop=False),
                                     r=[B_tT[0], B_Sb], w=[PB[4]])
                                S.op("pe", lambda e, h=h: e.matmul(psum[0:P, 4, h * 64:(h + 1) * 64], lhsT=LL[1][0:P, h, 0:P], rhs=NU[0:P, h, :],
                                                                   start=False, stop=False), r=[B_LL[1], B_NU], w=[PB[4]])
                                S.op("pe", lambda e, h=h: e.matmul(psum[0:P, 4, h * 64:(h + 1) * 64], lhsT=LL[2][0:P, h, 0:P],
                                                                   rhs=vbf[0:P, h * 64:(h + 1) * 64], start=False, stop=True),
                                     r=[B_LL[2], B_hh[4]], w=[PB[4]])
                            sub(12)
                            for hp in range(4):
                                for eo in range(2):
                                    h = 2 * hp + eo
                                    osl = psum[:, 5, (hp * 2 + eo) * 64:(hp * 2 + eo + 1) * 64]
                                    S.op("pe", lambda e, hp=hp, h=h, osl=osl: e.matmul(osl, lhsT=h_[1][0:P, hp * 128:(hp + 1) * 128],
                                                                                       rhs=vbf[0:P, h * 64:(h + 1) * 64], start=True, stop=False),
                                         r=[B_hh[1], B_hh[4]], w=[PB[5]])
                                    S.op("pe", lambda e, hp=hp, h=h, osl=osl: e.matmul(osl, lhsT=h_[2][0:P, hp * 128:(hp + 1) * 128],
                                                                                       rhs=NU[0:P, h, :], start=False, stop=True),
                                         r=[B_hh[2], B_NU], w=[PB[5]])
                            for eo in range(2):
                                pr = slice(eo * 64, eo * 64 + 64)
                                S.op("dve", lambda e, pr=pr, eo=eo: e.tensor_tensor(
                                    Sst[pr, :, :], Sst[pr, :, :], psum[pr, 5, :].rearrange("p (a e i) -> p a e i", a=4, e=2)[:, :, eo, :], ALU.add),
                                    r=[PB[5], B_S], w=[B_S])
                            S.op("dve", lambda e: e.tensor_tensor(Sst[:], Sst[:], gC[:, :].unsqueeze(2).broadcast_to([128, 4, 64]), ALU.mult),
                                 r=[B_S, B_gC], w=[B_S])
                            for eo in range(2):
                                S.op("act", lambda e, eo=eo: e.activation(out=Sbf[eo * 64:eo * 64 + 64, eo, :, :], in_=Sst[eo * 64:eo * 64 + 64, :, :],
                                                                          func=AF.Copy), r=[B_S], w=[B_Sb])
                            sub(13)
                            Y3 = psum[0:P, 4, :].rearrange("p (h d) -> p h d", h=8)
                            S.op("dve", lambda e: e.tensor_reduce(s8[3][0:P, :], Y3, axis=AX.X, op=ALU.add), r=[PB[4]], w=[B_s8[3]])
                            S.op("act", lambda e: e.activation(out=fb[0:P, :], in_=psum[0:P, 4, :], func=AF.Square), r=[PB[4]], w=[B_fb])
                            S.op("dve", lambda e: e.tensor_reduce(s8[4][0:P, :], fb[0:P, :].rearrange("p (h d) -> p h d", h=8), axis=AX.X, op=ALU.add),
                                 r=[B_fb], w=[B_s8[4]])
                            S.op("dve", lambda e: e.tensor_scalar_mul(s8[3][0:P, :], s8[3][0:P, :], 1.0 / 64), r=[B_s8[3]], w=[B_s8[3]])
                            S.op("dve", lambda e: e.tensor_tensor(s8[5][0:P, :], s8[3][0:P, :], s8[3][0:P, :], ALU.mult), r=[B_s8[3]], w=[B_s8[5]])
                            S.op("dve", lambda e: e.scalar_tensor_tensor(out=s8[4][0:P, :], in0=s8[4][0:P, :], scalar=1.0 / 64, in1=s8[5][0:P, :],
                                                                         op0=ALU.mult, op1=ALU.subtract), r=[B_s8[4], B_s8[5]], w=[B_s8[4]])
                            S.op("act", lambda e: e.activation(out=s8[4][0:P, :], in_=s8[4][0:P, :], func=AF.Ln, bias=GN_EPS), r=[B_s8[4]], w=[B_s8[4]])
                            S.op("act", lambda e: e.activation(out=s8[4][0:P, :], in_=s8[4][0:P, :], func=AF.Exp, scale=-0.5), r=[B_s8[4]], w=[B_s8[4]])
                            f3v = lambda t: t[0:P, :].rearrange("p (h d) -> p h d", h=8)
                            b8 = lambda t: t[0:P, :].unsqueeze(2).broadcast_to([P, 8, 64])
                            S.op("dve", lambda e: e.tensor_tensor(f3v(fb), Y3, b8(s8[3]), ALU.subtract), r=[PB[4], B_s8[3]], w=[B_fb])
                            S.op("dve", lambda e: e.tensor_tensor(f3v(fb), f3v(fb), b8(s8[4]), ALU.mult), r=[B_fb, B_s8[4]], w=[B_fb])
                            S.op("pool", lambda e: e.tensor_tensor(fb[0:P, :], fb[0:P, :], plw[0:P, :], ALU.mult), r=[B_fb, B_PRM], w=[B_fb])
                            S.op("pool", lambda e: e.tensor_tensor(fb[0:P, :], fb[0:P, :], plb[0:P, :], ALU.add), r=[B_fb, B_PRM], w=[B_fb])
                            S.op("pool", lambda e: e.tensor_tensor(fb[0:P, :], fb[0:P, :], bon[0:P, :], ALU.add), r=[B_fb, B_bon], w=[B_fb])
                            S.op("dve", lambda e: e.tensor_tensor(h_[5][0:P, :], fb[0:P, :], gate[0:P, :], ALU.mult), r=[B_fb, B_gate], w=[B_hh[5]])
                            sub(14)
                            transposes(h_[5], 4, 128, P, 6, B_hh[5], gT[:, :, 0:P], B_gT, evac="act")
                            for half in range(2):
                                bank = 7 - half
                                for k in range(4):
                                    S.op("pe", lambda e, k=k, half=half, bank=bank: e.matmul(
                                        psum[0:P, bank, :], lhsT=gT[:, k, 0:P], rhs=Wo[:, k, half * 512:(half + 1) * 512],
                                        start=(k == 0), stop=(k == 3)), r=[B_gT, B_Wo], w=[PB[bank]])
                                S.op("dve", lambda e, half=half, bank=bank: e.tensor_tensor(
                                    hti[0:P, half * 512:(half + 1) * 512], hti[0:P, half * 512:(half + 1) * 512], psum[0:P, bank, :], ALU.add),
                                    r=[PB[bank], B_hti], w=[B_hti])
                            S.dma("sp", hb[t0:t0 + P, :], hti[0:P, :], r=[B_hti], w=[B_h[i]])
                        sub(15)
                        for hp in range(4):
                            S.op("pe", lambda e, hp=hp: e.transpose(psum[0:64, 0, hp * 128:(hp + 1) * 128], Sst[:, hp, :], idf[:, :]),
                                 r=[B_S, B_cf], w=[PB[0]])
                        S.op("dve", lambda e: e.tensor_copy(wkvo[:].rearrange("p a b -> p (a b)"), psum[0:64, 0, :]), r=[PB[0]], w=[B_wo2])
                        S.dma("sp", O["wkv_" + g][l, b].rearrange("h i j -> i h j"), wkvo[:].rearrange("p a (e j) -> p (a e) j", e=2),
                              r=[B_wo2], w=[Buf()])
                        if _USE_SCHED:
                            S.end()
                    S.barrier()
                    checkpoint()

                    with ExitStack() as pbc:
                        Wc = pbc.enter_context(nc.sbuf_tensor(uname("Wc"), [128, 8, 1604], BF16))
                        Woc = pbc.enter_context(nc.sbuf_tensor(uname("Woc"), [128, 4, 1024], BF16))
                        B_Wc, B_Woc = Buf(), Buf()
                        with ExitStack() as pb_:
                            if _USE_SCHED:
                                S.begin()
                            sb = lambda name, shape, dt=F32: pb_.enter_context(nc.sbuf_tensor(uname(name), list(shape), dt))
                            Wb = sb("Wb", [128, 8, 2056], BF16)
                            Wo = sb("Wob", [128, 4, 1024], BF16)
                            KT = sb("KT", [128, 4, NK], BF16)
                            V = sb("Vb", [128, NKT, 8, 65], BF16)
                            C = sb("Cc", [128, NKT, 8])
                            nbias2 = [sb("nbias0", [128, NKT, 8]), sb("nbias1", [128, NKT, 8])]
                            tot = sb("tot", [128, 8])
                            fbias = sb("fbias", [128, 8])
                            LOGF = sb("LOGF", [128, NT, 8])
                            uT = [sb("uTb0", [128, 8, 128], BF16), sb("uTb1", [128, 8, 128], BF16)]
                            kf = [sb("kf0", [128, 512]), sb("kf1", [128, 512])]
                            vf = [sb("vf0", [128, 512]), sb("vf1", [128, 512])]
                            kbf = sb("kbf", [128, 512], BF16)
                            qbf = sb("qbf", [128, 512], BF16)
                            qT2 = [sb("qT0", [128, 4, 128], BF16), sb("qT1", [128, 4, 128], BF16)]
                            gate2 = [sb("gateb0", [128, 512], BF16), sb("gateb1", [128, 512], BF16)]
                            t1 = sb("t1b", [128, 512])
                            t2 = sb("t2b", [128, 512])
                            lf = sb("lf", [128, 8])
                            lf2 = sb("lf2", [128, 8])
                            cref = sb("cref", [128, 8])
                            pT = [sb("pT0", [128, 8, 128], BF16), sb("pT1", [128, 8, 128], BF16)]
                            rec = sb("rec", [128, 8])
                            yb = sb("yb", [128, 512])
                            gated = sb("gatedb", [128, 512], BF16)
                            gT = sb("gTb", [128, 4, 128], BF16)
                            ht = [sb("htb0", [128, D]), sb("htb1", [128, D])]
                            B_W, B_Wo, B_tot, B_fb, B_LOGF = [Buf() for _ in range(5)]
                            B_KTj = [Buf() for _ in range(NKT)]
                            B_Vj = [Buf() for _ in range(NKT)]
                            B_Cj = [Buf() for _ in range(NKT)]
                            B_nb2, B_qT2, B_gate2 = [Buf(), Buf()], [Buf(), Buf()], [Buf(), Buf()]
                            B_uT = [Buf(), Buf()]
                            B_kf = [Buf(), Buf()]
                            B_vf = [Buf(), Buf()]
                            B_kbf, B_qbf, B_t1, B_t2, B_lf, B_lf2, B_cref, B_rec, B_yb, B_gated, B_gT = [Buf() for _ in range(11)]
                            B_pT = [Buf(), Buf()]
                            B_ht = [Buf(), Buf()]
                            load_w(Wb, l, 2176, 2056, B_W)
                            load_wo(Wo, l, 512, B_Wo)
                            load_w(Wc, l, 4232 + 512, 1604 - 512, B_Wc, dst0=512)
                            for gg in range(4):
                                for n_ in range(2):
                                    cs = 4232 + n_ * 256 + gg * 64
                                    S.dma("pool", Wc[:, :, gg * 128 + n_ * 64:gg * 128 + n_ * 64 + 64],
                                          I["w_in"][l, :, cs:cs + 64].rearrange("(k p) n -> p k n", p=128), w=[B_Wc], part=2)
                            load_wo(Woc, l, 1024, B_Woc)
                            S.dma("sp", fbias[:], I["b_f_bias"][l, :].partition_broadcast(128), w=[B_fb])
                            S.op("pool", lambda e: e.memset(V[:].rearrange("p a h d -> p (a h d)"), 1.0), w=B_Vj)
                            S.op("dve", lambda e: e.memset(tot[:], 0.0), w=[B_tot])
                            S.op("pool", lambda e: e.memset(C[:].rearrange("p a h -> p (a h)"), 0.0), w=B_Cj)

                            def add_keys(j, kp, k_ap, B_k, v_ap, B_v, lf_ap, B_lf):
                                transposes(None, 4, 128, kp, 0, B_k, KT[:, :, j * 128:j * 128 + kp], B_KTj[j], evac="dve",
                                           src_blocks=[k_ap[0:kp, hp * 128:(hp + 1) * 128] for hp in range(4)])
                                S.op("act", lambda e: e.activation(out=V[0:kp, j, :, 0:64], in_=v_ap.rearrange("p (h d) -> p h d", h=8), func=AF.Copy),
                                     r=[B_v], w=[B_Vj[j]])
                                S.op("pe", lambda e: e.matmul(psum[0:kp, 1, 0:8], lhsT=tri[0:kp, 0:kp], rhs=lf_ap, start=True, stop=True),
                                     r=[B_cf, B_lf], w=[PB[1]])
                                S.op("pe", lambda e: e.matmul(psum[:, 1, 8:16], lhsT=e0row[0:kp, :], rhs=lf_ap, start=True, stop=True),
                                     r=[B_cf, B_lf], w=[PB[1]])
                                S.op("pe", lambda e: e.matmul(psum[:, 1, 16:24], lhsT=onesf[0:kp, :], rhs=lf_ap, start=True, stop=True),
                                     r=[B_cf, B_lf], w=[PB[1]])
                                S.op("dve", lambda e: e.tensor_tensor(C[0:kp, j, :], psum[0:kp, 1, 0:8], tot[0:kp, :], ALU.add), r=[PB[1], B_tot], w=[B_Cj[j]])
                                S.op("dve", lambda e: e.tensor_tensor(cref[:, :], psum[:, 1, 8:16], tot[:, :], ALU.add), r=[PB[1], B_tot], w=[B_cref])
                                S.op("dve", lambda e: e.tensor_tensor(tot[:], tot[:], psum[:, 1, 16:24], ALU.add), r=[PB[1], B_tot], w=[B_tot])

                            for j in range(NPT):
                                kfj, vfj = kf[j % 2], vf[j % 2]
                                S.dma("sp", kfj[:], I["cb_k"][l, b, j * 128:(j + 1) * 128, :], w=[B_kf[j % 2]])
                                S.dma("sp", vfj[:], I["cb_v"][l, b, j * 128:(j + 1) * 128, :], w=[B_vf[j % 2]])
                                S.dma("sp", lf[:], I["cb_logf"][l, b, j * 128:(j + 1) * 128, :], w=[B_lf])
                                S.op("pool", lambda e, kfj=kfj: e.tensor_copy(kbf[:], kfj[:]), r=[B_kf[j % 2]], w=[B_kbf])
                                add_keys(j, 128, kbf, B_kbf, vfj[:, :], B_vf[j % 2], lf[:, :], B_lf)

                            def tile_ctx_b(i):
                                par = i % 2
                                return (i * P, NPT + i, uT[par], B_uT[par], kf[par], vf[par], ht[par], B_ht[par],
                                        qT2[par], B_qT2[par], gate2[par], B_gate2[par], nbias2[par], B_nb2[par])

                            def front_b(i):
                                t0, j_new, uTi, B_uTi, kfi, vfi, hti, B_hti, qT, B_qT, gate, B_gate, nbias, B_nb = tile_ctx_b(i)
                                S.dma("act", uTi[:, :, 0:P], ut[:, 8 * t0:8 * t0 + 8 * P].rearrange("p (k t) -> p k t", k=8), r=[B_ut[i]], w=[B_uTi])
                                S.dma("act", hti[0:P, :], hb[t0:t0 + P, :], r=[B_h[i]], w=[B_hti])
                                proj(uTi, Wb, 0, 512, P, 1, B_uTi, B_W)
                                S.op("act", lambda e: e.activation(out=qbf[0:P, :], in_=psum[0:P, 1, :], func=AF.Copy), r=[PB[1]], w=[B_qbf])
                                transposes(qbf, 4, 128, P, 0, B_qbf, qT[:, :, 0:P], B_qT, evac="dve")
                                yield
                                proj(uTi, Wb, 512, 512, P, 1, B_uTi, B_W)
                                S.op("act", lambda e: e.activation(out=kfi[0:P, :], in_=psum[0:P, 1, :], func=AF.Copy), r=[PB[1]], w=[B_kf[i % 2]])
                                S.op("dve", lambda e: e.tensor_copy(kbf[0:P, :], psum[0:P, 1, :]), r=[PB[1]], w=[B_kbf])
                                S.dma("sp", O["bk_" + g][l, b, t0:t0 + P, :], kfi[0:P, :], r=[B_kf[i % 2]], w=[Buf()])
                                yield
                                proj(uTi, Wb, 1024, 512, P, 1, B_uTi, B_W)
                                S.op("act", lambda e: e.activation(out=vfi[0:P, :], in_=psum[0:P, 1, :], func=AF.Copy), r=[PB[1]], w=[B_vf[i % 2]])
                                S.dma("sp", O["bv_" + g][l, b, t0:t0 + P, :], vfi[0:P, :], r=[B_vf[i % 2]], w=[Buf()])
                                yield
                                proj(uTi, Wb, 1536, 8, P, 1, B_uTi, B_W)
                                S.op("dve", lambda e: e.tensor_tensor(lf2[0:P, :], psum[0:P, 1, 0:8], fbias[0:P, :], ALU.add), r=[PB[1], B_fb], w=[B_lf2])
                                S.op("act", lambda e: e.activation(out=lf2[0:P, :], in_=lf2[0:P, :], func=AF.Exp, scale=-1.0), r=[B_lf2], w=[B_lf2])
                                S.op("act", lambda e: e.activation(out=lf2[0:P, :], in_=lf2[0:P, :], func=AF.Ln, bias=1.0), r=[B_lf2], w=[B_lf2])
                                S.op("dve", lambda e: e.tensor_scalar_mul(LOGF[0:P, i, :], lf2[0:P, :], -1.0), r=[B_lf2], w=[B_LOGF])
                                yield
                                proj(uTi, Wb, 1544, 512, P, 1, B_uTi, B_W)
                                silu_gate(gate, P, 1, t1, B_t1, t2, B_t2, B_gate)
                                yield
                                add_keys(j_new, P, kbf, B_kbf, vfi[0:P, :], B_vf[i % 2], LOGF[0:P, i, :], B_LOGF)
                                nk = j_new + 1
                                S.op("dve", lambda e: e.tensor_tensor(nbias[:, 0:nk, :], cref[:, :].unsqueeze(1).broadcast_to([128, nk, 8]),
                                                                      C[:, 0:nk, :], ALU.subtract), r=[B_cref] + B_Cj[0:nk], w=[B_nb])
                                yield

                            def back_b(i):
                                t0, j_new, uTi, B_uTi, kfi, vfi, hti, B_hti, qT, B_qT, gate, B_gate, nbias, B_nb = tile_ctx_b(i)
                                nk = j_new + 1
                                def fox_qk(j):
                                    kp = kp_of(j)
                                    sbank = 2 + 2 * (j % 2)
                                    for h in range(8):
                                        hp, po = h // 2, (h % 2) * 64
                                        bank = sbank + h % 2
                                        S.op("pe", lambda e, h=h, hp=hp, po=po, bank=bank, j=j, kp=kp: e.matmul(
                                            psum[0:kp, bank, hp * P:(hp + 1) * P], lhsT=KT[po:po + 64, hp, j * 128:j * 128 + kp],
                                            rhs=qT[po:po + 64, hp, 0:P], start=True, stop=True), r=[B_KTj[j], B_qT], w=[PB[bank]])

                                def fox_exp(j):
                                    kp = kp_of(j)
                                    sbank = 2 + 2 * (j % 2)
                                    pTj, B_pTj = pT[j % 2], B_pT[j % 2]
                                    for h in range(8):
                                        bank = sbank + h % 2
                                        S.op("act", lambda e, h=h, bank=bank, j=j, kp=kp, pTj=pTj: e.activation(
                                            out=pTj[0:kp, h, 0:P], in_=psum[0:kp, bank, (h // 2) * P:(h // 2 + 1) * P], func=AF.Exp,
                                            scale=0.125, bias=nbias[0:kp, j, h:h + 1]), r=[PB[bank], B_nb], w=[B_pTj])
                                    if j == nk - 1:
                                        S.op("dve", lambda e, kp=kp, pTj=pTj: e.tensor_tensor(pTj[0:kp, :, 0:P], pTj[0:kp, :, 0:P], bc_h(m_iu, 8, kp, P), ALU.mult),
                                             r=[B_pTj, B_cb], w=[B_pTj])

                                def fox_pv(j):
                                    kp = kp_of(j)
                                    pTj, B_pTj = pT[j % 2], B_pT[j % 2]
                                    for h in range(8):
                                        ab = 6 + h // 4
                                        S.op("pe", lambda e, h=h, ab=ab, j=j, kp=kp, pTj=pTj: e.matmul(
                                            psum[0:P, ab, (h % 4) * 65:(h % 4 + 1) * 65], lhsT=pTj[0:kp, h, 0:P], rhs=V[0:kp, j, h, :],
                                            start=(j == 0 and h % 4 == 0), stop=(j == nk - 1), skip_group_check=True), r=[B_pTj, B_Vj[j]], w=[PB[ab]])

                                fox_qk(0)
                                for j in range(nk):
                                    if j + 1 < nk:
                                        fox_qk(j + 1)
                                    fox_exp(j)
                                    fox_pv(j)
                                    yield
                                for half in range(2):
                                    ab = 6 + half
                                    acc3 = psum[0:P, ab, 0:260].rearrange("p (h d) -> p h d", h=4)
                                    S.op("dve", lambda e, half=half, acc3=acc3: e.reciprocal(rec[0:P, half * 4:half * 4 + 4].unsqueeze(2), acc3[:, :, 64:65]),
                                         r=[PB[ab]], w=[B_rec])
                                    S.op("dve", lambda e, half=half, acc3=acc3: e.tensor_tensor(
                                        yb[0:P, half * 256:(half + 1) * 256].rearrange("p (h d) -> p h d", h=4), acc3[:, :, 0:64],
                                        rec[0:P, half * 4:half * 4 + 4].unsqueeze(2).broadcast_to([P, 4, 64]), ALU.mult), r=[PB[ab], B_rec], w=[B_yb])
                                S.op("pool", lambda e: e.tensor_tensor(gated[0:P, :], yb[0:P, :], gate[0:P, :], ALU.mult), r=[B_yb, B_gate], w=[B_gated])
                                transposes(gated, 4, 128, P, 2, B_gated, gT[:, :, 0:P], B_gT, evac="act")
                                yield
                                for half in range(2):
                                    bank = 3 + half
                                    for k in range(4):
                                        S.op("pe", lambda e, k=k, half=half, bank=bank: e.matmul(
                                            psum[0:P, bank, :], lhsT=gT[:, k, 0:P], rhs=Wo[:, k, half * 512:(half + 1) * 512],
                                            start=(k == 0), stop=(k == 3)), r=[B_gT, B_Wo], w=[PB[bank]])
                                    S.op("dve", lambda e, half=half, bank=bank: e.tensor_tensor(
                                        hti[0:P, half * 512:(half + 1) * 512], hti[0:P, half * 512:(half + 1) * 512], psum[0:P, bank, :], ALU.add),
                                        r=[PB[bank], B_hti], w=[B_hti])
                                S.dma("sp", hb[t0:t0 + P, :], hti[0:P, :], r=[B_hti], w=[B_h[i]])
                                yield

                            pipeline(front_b, back_b, NT)
                            S.dma("sp", O["blogf_" + g][l, b].rearrange("(t p) h -> p t h", p=P), LOGF[0:P, :, :], r=[B_LOGF], w=[Buf()])
                            if _USE_SCHED:
                                S.end()
                        S.barrier()
                        checkpoint()

                        with ExitStack() as pc_:
                            if _USE_SCHED:
                                S.begin()
                            sc = lambda name, shape, dt=F32: pc_.enter_context(nc.sbuf_tensor(uname(name), list(shape), dt))
                            Wo = Woc
                            KcT = sc("KcT", [128, NK], BF16)
                            Vc = sc("Vc", [128, NKT, 2, 65], BF16)
                            KiT = sc("KiT", [64, NK], BF16)
                            uT = [sc("uTc0", [128, 8, 128], BF16), sc("uTc1", [128, 8, 128], BF16)]
                            kvf = [sc("kvf0", [128, 256]), sc("kvf1", [128, 256])]
                            kif = [sc("kif0", [128, 64]), sc("kif1", [128, 64])]
                            kcb = sc("kcb", [128, 128], BF16)
                            kib = sc("kib", [128, 64], BF16)
                            qbf = sc("qbfc", [128, 512], BF16)
                            qib = sc("qib", [128, 256], BF16)
                            qT2 = [sc("qTc0", [128, 512], BF16), sc("qTc1", [128, 512], BF16)]
                            qiT = sc("qiT", [64, 4, 128], BF16)
                            wi = sc("wi", [128, 4])
                            gate2 = [sc("gatec0", [128, 512], BF16), sc("gatec1", [128, 512], BF16)]
                            t1 = sc("t1c", [128, 512])
                            t2 = sc("t2c", [128, 512])
                            SC2 = [sc("SC0", [128, NK]), sc("SC1", [128, NK])]
                            WK2 = [sc("WK0", [128, NK]), sc("WK1", [128, NK])]
                            RL = [sc("RL0", [128, 512]), sc("RL1", [128, 512])]
                            m82 = [sc("m8_0", [128, 8]), sc("m8_1", [128, 8])]
                            bs2 = [sc("bs_0", [128, 4]), sc("bs_1", [128, 4])]
                            B_bs2 = [Buf(), Buf()]
                            msk2 = [sc("msk0", [128, NK], BF16), sc("msk1", [128, NK], BF16)]
                            MT2 = [sc("MT0", [128, NKT, 128], BF16), sc("MT1", [128, NKT, 128], BF16)]
                            pT = [sc("pTc0", [128, 8, 128], BF16), sc("pTc1", [128, 8, 128], BF16)]
                            rec = sc("recc", [128, 8])
                            yb = sc("ybc", [128, 512])
                            gated = sc("gatedc", [128, 512], BF16)
                            gT = sc("gTc", [128, 4, 128], BF16)
                            ht = [sc("htc0", [128, D]), sc("htc1", [128, D])]
                            fin_g = sc("fin_g", [128, D])
                            junk = sc("junkc", [128, D], BF16)
                            stat = sc("statc", [128, 4])
                            B_W, B_Wo = B_Wc, B_Woc
                            B_KcTj = [Buf() for _ in range(NKT)]
                            B_Vcj = [Buf() for _ in range(NKT)]
                            B_KiTj = [Buf() for _ in range(NKT)]
                            B_qT2, B_gate2, B_MT2 = [Buf(), Buf()], [Buf(), Buf()], [Buf(), Buf()]
                            B_uT = [Buf(), Buf()]
                            B_kvf = [Buf(), Buf()]
                            B_kif = [Buf(), Buf()]
                            B_kcb, B_kib, B_qbf, B_qib, B_qiT, B_wi, B_t1, B_t2 = [Buf() for _ in range(8)]
                            B_rec, B_yb, B_gated, B_gT, B_fg, B_junk, B_stat = [Buf() for _ in range(7)]
                            B_SC2, B_WK2, B_m82, B_msk2 = [Buf(), Buf()], [Buf(), Buf()], [Buf(), Buf()], [Buf(), Buf()]
                            B_RL = [Buf(), Buf()]
                            B_pT = [Buf(), Buf()]
                            B_ht = [Buf(), Buf()]
                            if last:
                                S.dma("sp", fin_g[:], I["final_g"].partition_broadcast(128), w=[B_fg])
                            S.op("pool", lambda e: e.memset(Vc[:].rearrange("p a h d -> p (a h d)"), 1.0), w=B_Vcj)

                            def add_keys_c(j, kp, k_ap, B_k, v_ap, B_v, ki_ap, B_ki):
                                S.op("pe", lambda e: e.transpose(psum_bf[:, 0, 0:kp], k_ap, idb[0:kp, 0:kp]), r=[B_k, B_cb], w=[PB[0]])
                                S.op("pe", lambda e: e.transpose(psum_bf[0:64, 0, 128:128 + kp], ki_ap, idb[0:kp, 0:kp]), r=[B_ki, B_cb], w=[PB[0]])
                                S.op("dve", lambda e: e.tensor_copy(KcT[:, j * 128:j * 128 + kp], psum_bf[:, 0, 0:kp]), r=[PB[0]], w=[B_KcTj[j]])
                                S.op("dve", lambda e: e.tensor_copy(KiT[:, j * 128:j * 128 + kp], psum_bf[0:64, 0, 128:128 + kp]), r=[PB[0]], w=[B_KiTj[j]])
                                S.op("act", lambda e: e.activation(out=Vc[0:kp, j, :, 0:64], in_=v_ap.rearrange("p (h d) -> p h d", h=2), func=AF.Copy),
                                     r=[B_v], w=[B_Vcj[j]])

                            for j in range(NPT):
                                kvj, kij = kvf[j % 2], kif[j % 2]
                                S.dma("sp", kvj[:, 0:128], I["cc_k"][l, b, j * 128:(j + 1) * 128, :], w=[B_kvf[j % 2]], part=1)
                                S.dma("sp", kvj[:, 128:256], I["cc_v"][l, b, j * 128:(j + 1) * 128, :], w=[B_kvf[j % 2]], part=2)
                                S.dma("sp", kij[:], I["cc_kidx"][l, b, j * 128:(j + 1) * 128, :], w=[B_kif[j % 2]])
                                S.op("pool", lambda e, kvj=kvj: e.tensor_copy(kcb[:], kvj[:, 0:128]), r=[B_kvf[j % 2]], w=[B_kcb])
                                S.op("pool", lambda e, kij=kij: e.tensor_copy(kib[:], kij[:]), r=[B_kif[j % 2]], w=[B_kib])
                                add_keys_c(j, 128, kcb[:, :], B_kcb, kvj[:, 128:256], B_kvf[j % 2], kib[:, :], B_kib)

                            def tile_ctx_c(i):
                                par = i % 2
                                return (i * P, NPT + i, uT[par], B_uT[par], kvf[par], kif[par], ht[par], B_ht[par],
                                        qT2[par], B_qT2[par], gate2[par], B_gate2[par], MT2[par], B_MT2[par])

                            def front_c(i):
                                t0, j_new, uTi, B_uTi, kvi, kii, hti, B_hti, qT, B_qT, gate, B_gate, MT, B_MT = tile_ctx_c(i)
                                nk = j_new + 1
                                nvis = j_new * 128 + P
                                SC, WK, m8, msk = SC2[i % 2], WK2[i % 2], m82[i % 2], msk2[i % 2]
                                B_SC, B_WK, B_m8, B_msk = B_SC2[i % 2], B_WK2[i % 2], B_m82[i % 2], B_msk2[i % 2]
                                S.dma("act", uTi[:, :, 0:P], ut[:, 8 * t0:8 * t0 + 8 * P].rearrange("p (k t) -> p k t", k=8), r=[B_ut[i]], w=[B_uTi])
                                S.dma("act", hti[0:P, :], hb[t0:t0 + P, :], r=[B_h[i]], w=[B_hti])
                                proj(uTi, Wc, 0, 512, P, 1, B_uTi, B_W)
                                S.op("act", lambda e: e.activation(out=qbf[0:P, :], in_=psum[0:P, 1, :], func=AF.Copy), r=[PB[1]], w=[B_qbf])
                                transposes(qbf, 4, 128, P, 0, B_qbf, qT[:, 0:4 * P].rearrange("p (g t) -> p g t", g=4), B_qT, evac="dve")
                                yield
                                proj(uTi, Wc, 512, 512, P, 1, B_uTi, B_W)
                                S.op("act", lambda e: e.activation(out=kvi[0:P, :], in_=psum[0:P, 1, 0:256], func=AF.Copy), r=[PB[1]], w=[B_kvf[i % 2]])
                                S.op("dve", lambda e: e.tensor_copy(kcb[0:P, :], psum[0:P, 1, 0:128]), r=[PB[1]], w=[B_kcb])
                                S.op("dve", lambda e: e.tensor_copy(qib[0:P, :], psum[0:P, 1, 256:512]), r=[PB[1]], w=[B_qib])
                                S.dma("sp", O["ck_" + g][l, b, t0:t0 + P, :], kvi[0:P, 0:128], r=[B_kvf[i % 2]], w=[Buf()])
                                S.dma("sp", O["cv_" + g][l, b, t0:t0 + P, :], kvi[0:P, 128:256], r=[B_kvf[i % 2]], w=[Buf()])
                                yield
                                proj(uTi, Wc, 1024, 68, P, 1, B_uTi, B_W)
                                S.op("act", lambda e: e.activation(out=kii[0:P, :], in_=psum[0:P, 1, 0:64], func=AF.Copy), r=[PB[1]], w=[B_kif[i % 2]])
                                S.op("dve", lambda e: e.tensor_copy(kib[0:P, :], psum[0:P, 1, 0:64]), r=[PB[1]], w=[B_kib])
                                S.op("dve", lambda e: e.tensor_scalar_mul(wi[0:P, :], psum[0:P, 1, 64:68], 1.0 / 16), r=[PB[1]], w=[B_wi])
                                S.dma("sp", O["cki_" + g][l, b, t0:t0 + P, :], kii[0:P, :], r=[B_kif[i % 2]], w=[Buf()])
                                yield
                                proj(uTi, Wc, 1092, 512, P, 1, B_uTi, B_W)
                                silu_gate(gate, P, 1, t1, B_t1, t2, B_t2, B_gate)
                                add_keys_c(j_new, P, kcb[0:P, :], B_kcb, kvi[0:P, 128:256], B_kvf[i % 2], kib[0:P, :], B_kib)
                                yield
                                transposes(None, 4, 64, P, 0, B_qib, qiT[:, :, 0:P], B_qiT, evac="dve",
                                           src_blocks=[qib[0:P, ih * 64:(ih + 1) * 64] for ih in range(4)])
                                nch = (nvis + 511) // 512
                                for cch in range(nch):
                                    c0 = cch * 512
                                    cn = min(512, nvis - c0)
                                    for ih in range(4):
                                        bank = (cch * 4 + ih) % 2
                                        RLb, B_RLb = RL[(cch * 4 + ih) % 2], B_RL[(cch * 4 + ih) % 2]
                                        S.op("pe", lambda e, ih=ih, bank=bank, c0=c0, cn=cn: e.matmul(psum[0:P, bank, 0:cn], lhsT=qiT[:, ih, 0:P],
                                                                                                      rhs=KiT[:, c0:c0 + cn], start=True, stop=True),
                                             r=[B_qiT] + B_KiTj[c0 // 128:(c0 + cn + 127) // 128], w=[PB[bank]])
                                        S.op("act", lambda e, bank=bank, cn=cn, RLb=RLb: e.activation(out=RLb[0:P, 0:cn], in_=psum[0:P, bank, 0:cn], func=AF.Relu),
                                             r=[PB[bank]], w=[B_RLb])
                                        S.op("dve", lambda e, ih=ih, c0=c0, cn=cn, RLb=RLb: e.scalar_tensor_tensor(
                                            out=SC[0:P, c0:c0 + cn], in0=RLb[0:P, 0:cn], scalar=wi[0:P, ih:ih + 1],
                                            in1=(negeps[0:P, c0:c0 + cn] if ih == 0 else SC[0:P, c0:c0 + cn]), op0=ALU.mult, op1=ALU.add),
                                            r=[B_RLb, B_wi, B_cf] + ([B_SC] if ih else []), w=[B_SC])
                                    yield
                                if P == 128:
                                    S.op("pool", lambda e: e.memset(SC[0:64, nvis - 64:nvis], NEG), w=[B_SC])
                                nvalid_max = nvis
                                if nvalid_max > TOPK and _ACT_TOPK and P == 128 and i in _ACT_TILES:
                                    bsv, B_bsv = bs2[i % 2], B_bs2[i % 2]
                                    S.op("dve", lambda e: e.memset(bsv[0:P, :], 0.0), w=[B_bsv])
                                    delta = 8.0
                                    for t_ in range(41):
                                        S.op("act", lambda e: e.activation(out=msk[0:P, 0:nvis], in_=SC[0:P, 0:nvis], func=AF.Sign,
                                                                           bias=bsv[0:P, 0:1], accum_out=bsv[0:P, 1:2]),
                                             r=[B_SC, B_bsv], w=[B_bsv, B_msk], c=0.25 + nvis * 0.85e-3)
                                        S.op("act", lambda e: e.activation(out=bsv[0:P, 2:3], in_=bsv[0:P, 1:2], func=AF.Sign,
                                                                           bias=-(2.0 * TOPK - nvis - 0.5)), r=[B_bsv], w=[B_bsv], c=0.25)
                                        S.op("act", lambda e, delta=delta: e.activation(out=bsv[0:P, 0:1], in_=bsv[0:P, 2:3], func=AF.Identity,
                                                                                        scale=-delta, bias=bsv[0:P, 0:1]), r=[B_bsv], w=[B_bsv], c=0.25)
                                        delta *= 0.5
                                        yield
                                    S.op("dve", lambda e, delta=delta: e.tensor_scalar(m8[0:P, 7:8], bsv[0:P, 0:1], -1.0, -2.0 * delta, op0=ALU.mult, op1=ALU.add),
                                         r=[B_bsv], w=[B_m8])
                                    S.op("dve", lambda e: e.tensor_scalar(msk[0:P, 0:nvis], SC[0:P, 0:nvis], m8[0:P, 7:8], None, op0=ALU.is_ge),
                                         r=[B_SC, B_m8], w=[B_msk])
                                elif nvalid_max > TOPK:
                                    S.op("pool", lambda e: e.tensor_copy(WK[0:P, 0:nvis], SC[0:P, 0:nvis]), r=[B_SC], w=[B_WK], c=0.3 + nvis * 2.0e-3)
                                    for r_ in range(TOPK // 8):
                                        S.op("dve", lambda e: e.max(out=m8[0:P, :], in_=WK[0:P, 0:nvis]), r=[B_WK], w=[B_m8], c=0.15 + nvis * 1.05e-3)
                                        if r_ < TOPK // 8 - 1:
                                            S.op("dve", lambda e: e.match_replace(out=WK[0:P, 0:nvis], in_to_replace=m8[0:P, :], in_values=WK[0:P, 0:nvis],
                                                                                  imm_value=NEGR), r=[B_m8, B_WK], w=[B_WK], c=0.2 + nvis * 1.05e-3)
                                        yield
                                    S.op("dve", lambda e: e.tensor_scalar(msk[0:P, 0:nvis], SC[0:P, 0:nvis], m8[0:P, 7:8], None, op0=ALU.is_ge),
                                         r=[B_SC, B_m8], w=[B_msk])
                                else:
                                    S.op("dve", lambda e: e.tensor_scalar(msk[0:P, 0:nvis], SC[0:P, 0:nvis], -1.0e29, None, op0=ALU.is_ge),
                                         r=[B_SC], w=[B_msk])
                                yield

                            def back_c(i):
                                t0, j_new, uTi, B_uTi, kvi, kii, hti, B_hti, qT, B_qT, gate, B_gate, MT, B_MT = tile_ctx_c(i)
                                nk = j_new + 1
                                msk, B_msk = msk2[i % 2], B_msk2[i % 2]
                                for j0 in range(0, nk, 8):
                                    jn = min(8, nk - j0)
                                    for jj in range(jn):
                                        j = j0 + jj
                                        kp = kp_of(j)
                                        S.op("pe", lambda e, j=j, jj=jj, kp=kp: e.transpose(psum_bf[0:kp, 2, jj * P:(jj + 1) * P],
                                                                                            msk[0:P, j * 128:j * 128 + kp], idb[0:P, 0:P]),
                                             r=[B_msk, B_cb], w=[PB[2]])
                                    S.op("act", lambda e, j0=j0, jn=jn: e.activation(out=MT[:, j0:j0 + jn, 0:P],
                                                                                     in_=psum_bf[:, 2, 0:jn * P].rearrange("p (j t) -> p j t", j=jn),
                                                                                     func=AF.Identity, scale=240000.0, bias=-240000.0),
                                         r=[PB[2]], w=[B_MT])
                                def dsa_qk(j):
                                    kp = kp_of(j)
                                    off = nk - 1 - j
                                    for n_ in range(2):
                                        bank = 2 + 2 * (j % 2) + n_
                                        S.op("pe", lambda e, n_=n_, bank=bank, j=j, kp=kp: e.matmul(
                                            psum[0:kp, bank, 0:4 * P], lhsT=KcT[n_ * 64:(n_ + 1) * 64, j * 128:j * 128 + kp],
                                            rhs=qT[n_ * 64:(n_ + 1) * 64, 0:4 * P], start=True, stop=False), r=[B_KcTj[j], B_qT], w=[PB[bank]])
                                        S.op("pe", lambda e, bank=bank, j=j, kp=kp: e.matmul(
                                            psum[0:kp, bank, 0:4 * P], lhsT=idb[0:kp, 0:kp],
                                            rhs=MT[0:kp, j, 0:P].unsqueeze(1).broadcast_to([kp, 4, P]), start=False, stop=(off > 1)),
                                            r=[B_MT, B_cb], w=[PB[bank]])
                                        if off <= 1:
                                            S.op("pe", lambda e, n_=n_, bank=bank, off=off, kp=kp: e.matmul(
                                                psum[0:kp, bank, 0:4 * P], lhsT=idb[0:kp, 0:kp],
                                                rhs=ebt[0:kp, n_ * 4:n_ * 4 + 4, off, 0:P], start=False, stop=True),
                                                r=[B_ebt, B_cb], w=[PB[bank]])

                                def dsa_exp(j):
                                    kp = kp_of(j)
                                    pTj, B_pTj = pT[j % 2], B_pT[j % 2]
                                    for n_ in range(2):
                                        bank = 2 + 2 * (j % 2) + n_
                                        S.op("act", lambda e, n_=n_, bank=bank, kp=kp, pTj=pTj: e.activation(
                                            out=pTj[0:kp, n_ * 4:n_ * 4 + 4, 0:P], in_=psum[0:kp, bank, 0:4 * P].rearrange("p (g t) -> p g t", g=4),
                                            func=AF.Exp, scale=0.125), r=[PB[bank]], w=[B_pTj])

                                def dsa_pv(j):
                                    kp = kp_of(j)
                                    pTj, B_pTj = pT[j % 2], B_pT[j % 2]
                                    for h in range(8):
                                        ab = 6 + h // 4
                                        S.op("pe", lambda e, h=h, ab=ab, j=j, kp=kp, pTj=pTj: e.matmul(
                                            psum[0:P, ab, (h % 4) * 65:(h % 4 + 1) * 65], lhsT=pTj[0:kp, h, 0:P], rhs=Vc[0:kp, j, h // 4, :],
                                            start=(j == 0 and h % 4 == 0), stop=(j == nk - 1), skip_group_check=True), r=[B_pTj, B_Vcj[j]], w=[PB[ab]])

                                dsa_qk(0)
                                for j in range(nk):
                                    if j + 1 < nk:
                                        dsa_qk(j + 1)
                                    dsa_exp(j)
                                    dsa_pv(j)
                                    yield
                                for half in range(2):
                                    ab = 6 + half
                                    acc3 = psum[0:P, ab, 0:260].rearrange("p (h d) -> p h d", h=4)
                                    S.op("dve", lambda e, half=half, acc3=acc3: e.reciprocal(rec[0:P, half * 4:half * 4 + 4].unsqueeze(2), acc3[:, :, 64:65]),
                                         r=[PB[ab]], w=[B_rec])
                                    S.op("dve", lambda e, half=half, acc3=acc3: e.tensor_tensor(
                                        yb[0:P, half * 256:(half + 1) * 256].rearrange("p (h d) -> p h d", h=4), acc3[:, :, 0:64],
                                        rec[0:P, half * 4:half * 4 + 4].unsqueeze(2).broadcast_to([P, 4, 64]), ALU.mult), r=[PB[ab], B_rec], w=[B_yb])
                                S.op("pool", lambda e: e.tensor_tensor(gated[0:P, :], yb[0:P, :], gate[0:P, :], ALU.mult), r=[B_yb, B_gate], w=[B_gated])
                                transposes(gated, 4, 128, P, 2, B_gated, gT[:, :, 0:P], B_gT, evac="act")
                                yield
                                for half in range(2):
                                    bank = 3 + half
                                    for k in range(4):
                                        S.op("pe", lambda e, k=k, half=half, bank=bank: e.matmul(
                                            psum[0:P, bank, :], lhsT=gT[:, k, 0:P], rhs=Wo[:, k, half * 512:(half + 1) * 512],
                                            start=(k == 0), stop=(k == 3)), r=[B_gT, B_Wo], w=[PB[bank]])
                                    S.op("dve", lambda e, half=half, bank=bank: e.tensor_tensor(
                                        hti[0:P, half * 512:(half + 1) * 512], hti[0:P, half * 512:(half + 1) * 512], psum[0:P, bank, :], ALU.add),
                                        r=[PB[bank], B_hti], w=[B_hti])
                                if not last:
                                    S.dma("sp", hb[t0:t0 + P, :], hti[0:P, :], r=[B_hti], w=[B_h[i]])
                                else:
                                    rms_scale(hti, P, junk, B_junk, stat, B_stat, B_hti)
                                    S.op("dve", lambda e: e.scalar_tensor_tensor(out=hti[0:P, :], in0=hti[0:P, :], scalar=stat[0:P, 2:3], in1=fin_g[0:P, :],
                                                                                 op0=ALU.mult, op1=ALU.mult), r=[B_hti, B_stat, B_fg], w=[B_hti])
                                    S.dma("sp", yout[t0:t0 + P, :], hti[0:P, :], r=[B_hti], w=[B_y])
                                yield

                            pipeline(front_c, back_c, NT)
                            if _USE_SCHED:
                                S.end()
                        S.barrier()
                        checkpoint()
        except _Stop:
            pass

        _DEAD["v"] = False
        S.barrier()
    return nc


_CACHE = {}


def _run(inputs, NP, T, NS, TS, PAST, ncores):
    key = (NP, T, NS, TS, PAST)
    if key not in _CACHE:
        _CACHE[key] = build(NP, T, NS, TS, PAST)
    nc = _CACHE[key]
    cfc, cbc = host_consts(max(T, PAST + 128))
    f = lambda a: np.ascontiguousarray(np.asarray(a, dtype=np.float32))
    in_maps = []
    for c in range(ncores):
        m = {}
        m["x_p"] = f(inputs["x_prompt"][c * NP:(c + 1) * NP])
        m["x_s"] = f(inputs["x_sample"][c * NS:(c + 1) * NS])
        m["st_wkv"] = f(inputs["state_a_wkv"][:, c * NS:(c + 1) * NS])
        m["st_shift"] = f(inputs["state_a_shift"][:, c * NS:(c + 1) * NS])
        m["cb_k"] = f(np.asarray(inputs["cache_b_k"])[:, c * NS:(c + 1) * NS].reshape(L, NS, PAST, 512))
        m["cb_v"] = f(np.asarray(inputs["cache_b_v"])[:, c * NS:(c + 1) * NS].reshape(L, NS, PAST, 512))
        m["cb_logf"] = f(inputs["cache_b_logf"][:, c * NS:(c + 1) * NS])
        m["cc_k"] = f(np.asarray(inputs["cache_c_k"])[:, c * NS:(c + 1) * NS].reshape(L, NS, PAST, 128))
        m["cc_v"] = f(np.asarray(inputs["cache_c_v"])[:, c * NS:(c + 1) * NS].reshape(L, NS, PAST, 128))
        m["cc_kidx"] = f(inputs["cache_c_kidx"][:, c * NS:(c + 1) * NS])
        for nm in ("norm_g", "w_in", "w_out", "a_mu", "a_w0", "a_a0", "a_k_k", "a_k_a", "a_lnx_w", "a_lnx_b", "a_w_b", "a_a_b",
                   "b_f_bias", "final_g"):
            m[nm] = f(inputs[nm])
        m["a_r_k"] = f(np.asarray(inputs["a_r_k"]).reshape(L, 512))
        m["t5_table"] = f(np.asarray(inputs["t5_table"]).reshape(256))
        m["cf"] = cfc
        m["cb"] = cbc
        in_maps.append(m)
    res = run_bass_kernel_spmd(nc, in_maps, core_ids=list(range(ncores)))
    R = res.results
    cat = lambda name, ax: np.concatenate([np.asarray(R[c][name]) for c in range(ncores)], axis=ax)
    outs = []
    for g, nb, tt in (("p", NP, T), ("s", NS, TS)):
        B = nb * ncores
        outs.append([
            cat("y_" + g, 0),
            cat("wkv_" + g, 1),
            cat("shift_" + g, 1),
            cat("bk_" + g, 1).reshape(L, B, tt, 8, 64),
            cat("bv_" + g, 1).reshape(L, B, tt, 8, 64),
            cat("blogf_" + g, 1),
            cat("ck_" + g, 1).reshape(L, B, tt, 2, 64),
            cat("cv_" + g, 1).reshape(L, B, tt, 2, 64),
            cat("cki_" + g, 1),
        ])
    p, s = outs
    return (p[0], s[0], p[1], p[2], p[3], p[4], p[5], p[6], p[7], p[8], s[1], s[2], s[3], s[4], s[5], s[6], s[7], s[8])


def kernel(**inputs):
    out = _run(inputs, 2, 2048, 1, 64, 1024, NCORES)
    return tuple(np.ascontiguousarray(o, dtype=np.float32) for o in out)
```
